# Optimizing a Trainium2 kernel written in Bass

```python
import math
import jax, jax.numpy as jnp
from jax import lax
import numpy as np

D_MODEL = 1024
BATCH = 4
SEQ = 8192
DEPTH = 4

GRID_W = 64
CTX_LEN = 256
N_MIXERS = 2
NORM_EPS = 1e-6

MLA_HEADS = 8
QK_NOPE = 128
QK_ROPE = 64
V_DIM = 128
Q_LORA = 256
KV_LORA = 128
MLA_WIDTH = MLA_HEADS * V_DIM
MLA_IN_WIDTH = Q_LORA + KV_LORA + QK_ROPE + MLA_WIDTH
ROPE_AXIS_DIM = QK_ROPE // 2
ROPE_THETA = 10000.0
Q_BLOCK = 128

FOURIER_WIDTH = D_MODEL
FOURIER_GROUPS = 8
FOURIER_GROUP_DIM = FOURIER_WIDTH // FOURIER_GROUPS

N_MLA_LAYERS = (DEPTH + 1) // 2
N_FOURIER_LAYERS = DEPTH // 2
ADA_STD = 0.5 * D_MODEL ** -0.5

kernel_name = "hybrid_mla_fourier_gated_prefix_dit"


def rms_norm(x, g):
    xf = x.astype(jnp.float32)
    y = xf * lax.rsqrt(jnp.mean(xf * xf, axis=-1, keepdims=True) + NORM_EPS)
    return (y * g.astype(jnp.float32)).astype(x.dtype)


def axial_rope_tables(row, col):
    inv_freq = 1.0 / (ROPE_THETA ** (jnp.arange(0, ROPE_AXIS_DIM, 2, dtype=jnp.float32) / ROPE_AXIS_DIM))
    ang_r = row.astype(jnp.float32)[:, None] * inv_freq[None, :]
    ang_c = col.astype(jnp.float32)[:, None] * inv_freq[None, :]
    return jnp.cos(ang_r), jnp.sin(ang_r), jnp.cos(ang_c), jnp.sin(ang_c)


def _rotate(t, cos, sin):
    t1, t2 = jnp.split(t, 2, axis=-1)
    return jnp.concatenate([t1 * cos - t2 * sin, t2 * cos + t1 * sin], axis=-1)


def apply_axial_rope(t, tables):
    cos_r, sin_r, cos_c, sin_c = [a.astype(t.dtype)[None, :, None, :] for a in tables]
    t_r, t_c = jnp.split(t, 2, axis=-1)
    return jnp.concatenate([_rotate(t_r, cos_r, sin_r), _rotate(t_c, cos_c, sin_c)], axis=-1)


def mla_query(c_q, g_qa, w_qup):
    b, t, _ = c_q.shape
    q = (rms_norm(c_q, g_qa) @ w_qup).reshape(b, t, MLA_HEADS, QK_NOPE + QK_ROPE)
    return q[..., :QK_NOPE], q[..., QK_NOPE:]


def mla_keyvalue(c_kv, g_kva, w_kvup):
    b, t, _ = c_kv.shape
    kv = (rms_norm(c_kv, g_kva) @ w_kvup).reshape(b, t, MLA_HEADS, QK_NOPE + V_DIM)
    return kv[..., :QK_NOPE], kv[..., QK_NOPE:]


def assemble_keys(k_nope, k_pe):
    return jnp.concatenate([k_nope, jnp.broadcast_to(k_pe, k_nope.shape[:-1] + (QK_ROPE,))], axis=-1)


def dense_attention(q, k, v):
    scale = 1.0 / math.sqrt(QK_NOPE + QK_ROPE)
    s = jnp.einsum('bqhd,bkhd->bhqk', q, k).astype(jnp.float32) * scale
    p = jax.nn.softmax(s, axis=-1).astype(v.dtype)
    return jnp.einsum('bhqk,bkhd->bqhd', p, v)


def latent_attention(q, k_lat, v_lat, k_ctx, v_ctx):
    b, t, h, dk = q.shape
    k_all = jnp.concatenate([k_ctx, k_lat], axis=1)
    v_all = jnp.concatenate([v_ctx, v_lat], axis=1)
    nb = t // Q_BLOCK
    qb = q.reshape(b, nb, Q_BLOCK, h, dk).transpose(1, 0, 2, 3, 4)
    o = lax.map(lambda qblk: dense_attention(qblk, k_all, v_all), qb)
    return o.transpose(1, 0, 2, 3, 4).reshape(b, t, h * V_DIM)


def fourier_mix(u, w_in, w_out):
    proj = u @ w_in
    z, gate = proj[..., :FOURIER_WIDTH], proj[..., FOURIER_WIDTH:]
    b, t, _ = z.shape
    zg = z.reshape(b, t, FOURIER_GROUPS, FOURIER_GROUP_DIM).astype(jnp.float32)
    y = jnp.fft.fftn(zg, axes=(1, 3), norm="ortho").real.astype(u.dtype).reshape(b, t, FOURIER_WIDTH)
    return (y * jax.nn.silu(gate)) @ w_out


def ada_params(cond_act, w, b):
    mod = cond_act @ w + b
    return jnp.split(mod, 3, axis=-1)


def setup_inputs(seed: int = 0) -> dict:
    key = jax.random.key(seed)
    ks = jax.random.split(key, 16)
    n = jax.random.normal
    f32 = jnp.float32
    return {
        "x": n(ks[0], (BATCH, SEQ, D_MODEL), f32),
        "c": n(ks[1], (BATCH, D_MODEL), f32),
        "ctx": n(ks[2], (BATCH, CTX_LEN, D_MODEL), f32),
        "c_ctx": n(ks[3], (D_MODEL,), f32),
        "norm_g": 1.0 + 0.02 * n(ks[4], (DEPTH, D_MODEL), f32),
        "w_ada": ADA_STD * n(ks[5], (DEPTH, D_MODEL, 3 * D_MODEL), f32),
        "b_ada": 0.02 * n(ks[6], (DEPTH, 3 * D_MODEL), f32),
        "mla_w_in": D_MODEL ** -0.5 * n(ks[7], (N_MLA_LAYERS, D_MODEL, MLA_IN_WIDTH), f32),
        "mla_g_qa": 1.0 + 0.02 * n(ks[8], (N_MLA_LAYERS, Q_LORA), f32),
        "mla_w_qup": Q_LORA ** -0.5 * n(ks[9], (N_MLA_LAYERS, Q_LORA, MLA_HEADS * (QK_NOPE + QK_ROPE)), f32),
        "mla_g_kva": 1.0 + 0.02 * n(ks[10], (N_MLA_LAYERS, KV_LORA), f32),
        "mla_w_kvup": KV_LORA ** -0.5 * n(ks[11], (N_MLA_LAYERS, KV_LORA, MLA_HEADS * (QK_NOPE + V_DIM)), f32),
        "mla_w_out": MLA_WIDTH ** -0.5 * n(ks[12], (N_MLA_LAYERS, MLA_WIDTH, D_MODEL), f32),
        "fno_w_in": D_MODEL ** -0.5 * n(ks[13], (N_FOURIER_LAYERS, D_MODEL, 2 * FOURIER_WIDTH), f32),
        "fno_w_out": FOURIER_WIDTH ** -0.5 * n(ks[14], (N_FOURIER_LAYERS, FOURIER_WIDTH, D_MODEL), f32),
        "final_g": 1.0 + 0.02 * n(ks[15], (D_MODEL,), f32),
    }


def reference(x, c, ctx, c_ctx, norm_g, w_ada, b_ada, mla_w_in, mla_g_qa, mla_w_qup, mla_g_kva,
              mla_w_kvup, mla_w_out, fno_w_in, fno_w_out, final_g):
    n_tok = x.shape[1]
    rows = n_tok // GRID_W
    row = jnp.repeat(jnp.arange(rows, dtype=jnp.int32), GRID_W)
    col = jnp.tile(jnp.arange(GRID_W, dtype=jnp.int32), rows)
    rope = axial_rope_tables(row, col)

    act_lat = jax.nn.silu(c)
    act_ctx = jax.nn.silu(c_ctx)
    h_lat, h_ctx = x, ctx
    kv_lo, kv_hi = Q_LORA, Q_LORA + KV_LORA
    pe_hi = kv_hi + QK_ROPE

    for i in range(DEPTH):
        mixer = i % N_MIXERS
        idx = i // N_MIXERS
        ctx_later = any(j % N_MIXERS == 0 for j in range(i + 1, DEPTH))

        sh_l, sc_l, gt_l = ada_params(act_lat, w_ada[i], b_ada[i])
        u_lat = rms_norm(h_lat, norm_g[i]) * (1.0 + sc_l[:, None, :]) + sh_l[:, None, :]
        sh_c, sc_c, gt_c = ada_params(act_ctx, w_ada[i], b_ada[i])

        if mixer == 0:
            w_in = mla_w_in[idx]
            u_ctx = rms_norm(h_ctx, norm_g[i]) * (1.0 + sc_c) + sh_c
            p_lat = u_lat @ w_in
            qn_l, qp_l = mla_query(p_lat[..., :kv_lo], mla_g_qa[idx], mla_w_qup[idx])
            kn_l, v_l = mla_keyvalue(p_lat[..., kv_lo:kv_hi], mla_g_kva[idx], mla_w_kvup[idx])
            kp_l = apply_axial_rope(p_lat[..., None, kv_hi:pe_hi], rope)
            q_l = jnp.concatenate([qn_l, apply_axial_rope(qp_l, rope)], axis=-1)
            k_l = assemble_keys(kn_l, kp_l)
            if ctx_later:
                p_ctx = u_ctx @ w_in
            else:
                p_ctx = u_ctx @ w_in[:, kv_lo:pe_hi]
                p_ctx = jnp.concatenate([jnp.zeros(p_ctx.shape[:-1] + (kv_lo,), p_ctx.dtype), p_ctx], axis=-1) if False else p_ctx
            off = 0 if not ctx_later else kv_lo
            kn_c, v_c = mla_keyvalue(p_ctx[..., off:off + KV_LORA], mla_g_kva[idx], mla_w_kvup[idx])
            k_c = assemble_keys(kn_c, p_ctx[..., None, off + KV_LORA:off + KV_LORA + QK_ROPE])

            o_lat = latent_attention(q_l, k_l, v_l, k_c, v_c)
            y_lat = (o_lat * jax.nn.silu(p_lat[..., pe_hi:])) @ mla_w_out[idx]
            if ctx_later:
                qn_c, qp_c = mla_query(p_ctx[..., :kv_lo], mla_g_qa[idx], mla_w_qup[idx])
                q_c = jnp.concatenate([qn_c, qp_c], axis=-1)
                b_, t_c = h_ctx.shape[0], h_ctx.shape[1]
                o_ctx = dense_attention(q_c, k_c, v_c).reshape(b_, t_c, MLA_WIDTH)
                y_ctx = (o_ctx * jax.nn.silu(p_ctx[..., pe_hi:])) @ mla_w_out[idx]
                h_ctx = h_ctx + gt_c * y_ctx
            h_lat = h_lat + gt_l[:, None, :] * y_lat
        else:
            y_lat = fourier_mix(u_lat, fno_w_in[idx], fno_w_out[idx])
            if ctx_later:
                u_ctx = rms_norm(h_ctx, norm_g[i]) * (1.0 + sc_c) + sh_c
                h_ctx = h_ctx + gt_c * fourier_mix(u_ctx, fno_w_in[idx], fno_w_out[idx])
            h_lat = h_lat + gt_l[:, None, :] * y_lat

    return rms_norm(h_lat, final_g)
```

```python
import contextlib
import math
import numpy as np
import ml_dtypes
import concourse.bass as bass
import concourse.mybir as mybir
from concourse.bass_utils import run_bass_kernel_spmd

F32 = mybir.dt.float32
BF16 = mybir.dt.bfloat16
AF = mybir.ActivationFunctionType
ALU = mybir.AluOpType

T = 8192
D = 1024
NCTX = 256
NK = T + NCTX
NKC = NK // 128
EPS = 1e-6
SCALE = 1.0 / math.sqrt(192.0)


class Op:
    __slots__ = ("eng", "fn", "deps", "dma", "semkey", "signal", "sem", "val", "key", "idx", "impl")

    def __init__(self, eng, fn, dma, semkey):
        self.eng = eng
        self.fn = fn
        self.deps = []
        self.dma = dma
        self.semkey = semkey
        self.signal = False
        self.sem = None
        self.val = 0
        self.key = 0.0
        self.idx = 0
        self.impl = []


class Phase:
    ENGS = ("sp", "pe", "act", "dve", "pool")

    def __init__(self, nc, name):
        self.nc = nc
        self.name = name
        self.ops = []
        self.last_writer = {}
        self.readers = {}
        self.stack = contextlib.ExitStack()
        self.n_alloc = 0
        self.cur_key = None

    def sb(self, shape, dtype, name="t"):
        self.n_alloc += 1
        t = self.stack.enter_context(self.nc.sbuf_tensor(f"{self.name}_{name}{self.n_alloc}", list(shape), dtype))
        return t, f"{name}{self.n_alloc}"

    def ps(self, shape, dtype, name="p"):
        self.n_alloc += 1
        t = self.stack.enter_context(self.nc.psum_tensor(f"{self.name}_{name}{self.n_alloc}", list(shape), dtype))
        return t, f"{name}{self.n_alloc}"

    def op(self, eng, fn, reads=(), writes=(), dma=False, semkey=None):
        o = Op(eng, fn, dma, semkey)
        o.idx = len(self.ops)
        o.key = self.cur_key if self.cur_key is not None else 0.0
        deps = []
        for t in list(reads) + list(writes):
            w = self.last_writer.get(t)
            if w is not None:
                deps.append(w)
        for t in writes:
            deps.extend(self.readers.get(t, ()))
        seen = set()
        for d in deps:
            if id(d) in seen or d is o:
                continue
            seen.add(id(d))
            if d.eng == "pe" and eng == "pe" and not d.dma and not dma:
                o.impl.append(d)
                continue
            o.deps.append(d)
            d.signal = True
        for t in writes:
            self.last_writer[t] = o
            self.readers[t] = []
        for t in reads:
            self.readers.setdefault(t, []).append(o)
        self.ops.append(o)
        return o

    def dma(self, queue, out, in_, reads=(), writes=(), semkey=None, slow=False):
        assert semkey is not None
        if slow:
            fn = lambda e: e.dma_start(out=out, in_=in_, allow_slow_non_contiguous=True)
        else:
            fn = lambda e: e.dma_start(out=out, in_=in_)
        return self.op(queue, fn, reads, writes, dma=True, semkey=semkey)

    def emit(self):
        nc = self.nc
        self.ops.sort(key=lambda o: (o.key, o.idx))
        pos = {id(o): i for i, o in enumerate(self.ops)}
        for o in self.ops:
            for d in o.deps + o.impl:
                assert pos[id(d)] < pos[id(o)], f"{self.name}: pipelining key inverts a dependency ({d.eng}->{o.eng})"
        keys = []
        for o in self.ops:
            if o.dma:
                o.signal = True
                if o.semkey not in keys:
                    keys.append(o.semkey)
        sems = {}
        for e in self.ENGS:
            sems[("eng", e)] = nc.alloc_semaphore(name=f"{self.name}_s_{e}")
        for i, k in enumerate(keys):
            sems[("dma", k)] = nc.alloc_semaphore(name=f"{self.name}_d{i}")
        counts = {k: 0 for k in sems}
        for o in self.ops:
            if not o.signal:
                continue
            k = ("dma", o.semkey) if o.dma else ("eng", o.eng)
            counts[k] += 16 if o.dma else 1
            o.sem = k
            o.val = counts[k]
        per_eng = {e: [] for e in self.ENGS}
        for o in self.ops:
            per_eng[o.eng].append(o)
        final_dma = {k: v for k, v in counts.items() if k[0] == "dma" and v > 0}

        def body(ename):
            def f(eng):
                waited = {}
                for o in per_eng[ename]:
                    need = {}
                    for d in o.deps:
                        if need.get(d.sem, 0) < d.val:
                            need[d.sem] = d.val
                    for k, v in need.items():
                        if waited.get(k, 0) >= v:
                            continue
                        eng.wait_ge(sems[k], v)
                        waited[k] = v
                    ins = o.fn(eng)
                    if o.signal:
                        ins.then_inc(sems[o.sem], 16 if o.dma else 1)
                if ename == "sp":
                    for k, v in final_dma.items():
                        eng.wait_ge(sems[k], v)
                    for e2 in ("pe", "act", "dve", "pool"):
                        v = counts[("eng", e2)]
                        if v > 0:
                            eng.wait_ge(sems[("eng", e2)], v)
            return f

        with nc.Block() as block:
            block.sync(body("sp"))
            block.tensor(body("pe"))
            block.scalar(body("act"))
            block.vector(body("dve"))
            block.gpsimd(body("pool"))
        nc.all_engine_barrier()
        nc.clear_and_free_semaphores(list(sems.values()))
        nc.all_engine_barrier()
        self.stack.close()
        return len(self.ops)


class Rot:
    def __init__(self, items):
        self.items = items
        self.i = 0

    def next(self):
        it = self.items[self.i % len(self.items)]
        self.i += 1
        return it


def rot_sb(ph, n, shape, dtype, name):
    return Rot([ph.sb(shape, dtype, name) for _ in range(n)])


def rot_ps(ph, n, shape, dtype, name):
    return Rot([ph.ps(shape, dtype, name) for _ in range(n)])


def OP_mm(ph, out, lhsT, rhs, start, stop, r, w):
    return ph.op("pe", lambda e: e.matmul(out, lhsT, rhs, start=start, stop=stop), r, w)


def OP_tr(ph, out, in_, ident, r, w):
    return ph.op("pe", lambda e: e.transpose(out, in_, ident), r, w)


def OP_act(ph, out, in_, func, r, w, bias=None, scale=None, accum=None):
    kw = {}
    if bias is not None:
        kw["bias"] = bias
    if scale is not None:
        kw["scale"] = scale
    if accum is not None:
        kw["accum_out"] = accum
    return ph.op("act", lambda e: e.activation(out, in_, func, **kw), r, w)


def OP_ts(ph, eng, out, in0, s1, s2, op0, op1, r, w):
    if op1 is None:
        return ph.op(eng, lambda e: e.tensor_scalar(out, in0, s1, None, op0), r, w)
    return ph.op(eng, lambda e: e.tensor_scalar(out, in0, s1, s2, op0, op1), r, w)


def OP_stt(ph, eng, out, in0, scalar, in1, op0, op1, r, w):
    return ph.op(eng, lambda e: e.scalar_tensor_tensor(out, in0, scalar, in1, op0, op1), r, w)


def OP_tt(ph, eng, out, in0, in1, op, r, w):
    return ph.op(eng, lambda e: e.tensor_tensor(out, in0, in1, op), r, w)


def OP_recip(ph, out, in_, r, w):
    return ph.op("dve", lambda e: e.reciprocal(out, in_), r, w)


def OP_rstd(ph, out, in_, k):
    OP_act(ph, out, in_, AF.Sqrt, [k], [k], bias=EPS)
    return OP_recip(ph, out, out, [k], [k])


def OP_cp(ph, eng, out, in_, r, w):
    if eng == "act":
        return ph.op(eng, lambda e: e.activation(out, in_, AF.Copy), r, w)
    return ph.op(eng, lambda e: e.tensor_copy(out, in_), r, w)


class Builder:
    def __init__(self, debug=False, upto=None, steps=None, inject=()):
        self.debug = debug
        self.upto = upto
        self.steps = steps
        self.inject = inject
        nc = bass.Bass("TRN2", target_bir_lowering=False)
        self.nc = nc
        self.I = {}
        self.S = {}
        din = lambda n, s, dt=F32: self.I.__setitem__(n, nc.dram_tensor(n, list(s), dt, kind="ExternalInput").ap())
        din("x", [T, D]); din("ctx", [NCTX, D]); din("cvecT", [128, 8, 2])
        din("norm_g", [4, D]); din("w_ada", [4, D, 3 * D]); din("b_ada", [4, 3 * D])
        din("mla_w_in", [2, D, 1472]); din("mla_g_qa", [2, 256]); din("mla_w_qup", [2, 256, 1536])
        din("mla_w_qup_sw", [2, 256, 512]); din("mla_g_kva", [2, 128]); din("mla_w_kvup", [2, 128, 2048])
        din("mla_w_ukT", [2, 8, 128, 128]); din("mla_w_out", [2, D, D])
        din("fno_w_in", [2, D, 2 * D]); din("fno_w_out", [2, D, D]); din("final_g", [1, D])
        din("rope_k", [T, 2, 64]); din("rope_q", [2, 64, T])
        din("ident", [128, 128], BF16); din("dft_t1", [128, 2, 128], BF16); din("dft_tw", [128, 2, 64])
        din("dft_r2", [128, 128], BF16); din("dft_c1", [128, 2, 128], BF16); din("dft_ctx", [128, 2, 512], BF16)
        self.out = nc.dram_tensor("out", [T, D], F32, kind="ExternalOutput").ap()
        kind = "ExternalOutput" if debug else "Internal"
        dsc = lambda n, s, dt: self.S.__setitem__(n, nc.dram_tensor("s_" + n, list(s), dt, kind=kind).ap())
        dsc("h", [T, D], F32); dsc("hc", [NCTX, D], F32); dsc("modd", [4, 2, 3, D], F32)
        dsc("QA", [8, 128, NK], BF16); dsc("QR", [8, 64, NK], BF16)
        dsc("KLT", [128, NK], BF16); dsc("KRT", [64, NK], BF16); dsc("V", [128, NKC, 128], BF16)
        dsc("GATE", [D, NK], BF16); dsc("Bd", [2, 64, 128, D], BF16)

        for n in inject:
            a = self.S[n]
            self.I["inj_" + n] = nc.dram_tensor("inj_" + n, list(a.shape), a.dtype, kind="ExternalInput").ap()

    def phase_inject(self):
        ph = Phase(self.nc, "inj")
        for n in self.inject:
            ph.dma("sp", self.S[n], self.I["inj_" + n], semkey="inj_" + n)
        ph.emit()

    def phase_mod(self):
        nc, I, S = self.nc, self.I, self.S
        ph = Phase(nc, "p0")
        actT, actT_k = ph.sb([128, 8, 2], F32, "actT")
        ph.dma("sp", actT[:], I["cvecT"], writes=[actT_k], semkey=actT_k)
        OP_act(ph, actT[:], actT[:], AF.Silu, [actT_k], [actT_k])
        wrot = rot_sb(ph, 3, [128, 3 * D], F32, "wada")
        psb = [ph.ps([128, 512], F32, "pm") for _ in range(6)]
        for i in range(4):
            bt, bt_k = ph.sb([2, 3 * D], F32, "bada")
            gt, gt_k = ph.sb([2, D], F32, "ng")
            for s in range(2):
                ph.dma("sp", bt[s:s + 1, :], I["b_ada"][i:i + 1, :], writes=[bt_k], semkey=bt_k)
                ph.dma("sp", gt[s:s + 1, :], I["norm_g"][i:i + 1, :], writes=[gt_k], semkey=gt_k)
            for c in range(8):
                wt, wt_k = wrot.next()
                ph.dma("sp", wt[:], I["w_ada"][i, c * 128:(c + 1) * 128, :], writes=[wt_k], semkey=wt_k)
                for n in range(6):
                    OP_mm(ph, psb[n][0][0:2, :], actT[:, c, :], wt[:, n * 512:(n + 1) * 512], c == 0, c == 7,
                          [actT_k, wt_k], [psb[n][1]])
            md, md_k = ph.sb([2, 3 * D], F32, "mod")
            for n in range(6):
                OP_tt(ph, "dve", md[:, n * 512:(n + 1) * 512], psb[n][0][0:2, :], bt[:, n * 512:(n + 1) * 512],
                      ALU.add, [psb[n][1], bt_k], [md_k])
            OP_stt(ph, "dve", md[:, D:2 * D], md[:, D:2 * D], 1.0, gt[:], ALU.add, ALU.mult, [md_k, gt_k], [md_k])
            for s in range(2):
                ph.dma("sp", S["modd"][i, s:s + 1, 0, :], md[s:s + 1, D:2 * D], reads=[md_k], semkey=md_k + "o")
                ph.dma("sp", S["modd"][i, s:s + 1, 1, :], md[s:s + 1, 0:D], reads=[md_k], semkey=md_k + "o")
                ph.dma("sp", S["modd"][i, s:s + 1, 2, :], md[s:s + 1, 2 * D:3 * D], reads=[md_k], semkey=md_k + "o")
        ph.emit()

    def load_bcast(self, ph, dram_row_ap, n, name):
        t, k = ph.sb([128, n], F32, name)
        ph.dma("sp", t[:], dram_row_ap.partition_broadcast(128), writes=[k], semkey=k)
        return t, k

    def load_w_bf16(self, ph, dst, dst_k, src, stage):
        st, st_k = stage.next()
        n = src.shape[-1]
        ph.dma("sp", st[:, 0:n], src, writes=[st_k], semkey=st_k)
        OP_cp(ph, "pool", dst, st[:, 0:n], [st_k], [dst_k])

    def u_tile(self, ph, R, src_ap, G1b, SHb, mod_keys, uT, uT_k, slot, ident, ident_k, key2=None):
        xt, xt_k = R["x"].next()
        if isinstance(src_ap, (list, tuple)):
            hp = 128 // len(src_ap)
            for j, a in enumerate(src_ap):
                ph.dma("sp", xt[j * hp:(j + 1) * hp, :], a, writes=[xt_k], semkey=xt_k)
        else:
            ph.dma("sp", xt[:], src_ap, writes=[xt_k], semkey=xt_k)
        junk, junk_k = R["junk"].next()
        ss, ss_k = R["ss"].next()
        OP_act(ph, junk[:], xt[:], AF.Square, [xt_k], [junk_k, ss_k], scale=1.0 / 32.0, accum=ss[:, 0:1])
        OP_rstd(ph, ss[:, 1:2], ss[:, 0:1], ss_k)
        un, un_k = R["un"].next()
        OP_stt(ph, "dve", un[:], xt[:], ss[:, 1:2], G1b[:], ALU.mult, ALU.mult, [xt_k, ss_k, mod_keys[0]], [un_k])
        ub, ub_k = R["ub"].next()
        OP_tt(ph, "pool", ub[:], un[:], SHb[:], ALU.add, [un_k, mod_keys[1]], [ub_k])
        if key2 is not None:
            ph.cur_key = key2
        pT, pT_k = R["pT"].next()
        for c in range(8):
            OP_tr(ph, pT[:, c * 128:(c + 1) * 128], ub[:, c * 128:(c + 1) * 128], ident[:], [ub_k, ident_k], [pT_k])
        OP_cp(ph, "act", uT[:, :, slot * 128:(slot + 1) * 128], pT[:].rearrange("p (c t) -> p c t", c=8),
              [pT_k], [uT_k + f"_{slot}"])
        return xt, xt_k

    def phase_mla_a(self, li, src_lat, src_ctx, ctx_q):
        nc, I, S = self.nc, self.I, self.S
        idx = li // 2
        ph = Phase(nc, f"a{li}")
        ident, ident_k = ph.sb([128, 128], BF16, "ident")
        ph.dma("sp", ident[:], I["ident"], writes=[ident_k], semkey=ident_k)
        mods = {}
        for s in range(2):
            mods[s] = (self.load_bcast(ph, S["modd"][li, s:s + 1, 0, :], D, "g1b"),
                       self.load_bcast(ph, S["modd"][li, s:s + 1, 1, :], D, "shb"))
        gqa, gqa_k = self.load_bcast(ph, I["mla_g_qa"][idx:idx + 1, :], 256, "gqa")
        gkva, gkva_k = self.load_bcast(ph, I["mla_g_kva"][idx:idx + 1, :], 128, "gkva")
        stage = rot_sb(ph, 2, [128, 1536], F32, "wst")
        w_in, w_in_k = ph.sb([128, 8, 1472], BF16, "w_in")
        for c in range(8):
            self.load_w_bf16(ph, w_in[:, c, :], w_in_k, I["mla_w_in"][idx, c * 128:(c + 1) * 128, :], stage)
        w_q, w_q_k = ph.sb([128, 2, 1536], BF16, "w_q")
        w_qs, w_qs_k = ph.sb([128, 2, 512], BF16, "w_qs")
        for c in range(2):
            self.load_w_bf16(ph, w_q[:, c, :], w_q_k, I["mla_w_qup"][idx, c * 128:(c + 1) * 128, :], stage)
            self.load_w_bf16(ph, w_qs[:, c, :], w_qs_k, I["mla_w_qup_sw"][idx, c * 128:(c + 1) * 128, :], stage)
        w_uk, w_uk_k = ph.sb([128, 8, 128], BF16, "w_uk")
        for h in range(8):
            self.load_w_bf16(ph, w_uk[:, h, :], w_uk_k, I["mla_w_ukT"][idx, h], stage)

        R = {"x": rot_sb(ph, 3, [128, D], F32, "x"), "junk": rot_sb(ph, 1, [128, D], BF16, "junk"),
             "ss": rot_sb(ph, 4, [128, 8], F32, "ss"), "un": rot_sb(ph, 3, [128, D], F32, "un"),
             "ub": rot_sb(ph, 3, [128, D], BF16, "ub"), "pT": rot_ps(ph, 2, [128, D], BF16, "pT")}
        ss2_r = rot_sb(ph, 3, [128, 8], F32, "ss2")
        junk2_r = rot_sb(ph, 1, [128, 512], BF16, "junk2")
        uTs = rot_sb(ph, 2, [128, 8, 512], BF16, "uT")
        cqTs = rot_sb(ph, 2, [128, 2, 512], BF16, "cqT")
        psm = rot_ps(ph, 1, [128, 512], F32, "psm")
        pT2 = rot_ps(ph, 1, [128, 512], BF16, "pT2")
        pbig = rot_ps(ph, 4, [128, 512], F32, "pbig")
        rk = rot_sb(ph, 2, [128, 2, 64], F32, "rk")
        sm = rot_sb(ph, 2, [128, 512], BF16, "sm")
        tmpk = rot_sb(ph, 2, [128, 3, 64], F32, "tmpk")
        kst = rot_sb(ph, 2, [128, 2, 128], BF16, "kst")
        gst = rot_sb(ph, 3, [128, 512], BF16, "gst")
        qn_r = rot_sb(ph, 2, [128, 512], BF16, "qn")
        qa_r = rot_sb(ph, 3, [128, 512], BF16, "qa")
        qr_r = rot_sb(ph, 3, [64, 512], BF16, "qr")
        rq_r = rot_sb(ph, 2, [64, 2, 512], F32, "rq")
        t12 = rot_sb(ph, 2, [64, 2, 512], F32, "t12")

        groups = [("lat", g * 512, 512) for g in range(T // 512)] + [("ctx", 0, NCTX)]
        tg = 0
        ph.cur_key = -1.0
        for kind, t0, ntok in groups:
            s = 0 if kind == "lat" else 1
            (G1b, G1b_k), (SHb, SHb_k) = mods[s]
            nt = ntok // 128
            uT, uT_k = uTs.next()
            cqT, cqT_k = cqTs.next()
            gcol0 = t0 if kind == "lat" else T
            kcol0 = (NCTX + t0) if kind == "lat" else 0
            uT_toks = [uT_k + f"_{j}" for j in range(nt)]
            cq_toks = [cqT_k + f"_{j}" for j in range(nt)]
            for j in range(nt):
                r0 = t0 + j * 128
                src = (src_lat if kind == "lat" else src_ctx)[r0:r0 + 128, :]
                ph.cur_key = float(tg)
                self.u_tile(ph, R, src, G1b, SHb, (G1b_k, SHb_k), uT, uT_k, j, ident, ident_k, key2=tg + 1.5)
                ph.cur_key = tg + 1.5
                tg += 1
                pm, pm_k = psm.next()
                for c in range(8):
                    OP_mm(ph, pm[:, 0:448], uT[:, c, j * 128:(j + 1) * 128], w_in[:, c, 0:448], c == 0, c == 7,
                          [uT_toks[j], w_in_k], [pm_k])
                ss, ss_k = ss2_r.next()
                junk, junk_k = junk2_r.next()
                OP_act(ph, junk[:, 0:256], pm[:, 0:256], AF.Square, [pm_k], [junk_k, ss_k], scale=1.0 / 16.0,
                       accum=ss[:, 0:1])
                OP_act(ph, junk[:, 256:384], pm[:, 256:384], AF.Square, [pm_k], [junk_k, ss_k],
                       scale=1.0 / math.sqrt(128.0), accum=ss[:, 1:2])
                OP_rstd(ph, ss[:, 2:4], ss[:, 0:2], ss_k)
                smt, smt_k = sm.next()
                OP_stt(ph, "dve", smt[:, 0:256], pm[:, 0:256], ss[:, 2:3], gqa[:], ALU.mult, ALU.mult,
                       [pm_k, ss_k, gqa_k], [smt_k])
                OP_stt(ph, "dve", smt[:, 256:384], pm[:, 256:384], ss[:, 3:4], gkva[:], ALU.mult, ALU.mult,
                       [pm_k, ss_k, gkva_k], [smt_k])
                if kind == "lat":
                    rkt, rkt_k = rk.next()
                    ph.dma("sp", rkt[:], I["rope_k"][r0:r0 + 128], writes=[rkt_k], semkey=rkt_k)
                    tk, tk_k = tmpk.next()
                    OP_tt(ph, "dve", tk[:, 0, :], pm[:, 384:448], rkt[:, 0, :], ALU.mult, [pm_k, rkt_k], [tk_k])
                    pv = pm[:, 384:448].rearrange("p (b h d) -> p b h d", b=2, h=2)
                    sv = rkt[:, 1, :].rearrange("p (b h d) -> p b h d", b=2, h=2)
                    dv = tk[:, 1, :].rearrange("p (b h d) -> p b h d", b=2, h=2)
                    OP_tt(ph, "dve", dv[:, :, 0, :], pv[:, :, 1, :], sv[:, :, 0, :], ALU.mult, [pm_k, rkt_k], [tk_k])
                    OP_tt(ph, "dve", dv[:, :, 1, :], pv[:, :, 0, :], sv[:, :, 1, :], ALU.mult, [pm_k, rkt_k], [tk_k])
                    OP_tt(ph, "dve", smt[:, 384:448], tk[:, 0, :], tk[:, 1, :], ALU.add, [tk_k], [smt_k])
                else:
                    OP_cp(ph, "dve", smt[:, 384:448], pm[:, 384:448], [pm_k], [smt_k])
                ph.dma("pool", S["V"][:, (kcol0 + j * 128) // 128, :], smt[:, 256:384], reads=[smt_k],
                       semkey=smt_k + "v")
                p2, p2_k = pT2.next()
                OP_tr(ph, p2[:, 0:128], smt[:, 0:128], ident[:], [smt_k, ident_k], [p2_k])
                OP_tr(ph, p2[:, 128:256], smt[:, 128:256], ident[:], [smt_k, ident_k], [p2_k])
                OP_tr(ph, p2[:, 256:384], smt[:, 256:384], ident[:], [smt_k, ident_k], [p2_k])
                OP_tr(ph, p2[0:64, 384:512], smt[:, 384:448], ident[:], [smt_k, ident_k], [p2_k])
                OP_cp(ph, "act", cqT[:, :, j * 128:(j + 1) * 128], p2[:, 0:256].rearrange("p (c t) -> p c t", c=2),
                      [p2_k], [cq_toks[j]])
                ks, ks_k = kst.next()
                OP_cp(ph, "act", ks[:, 0, :], p2[:, 256:384], [p2_k], [ks_k])
                OP_cp(ph, "act", ks[0:64, 1, :], p2[0:64, 384:512], [p2_k], [ks_k])
                ph.dma("pool", S["KLT"][:, kcol0 + j * 128:kcol0 + (j + 1) * 128], ks[:, 0, :], reads=[ks_k],
                       semkey=ks_k + "o")
                ph.dma("pool", S["KRT"][:, kcol0 + j * 128:kcol0 + (j + 1) * 128], ks[0:64, 1, :], reads=[ks_k],
                       semkey=ks_k + "o")
            ph.cur_key = tg - 1 + 3.5
            for n in range(8):
                pb, pb_k = pbig.next()
                for c in range(8):
                    OP_mm(ph, pb[:, 0:ntok], w_in[:, c, 448 + n * 128:448 + (n + 1) * 128], uT[:, c, 0:ntok],
                          c == 0, c == 7, uT_toks + [w_in_k], [pb_k])
                g, g_k = gst.next()
                OP_act(ph, g[:, 0:ntok], pb[:, 0:ntok], AF.Silu, [pb_k], [g_k])
                ph.dma("pool", S["GATE"][n * 128:(n + 1) * 128, gcol0:gcol0 + ntok], g[:, 0:ntok], reads=[g_k],
                       semkey=g_k + "o")
            if kind == "ctx" and not ctx_q:
                continue
            if kind == "lat":
                rq, rq_k = rq_r.next()
                ph.dma("sp", rq[:, :, 0:ntok], I["rope_q"][:, :, t0:t0 + ntok].rearrange("a d t -> d a t"),
                       writes=[rq_k], semkey=rq_k)
            for h in range(8):
                pb, pb_k = pbig.next()
                for c in range(2):
                    OP_mm(ph, pb[:, 0:ntok], w_q[:, c, h * 192:h * 192 + 128], cqT[:, c, 0:ntok], c == 0, c == 1,
                          cq_toks + [w_q_k], [pb_k])
                qn, qn_k = qn_r.next()
                OP_cp(ph, "act", qn[:, 0:ntok], pb[:, 0:ntok], [pb_k], [qn_k])
                pb2, pb2_k = pbig.next()
                OP_mm(ph, pb2[:, 0:ntok], w_uk[:, h, :], qn[:, 0:ntok], True, True, [qn_k, w_uk_k], [pb2_k])
                qa, qa_k = qa_r.next()
                OP_cp(ph, "dve", qa[:, 0:ntok], pb2[:, 0:ntok], [pb2_k], [qa_k])
                ph.dma("pool", S["QA"][h, :, gcol0:gcol0 + ntok], qa[:, 0:ntok], reads=[qa_k], semkey=qa_k + "o")
                pr, pr_k = pbig.next()
                for c in range(2):
                    OP_mm(ph, pr[0:64, 0:ntok], w_q[:, c, h * 192 + 128:h * 192 + 192], cqT[:, c, 0:ntok], c == 0,
                          c == 1, cq_toks + [w_q_k], [pr_k])
                qr, qr_k = qr_r.next()
                if kind == "lat":
                    pss, pss_k = pbig.next()
                    for c in range(2):
                        OP_mm(ph, pss[0:64, 0:ntok], w_qs[:, c, h * 64:(h + 1) * 64], cqT[:, c, 0:ntok], c == 0,
                              c == 1, cq_toks + [w_qs_k], [pss_k])
                    tt, tt_k = t12.next()
                    OP_tt(ph, "dve", tt[:, 0, 0:ntok], pr[0:64, 0:ntok], rq[:, 0, 0:ntok], ALU.mult, [pr_k, rq_k],
                          [tt_k])
                    OP_tt(ph, "dve", tt[:, 1, 0:ntok], pss[0:64, 0:ntok], rq[:, 1, 0:ntok], ALU.mult,
                          [pss_k, rq_k], [tt_k])
                    OP_tt(ph, "pool", qr[:, 0:ntok], tt[:, 0, 0:ntok], tt[:, 1, 0:ntok], ALU.add, [tt_k], [qr_k])
                else:
                    OP_cp(ph, "dve", qr[:, 0:ntok], pr[0:64, 0:ntok], [pr_k], [qr_k])
                ph.dma("pool", S["QR"][h, :, gcol0:gcol0 + ntok], qr[:, 0:ntok], reads=[qr_k], semkey=qr_k + "o")
        ph.emit()

    def phase_mla_b(self, li, src_lat, src_ctx, ctx_q):
        nc, I, S = self.nc, self.I, self.S
        idx = li // 2
        ph = Phase(nc, f"b{li}")
        KLT, KLT_k = ph.sb([128, NK], BF16, "KLT")
        KRT, KRT_k = ph.sb([128, NK], BF16, "KRT")
        V, V_k = ph.sb([128, NKC, 128], BF16, "V")
        ph.op("pool", lambda e: e.memset(KRT[64:128, :], 0.0), [], [KRT_k])
        for q4 in range(4):
            c0, c1 = q4 * (NK // 4), (q4 + 1) * (NK // 4)
            ph.dma("sp", KLT[:, c0:c1], S["KLT"][:, c0:c1], writes=[KLT_k], semkey=KLT_k)
        ph.dma("sp", KRT[0:64, :], S["KRT"], writes=[KRT_k], semkey=KRT_k)
        for q4 in range(3):
            ph.dma("sp", V[:, q4 * 22:(q4 + 1) * 22, :], S["V"][:, q4 * 22:(q4 + 1) * 22, :], writes=[V_k], semkey=V_k)
        ones, ones_k = ph.sb([128, 128], BF16, "ones")
        ph.op("pool", lambda e: e.memset(ones[:], 1.0), [], [ones_k])
        stage = rot_sb(ph, 2, [128, 1024], F32, "wst")
        w_uv, w_uv_k = ph.sb([128, 8, 128], BF16, "w_uv")
        w_o, w_o_k = ph.sb([128, 8, D], BF16, "w_o")
        for h in range(8):
            self.load_w_bf16(ph, w_uv[:, h, :], w_uv_k, I["mla_w_kvup"][idx, :, h * 256 + 128:h * 256 + 256], stage)
            self.load_w_bf16(ph, w_o[:, h, :], w_o_k, I["mla_w_out"][idx, h * 128:(h + 1) * 128, :], stage)
        GT = {}
        for s in range(2):
            GT[s] = self.load_bcast(ph, S["modd"][li, s:s + 1, 2, :], D, "gtb")

        qa_r = rot_sb(ph, 2, [128, 512], BF16, "qa")
        qr_r = rot_sb(ph, 2, [128, 512], BF16, "qr")
        for qr_t, qr_tk in qr_r.items:
            ph.op("pool", (lambda t: (lambda e: e.memset(t[64:128, :], 0.0)))(qr_t), [], [qr_tk])
        gt_r = rot_sb(ph, 4, [128, 512], BF16, "gate")
        ps_s = rot_ps(ph, 3, [128, 1024], F32, "pss")
        ps_o = rot_ps(ph, 1, [128, 512], F32, "pso")
        ps_u = rot_ps(ph, 1, [128, 512], F32, "psu")
        ps_l = ps_u
        posb_r = rot_sb(ph, 2, [128, 512], F32, "posb")
        pt_r = rot_sb(ph, 6, [128, 1024], BF16, "pt")
        tmp_r = rot_sb(ph, 2, [128, 1024], BF16, "ptsum")
        accA_r = rot_sb(ph, 4, [128, 1024], F32, "accA")
        accB_r = rot_sb(ph, 1, [128, 8], F32, "accB")
        t2_r = rot_sb(ph, 2, [128, 512], BF16, "tot2")
        rs_r = rot_sb(ph, 2, [128, 512], F32, "rs")
        op_r = rot_sb(ph, 2, [128, 512], BF16, "op")
        go_r = rot_sb(ph, 2, [128, 8, 512], BF16, "go")
        h_r = rot_sb(ph, 2, [128, D], F32, "h")
        t_r = rot_sb(ph, 2, [128, D], F32, "t")
        LAG = 2
        pending = []

        def run_pending():
            while pending:
                f = pending.pop(0)
                if f is not None:
                    f()

        tiles = [("lat", g * 512, 512) for g in range(T // 512)]
        if ctx_q:
            tiles.append(("ctx", 0, NCTX))

        class HC:
            pass

        heads = []
        for kind, t0, ntok in tiles:
            kcs = list(range(NKC)) if kind == "lat" else [0, 1]
            tile_ctx = {}
            for h in range(8):
                hc = HC()
                hc.kind, hc.t0, hc.ntok, hc.h = kind, t0, ntok, h
                hc.s = 0 if kind == "lat" else 1
                hc.gcol0 = t0 if kind == "lat" else T
                hc.pairs = [(kcs[2 * j], kcs[2 * j + 1]) for j in range(len(kcs) // 2)]
                hc.npair = len(hc.pairs)
                hc.tile_ctx = tile_ctx
                heads.append(hc)
        jobs = [(hc, j) for hc in heads for j in range(hc.npair)]

        def head_begin(hc):
            if hc.npair < 12:
                run_pending()
            if hc.h == 0:
                hc.tile_ctx["go"] = go_r.next()
            hc.go, hc.go_k = hc.tile_ctx["go"]
            ntok = hc.ntok
            hc.qa, hc.qa_k = qa_r.next()
            hc.qr, hc.qr_k = qr_r.next()
            hc.gt, hc.gt_k = gt_r.next()
            ph.dma("sp", hc.qa[:, 0:ntok], S["QA"][hc.h, :, hc.gcol0:hc.gcol0 + ntok], writes=[hc.qa_k],
                   semkey=hc.qa_k)
            ph.dma("sp", hc.qr[0:64, 0:ntok], S["QR"][hc.h, :, hc.gcol0:hc.gcol0 + ntok], writes=[hc.qr_k],
                   semkey=hc.qr_k)
            ph.dma("sp", hc.gt[:, 0:ntok], S["GATE"][hc.h * 128:(hc.h + 1) * 128, hc.gcol0:hc.gcol0 + ntok],
                   writes=[hc.gt_k], semkey=hc.gt_k)
            hc.po, hc.po_k = ps_o.next()
            hc.accA, hc.accA_k = accA_r.next()
            hc.usedA = False
            hc.pts = []

        def emit_qk(hc, j):
            ntok = hc.ntok
            pS, pS_k = ps_s.next()
            for half, kc in enumerate(hc.pairs[j]):
                o_ap = pS[:, half * 512:half * 512 + ntok]
                OP_mm(ph, o_ap, KLT[:, kc * 128:(kc + 1) * 128], hc.qa[:, 0:ntok], True, False, [KLT_k, hc.qa_k],
                      [pS_k])
                OP_mm(ph, o_ap, KRT[:, kc * 128:(kc + 1) * 128], hc.qr[:, 0:ntok], False, True, [KRT_k, hc.qr_k],
                      [pS_k])
            pt, pt_k = pt_r.next()
            if ntok == 512:
                sv, pv = pS[:], pt[:]
            else:
                sv = pS[:].rearrange("p (a t) -> p a t", a=2)[:, :, 0:ntok]
                pv = pt[:].rearrange("p (a t) -> p a t", a=2)[:, :, 0:ntok]
            OP_act(ph, pv, sv, AF.Exp, [pS_k], [pt_k], scale=SCALE)
            hc.pts.append((pt, pt_k))

            def accum(src_ap, src_k):
                accA, accA_k = hc.accA, hc.accA_k
                av = accA[:] if ntok == 512 else accA[:].rearrange("p (a t) -> p a t", a=2)[:, :, 0:ntok]
                if not hc.usedA:
                    OP_cp(ph, "dve", av, src_ap, [src_k], [accA_k])
                else:
                    OP_tt(ph, "dve", av, av, src_ap, ALU.add, [src_k, accA_k], [accA_k])
                hc.usedA = True

            if ntok == 512 and j % 2 == 1:
                tmp, tmp_k = tmp_r.next()
                pprev, pprev_k = hc.pts[j - 1]
                OP_tt(ph, "dve", tmp[:], pprev[:], pt[:], ALU.add, [pprev_k, pt_k], [tmp_k])
                accum(tmp[:], tmp_k)
            elif j == hc.npair - 1:
                accum(pv, pt_k)

        def emit_pv(hc, jj):
            ntok = hc.ntok
            ptj, ptj_k = hc.pts[jj]
            for half, kc in enumerate(hc.pairs[jj]):
                OP_mm(ph, hc.po[:, 0:ntok], V[:, kc, :], ptj[:, half * 512:half * 512 + ntok],
                      jj == 0 and half == 0, jj == hc.npair - 1 and half == 1, [V_k, ptj_k], [hc.po_k])

        def ep_stages(hc):
            h, po, po_k, accA, accA_k = hc.h, hc.posb, hc.posb_k, hc.accA, hc.accA_k
            gt, gt_k, go, go_k, ntok = hc.gt, hc.gt_k, hc.go, hc.go_k, hc.ntok
            st = {}

            def s0():
                st["t2"] = t2_r.next()
                t2, t2_k = st["t2"]
                OP_tt(ph, "dve", t2[:, 0:ntok], accA[:, 0:ntok], accA[:, 512:512 + ntok], ALU.add, [accA_k], [t2_k])

            def s1():
                t2, t2_k = st["t2"]
                st["pl"] = ps_l.next()
                pl, pl_k = st["pl"]
                OP_mm(ph, pl[:, 0:ntok], ones[:], t2[:, 0:ntok], True, True, [ones_k, t2_k], [pl_k])

            def s2():
                pl, pl_k = st["pl"]
                st["rs"] = rs_r.next()
                rs, rs_k = st["rs"]
                OP_act(ph, rs[:, 0:ntok], pl[:, 0:ntok], AF.Ln, [pl_k], [rs_k])
                OP_act(ph, rs[:, 0:ntok], rs[:, 0:ntok], AF.Exp, [rs_k], [rs_k], scale=-1.0)

            def s2b():
                rs, rs_k = st["rs"]
                st["opn"] = op_r.next()
                opn, opn_k = st["opn"]
                OP_tt(ph, "dve", opn[:, 0:ntok], po[:, 0:ntok], rs[:, 0:ntok], ALU.mult, [po_k, rs_k], [opn_k])

            def s3():
                opn, opn_k = st["opn"]
                st["pu"] = ps_u.next()
                pu, pu_k = st["pu"]
                OP_mm(ph, pu[:, 0:ntok], w_uv[:, h, :], opn[:, 0:ntok], True, True, [w_uv_k, opn_k], [pu_k])

            def s4():
                pu, pu_k = st["pu"]
                OP_tt(ph, "dve", go[:, h, 0:ntok], pu[:, 0:ntok], gt[:, 0:ntok], ALU.mult, [pu_k, gt_k],
                      [go_k + f"_{h}"])

            return [s0, None, s1, None, s2, None, None, s2b, None, None, s3, None, None, s4, None]

        def tile_out_stage(hc, ts):
            kind, t0, go, go_k, s = hc.kind, hc.t0, hc.go, hc.go_k, hc.s
            go_toks = [go_k + f"_{h}" for h in range(8)]

            def f():
                (GTb, GTb_k) = GT[s]
                r0 = t0 + ts * 128
                ht, ht_k = h_r.next()
                src = (src_lat if kind == "lat" else src_ctx)[r0:r0 + 128, :]
                dst = (S["h"] if kind == "lat" else S["hc"])[r0:r0 + 128, :]
                ph.dma("sp", ht[:], src, writes=[ht_k], semkey=ht_k)
                tt, tt_k = t_r.next()
                for half in range(2):
                    py, py_k = ps_u.next()
                    for h in range(8):
                        OP_mm(ph, py[:], go[:, h, ts * 128:(ts + 1) * 128], w_o[:, h, half * 512:(half + 1) * 512],
                              h == 0, h == 7, go_toks + [w_o_k], [py_k])
                    OP_tt(ph, "dve", tt[:, half * 512:(half + 1) * 512], py[:], GTb[:, half * 512:(half + 1) * 512],
                          ALU.mult, [py_k, GTb_k], [tt_k])
                OP_tt(ph, "pool", tt[:], tt[:], ht[:], ALU.add, [tt_k, ht_k], [tt_k])
                ph.dma("pool", dst, tt[:], reads=[tt_k], semkey=tt_k + "o")
            return f

        for g in range(len(jobs) + LAG):
            if g < len(jobs):
                hc, j = jobs[g]
                if j == 0:
                    head_begin(hc)
                if hc.npair >= 12 and j >= 2 and pending:
                    f = pending.pop(0)
                    if f is not None:
                        f()
                emit_qk(hc, j)
            if g >= LAG:
                hc2, j2 = jobs[g - LAG]
                emit_pv(hc2, j2)
                if j2 == hc2.npair - 1:
                    hc2.posb, hc2.posb_k = posb_r.next()
                    OP_cp(ph, "dve", hc2.posb[:, 0:hc2.ntok], hc2.po[:, 0:hc2.ntok], [hc2.po_k], [hc2.posb_k])
                    pending.extend(ep_stages(hc2))
                    if hc2.h == 7:
                        for ts in range(hc2.ntok // 128):
                            pending.append(None)
                            pending.append(tile_out_stage(hc2, ts))
                    if hc2.npair < 12:
                        run_pending()
        run_pending()
        ph.emit()

    def phase_f1(self, li, do_ctx):
        nc, I, S = self.nc, self.I, self.S
        idx = li // 2
        ph = Phase(nc, f"f{li}")
        ident, ident_k = ph.sb([128, 128], BF16, "ident")
        ph.dma("sp", ident[:], I["ident"], writes=[ident_k], semkey=ident_k)
        mods = {}
        for s in range(2):
            mods[s] = (self.load_bcast(ph, S["modd"][li, s:s + 1, 0, :], D, "g1b"),
                       self.load_bcast(ph, S["modd"][li, s:s + 1, 1, :], D, "shb"))
        stage = rot_sb(ph, 2, [128, 1024], F32, "wst")
        self.f_stage = stage
        w_in, w_in_k = ph.sb([128, 8, 2048], BF16, "w_in")
        for c in range(8):
            for hf in range(2):
                self.load_w_bf16(ph, w_in[:, c, hf * 1024:(hf + 1) * 1024], w_in_k,
                                 I["fno_w_in"][idx, c * 128:(c + 1) * 128, hf * 1024:(hf + 1) * 1024], stage)
        t1m, t1m_k = ph.sb([128, 2, 128], BF16, "t1m")
        ph.dma("sp", t1m[:], I["dft_t1"], writes=[t1m_k], semkey=t1m_k)
        tw, tw_k = ph.sb([128, 2, 64], F32, "tw")
        ph.dma("sp", tw[:], I["dft_tw"], writes=[tw_k], semkey=tw_k)
        R = {"x": rot_sb(ph, 3, [128, D], F32, "x"), "junk": rot_sb(ph, 1, [128, D], BF16, "junk"),
             "ss": rot_sb(ph, 4, [128, 8], F32, "ss"), "un": rot_sb(ph, 3, [128, D], F32, "un"),
             "ub": rot_sb(ph, 3, [128, D], BF16, "ub"), "pT": rot_ps(ph, 1, [128, D], BF16, "pT")}
        uTs = rot_sb(ph, 2, [128, 8, 512], BF16, "uT")
        pz = rot_ps(ph, 1, [128, 1024], F32, "pz")
        pa = rot_ps(ph, 1, [128, 2, 1024], F32, "pa")
        pg = rot_ps(ph, 1, [128, 512], F32, "pg")
        zt_r = rot_sb(ph, 2, [128, D], BF16, "zt")
        tm_r = rot_sb(ph, 1, [128, 2, D], F32, "tm")
        b_r = rot_sb(ph, 2, [128, 2, D], BF16, "bt")
        gst = rot_sb(ph, 3, [128, 512], BF16, "gst")
        hv = S["h"].rearrange("(t1 t2) d -> t2 t1 d", t2=64)
        if do_ctx:
            zc, zc_k = ph.sb([128, 2, D], BF16, "zc")
            gc, gc_k = ph.sb([128, 8, NCTX], BF16, "gc")
        groups = [("lat", g * 4, 4) for g in range(16)] + ([("ctx", 0, 2)] if do_ctx else [])
        tg = 0
        ph.cur_key = -1.0
        for kind, j0, nt in groups:
            s = 0 if kind == "lat" else 1
            (G1b, G1b_k), (SHb, SHb_k) = mods[s]
            uT, uT_k = uTs.next()
            ntok = nt * 128
            uT_toks = [uT_k + f"_{j}" for j in range(nt)]
            for j in range(nt):
                if kind == "lat":
                    t2 = j0 + j
                    src = hv[t2]
                else:
                    src = S["hc"][j * 128:(j + 1) * 128, :]
                ph.cur_key = float(tg)
                self.u_tile(ph, R, src, G1b, SHb, (G1b_k, SHb_k), uT, uT_k, j, ident, ident_k, key2=tg + 1.5)
                ph.cur_key = tg + 1.5
                tg += 1
                z, z_k = pz.next()
                for half in range(2):
                    for c in range(8):
                        OP_mm(ph, z[:, half * 512:(half + 1) * 512], uT[:, c, j * 128:(j + 1) * 128],
                              w_in[:, c, half * 512:(half + 1) * 512], c == 0, c == 7, [uT_toks[j], w_in_k], [z_k])
                if kind == "ctx":
                    OP_cp(ph, "dve", zc[:, j, :], z[:], [z_k], [zc_k + f"_{j}"])
                    continue
                zt, zt_k = zt_r.next()
                OP_cp(ph, "dve", zt[:], z[:], [z_k], [zt_k])
                a, a_k = pa.next()
                for comp in range(2):
                    for half in range(2):
                        OP_mm(ph, a[:, comp, half * 512:(half + 1) * 512], t1m[:, comp, :],
                              zt[:, half * 512:(half + 1) * 512], True, True, [t1m_k, zt_k], [a_k])
                tm, tm_k = tm_r.next()
                OP_act(ph, tm[:, 0, :], a[:, 1, :], AF.Copy, [a_k, tw_k], [tm_k], scale=tw[:, 1, t2:t2 + 1])
                OP_act(ph, tm[:, 1, :], a[:, 1, :], AF.Copy, [a_k, tw_k], [tm_k], scale=tw[:, 0, t2:t2 + 1])
                bt, bt_k = b_r.next()
                OP_stt(ph, "dve", bt[:, 0, :], a[:, 0, :], tw[:, 0, t2:t2 + 1], tm[:, 0, :], ALU.mult, ALU.subtract,
                       [a_k, tw_k, tm_k], [bt_k])
                OP_stt(ph, "dve", bt[:, 1, :], a[:, 0, :], tw[:, 1, t2:t2 + 1], tm[:, 1, :], ALU.mult, ALU.add,
                       [a_k, tw_k, tm_k], [bt_k])
                for comp in range(2):
                    ph.dma("pool", S["Bd"][comp, t2], bt[:, comp, :], reads=[bt_k], semkey=bt_k + "o")
            ph.cur_key = tg - 1 + 3.5
            for n in range(8):
                g_ps, g_ps_k = pg.next()
                for c in range(8):
                    OP_mm(ph, g_ps[:, 0:ntok], w_in[:, c, D + n * 128:D + (n + 1) * 128], uT[:, c, 0:ntok], c == 0,
                          c == 7, uT_toks + [w_in_k], [g_ps_k])
                if kind == "lat":
                    g, g_k = gst.next()
                    OP_act(ph, g[:, 0:ntok], g_ps[:, 0:ntok], AF.Silu, [g_ps_k], [g_k])
                    ph.dma("pool", S["GATE"][n * 128:(n + 1) * 128, j0 * 128:j0 * 128 + ntok], g[:, 0:ntok],
                           reads=[g_k], semkey=g_k + "o")
                else:
                    OP_act(ph, gc[:, n, :], g_ps[:, 0:ntok], AF.Silu, [g_ps_k], [gc_k + f"_{n}"])
        ph.cur_key = 1e6
        if do_ctx:
            self.f_ctx_tail(ph, li, zc, [zc_k + "_0", zc_k + "_1"], gc, [gc_k + f"_{n}" for n in range(8)], pz, pa, pg)
        ph.emit()

    def f_ctx_tail(self, ph, li, zc, zc_toks, gc, gc_toks, pz, pa, pg):
        nc, I, S = self.nc, self.I, self.S
        idx = li // 2
        dc, dc_k = ph.sb([128, 2, 512], BF16, "dctx")
        ph.dma("sp", dc[:], I["dft_ctx"], writes=[dc_k], semkey=dc_k)
        c1, c1_k = ph.sb([128, 2, 128], BF16, "c1")
        ph.dma("sp", c1[:], I["dft_c1"], writes=[c1_k], semkey=c1_k)
        stage = self.f_stage
        w_o, w_o_k = ph.sb([128, 8, D], BF16, "w_o")
        for c in range(8):
            self.load_w_bf16(ph, w_o[:, c, :], w_o_k, I["fno_w_out"][idx, c * 128:(c + 1) * 128, :], stage)
        GTb, GTb_k = self.load_bcast(ph, S["modd"][li, 1:2, 2, :], D, "gtc")
        xt, xt_k = ph.sb([128, 8, 512], BF16, "xT")
        gy, gy_k = ph.sb([128, 8, NCTX], BF16, "gy")
        for g in range(8):
            p, p_k = pg.next()
            for tc in range(2):
                OP_mm(ph, p[:], zc[:, tc, g * 128:(g + 1) * 128], dc[:, tc, :], tc == 0, tc == 1,
                      zc_toks + [dc_k], [p_k])
            OP_cp(ph, "dve", xt[:, g, :], p[:], [p_k], [xt_k + f"_{g}"])
            p2, p2_k = pg.next()
            OP_mm(ph, p2[:, 0:NCTX], c1[:, 0, :], xt[:, g, 0:NCTX], True, False, [c1_k, xt_k + f"_{g}"], [p2_k])
            OP_mm(ph, p2[:, 0:NCTX], c1[:, 1, :], xt[:, g, NCTX:2 * NCTX], False, True, [c1_k, xt_k + f"_{g}"], [p2_k])
            OP_tt(ph, "dve", gy[:, g, :], p2[:, 0:NCTX], gc[:, g, :], ALU.mult, [p2_k, gc_toks[g]], [gy_k + f"_{g}"])
        gy_toks = [gy_k + f"_{g}" for g in range(8)]
        h_r = rot_sb(ph, 2, [128, D], F32, "hcx")
        t_r = rot_sb(ph, 2, [128, D], F32, "tcx")
        for ts in range(2):
            ht, ht_k = h_r.next()
            ph.dma("sp", ht[:], S["hc"][ts * 128:(ts + 1) * 128, :], writes=[ht_k], semkey=ht_k)
            tt, tt_k = t_r.next()
            for half in range(2):
                py, py_k = pg.next()
                for g in range(8):
                    OP_mm(ph, py[:], gy[:, g, ts * 128:(ts + 1) * 128], w_o[:, g, half * 512:(half + 1) * 512],
                          g == 0, g == 7, gy_toks + [w_o_k], [py_k])
                OP_tt(ph, "dve", tt[:, half * 512:(half + 1) * 512], py[:], GTb[:, half * 512:(half + 1) * 512],
                      ALU.mult, [py_k, GTb_k], [tt_k])
            OP_tt(ph, "pool", tt[:], tt[:], ht[:], ALU.add, [tt_k, ht_k], [tt_k])
            ph.dma("pool", S["hc"][ts * 128:(ts + 1) * 128, :], tt[:], reads=[tt_k], semkey=tt_k + "o")

    def phase_f2(self, li, final):
        nc, I, S = self.nc, self.I, self.S
        idx = li // 2
        ph = Phase(nc, f"g{li}")
        r2, r2_k = ph.sb([128, 128], BF16, "r2")
        ph.dma("sp", r2[:], I["dft_r2"], writes=[r2_k], semkey=r2_k)
        c1, c1_k = ph.sb([128, 2, 128], BF16, "c1")
        ph.dma("sp", c1[:], I["dft_c1"], writes=[c1_k], semkey=c1_k)
        stage = rot_sb(ph, 2, [128, 1024], F32, "wst")
        w_o, w_o_k = ph.sb([128, 8, D], BF16, "w_o")
        for c in range(8):
            self.load_w_bf16(ph, w_o[:, c, :], w_o_k, I["fno_w_out"][idx, c * 128:(c + 1) * 128, :], stage)
        GTb, GTb_k = self.load_bcast(ph, S["modd"][li, 0:1, 2, :], D, "gtb")
        if final:
            FGb, FGb_k = self.load_bcast(ph, I["final_g"], D, "fg")
        KB = 8
        b_r = rot_sb(ph, 2, [128, KB, D], BF16, "bblk")
        g_r = rot_sb(ph, 2, [128, 8, KB * 128], BF16, "gblk")
        px = rot_ps(ph, 2, [128, 512], F32, "px")
        pyy = rot_ps(ph, 2, [128, 512], F32, "pyy")
        po = rot_ps(ph, 2, [128, 512], F32, "po")
        xt_r = rot_sb(ph, 2, [128, 8, KB * 128], BF16, "xT")
        gy_r = rot_sb(ph, 2, [128, 8, KB * 64], BF16, "gy")
        h_r = rot_sb(ph, 2, [128, D], F32, "h")
        t_r = rot_sb(ph, 2, [128, D], F32, "t")
        ss_r = rot_sb(ph, 4, [128, 8], F32, "ss")
        junk, junk_k = ph.sb([128, D], BF16, "junk")
        Bv = S["Bd"].rearrange("c t k d -> (c t) k d")
        hv = S["h"].rearrange("(k2 k1) d -> k1 k2 d", k1=128)
        ov = self.out.rearrange("(k2 k1) d -> k1 k2 d", k1=128)
        ph.cur_key = -1.0
        for blk in range(128 // KB):
            k10 = blk * KB
            ph.cur_key = float(blk)
            bb, bb_k = b_r.next()
            for q in range(2):
                ph.dma("sp", bb[:, q * 4:(q + 1) * 4, :], Bv[:, k10 + q * 4:k10 + (q + 1) * 4, :], writes=[bb_k],
                       semkey=bb_k)
            gb, gb_k = g_r.next()
            t20 = k10 % 64
            hi = k10 // 64
            for cc in range(8):
                ph.dma("sp", gb[:, cc, :], S["GATE"][cc * 128:(cc + 1) * 128, t20 * 128:(t20 + KB) * 128],
                       writes=[gb_k], semkey=gb_k)
            xt, xt_k = xt_r.next()
            gy, gy_k = gy_r.next()
            for cc in range(8):
                for q in range(KB // 4):
                    p, p_k = px.next()
                    for kk in range(4):
                        k1 = q * 4 + kk
                        OP_mm(ph, p[:, kk * 128:(kk + 1) * 128], bb[:, k1, cc * 128:(cc + 1) * 128], r2[:], True, True,
                              [bb_k, r2_k], [p_k])
                    xw = xt[:, cc, :].rearrange("p (c k j) -> p k c j", c=2, k=KB)[:, q * 4:(q + 1) * 4, :, :]
                    OP_cp(ph, "act", xw, p[:].rearrange("p (k c j) -> p k c j", k=4, c=2), [p_k], [xt_k + f"_{cc}"])
                ph.cur_key = blk + 1.5
                p2, p2_k = pyy.next()
                p2v = p2[:].rearrange("p (k j) -> p k j", k=KB)
                OP_mm(ph, p2[:], c1[:, 0, :], xt[:, cc, 0:KB * 64], True, False, [c1_k, xt_k + f"_{cc}"], [p2_k])
                OP_mm(ph, p2[:], c1[:, 1, :], xt[:, cc, KB * 64:KB * 128], False, True, [c1_k, xt_k + f"_{cc}"], [p2_k])
                gv = gb[:, cc, :].rearrange("p (t a two) -> p t a two", t=KB, two=2)[:, :, :, hi]
                OP_tt(ph, "dve", gy[:, cc, :].rearrange("p (k j) -> p k j", k=KB), p2v, gv, ALU.mult,
                      [p2_k, gb_k], [gy_k + f"_{cc}"])
                ph.cur_key = float(blk)
            gy_toks = [gy_k + f"_{cc}" for cc in range(8)]
            ph.cur_key = blk + 2.5
            for ts in range(KB // 2):
                ht, ht_k = h_r.next()
                k1a = k10 + 2 * ts
                ph.dma("sp", ht[0:64, :], hv[k1a], writes=[ht_k], semkey=ht_k)
                ph.dma("sp", ht[64:128, :], hv[k1a + 1], writes=[ht_k], semkey=ht_k)
                tt, tt_k = t_r.next()
                for half in range(2):
                    py, py_k = po.next()
                    for cc in range(8):
                        OP_mm(ph, py[:], gy[:, cc, ts * 128:(ts + 1) * 128], w_o[:, cc, half * 512:(half + 1) * 512],
                              cc == 0, cc == 7, gy_toks + [w_o_k], [py_k])
                    OP_tt(ph, "dve", tt[:, half * 512:(half + 1) * 512], py[:], GTb[:, half * 512:(half + 1) * 512],
                          ALU.mult, [py_k, GTb_k], [tt_k])
                OP_tt(ph, "pool", tt[:], tt[:], ht[:], ALU.add, [tt_k, ht_k], [tt_k])
                if final:
                    ss, ss_k = ss_r.next()
                    OP_act(ph, junk[:], tt[:], AF.Square, [tt_k], [junk_k, ss_k], scale=1.0 / 32.0, accum=ss[:, 0:1])
                    OP_rstd(ph, ss[:, 1:2], ss[:, 0:1], ss_k)
                    OP_stt(ph, "dve", tt[:], tt[:], ss[:, 1:2], FGb[:], ALU.mult, ALU.mult, [tt_k, ss_k, FGb_k], [tt_k])
                    dsts = (ov[k1a], ov[k1a + 1])
                else:
                    dsts = (hv[k1a], hv[k1a + 1])
                ph.dma("pool", dsts[0], tt[0:64, :], reads=[tt_k], semkey=tt_k + "o")
                ph.dma("pool", dsts[1], tt[64:128, :], reads=[tt_k], semkey=tt_k + "o")
        ph.emit()

    def build(self):
        S, I = self.S, self.I
        steps = [
            lambda: self.phase_mod(),
            lambda: self.phase_mla_a(0, I["x"], I["ctx"], True),
            lambda: self.phase_mla_b(0, I["x"], I["ctx"], True),
            lambda: self.phase_f1(1, True),
            lambda: self.phase_f2(1, False),
            lambda: self.phase_mla_a(2, S["h"], S["hc"], False),
            lambda: self.phase_mla_b(2, S["h"], S["hc"], False),
            lambda: self.phase_f1(3, False),
            lambda: self.phase_f2(3, True),
        ]
        if self.inject:
            self.phase_inject()
        if self.steps is not None:
            for i in self.steps:
                steps[i]()
            return self.nc
        n = len(steps) if self.upto is None else self.upto
        for i, st in enumerate(steps[:n]):
            with self.nc.named_scope(f"step{i}"):
                st()
        return self.nc


def _tables():
    bf = ml_dtypes.bfloat16
    row = np.repeat(np.arange(128), 64).astype(np.float32)
    col = np.tile(np.arange(64), 128).astype(np.float32)
    inv = (1.0 / (10000.0 ** (np.arange(0, 32, 2, dtype=np.float32) / 32.0))).astype(np.float32)
    ar = row[:, None] * inv[None, :]
    ac = col[:, None] * inv[None, :]
    cos_full = np.concatenate([np.cos(ar), np.cos(ar), np.cos(ac), np.cos(ac)], -1).astype(np.float32)
    sin_sgn = np.concatenate([-np.sin(ar), np.sin(ar), -np.sin(ac), np.sin(ac)], -1).astype(np.float32)
    rope_k = np.ascontiguousarray(np.stack([cos_full, sin_sgn], 1))
    rope_q = np.ascontiguousarray(np.stack([cos_full.T, sin_sgn.T], 0))
    ident = np.eye(128, dtype=np.float32).astype(bf)
    n128 = np.arange(128, dtype=np.float64)
    a1 = 2 * np.pi * np.outer(n128, n128) / 128.0
    sc = 1.0 / math.sqrt(128.0)
    dft_t1 = np.stack([np.cos(a1) * sc, np.sin(a1) * sc], 1).astype(np.float32).astype(bf)
    atw = 2 * np.pi * np.outer(n128, np.arange(64)) / 8192.0
    dft_tw = np.ascontiguousarray(np.stack([np.cos(atw), np.sin(atw)], 1).astype(np.float32))
    n64 = np.arange(64, dtype=np.float64)
    a2 = 2 * np.pi * np.outer(n64, n64) / 64.0
    C2, S2 = np.cos(a2) / 8.0, np.sin(a2) / 8.0
    dft_r2 = np.block([[C2, S2], [-S2, C2]]).astype(np.float32).astype(bf)
    dft_c1 = np.stack([np.cos(a1) * sc, -np.sin(a1) * sc], 1).astype(np.float32).astype(bf)
    n256 = np.arange(256, dtype=np.float64)
    a3 = 2 * np.pi * np.outer(n256, n256) / 256.0
    CS = np.concatenate([np.cos(a3), np.sin(a3)], 1) / 16.0
    dft_ctx = np.ascontiguousarray(CS.reshape(2, 128, 512).transpose(1, 0, 2)).astype(np.float32).astype(bf)
    return dict(rope_k=rope_k, rope_q=rope_q, ident=ident, dft_t1=dft_t1, dft_tw=dft_tw, dft_r2=dft_r2,
                dft_c1=dft_c1, dft_ctx=dft_ctx)


def make_in_maps(inputs, cores):
    f = lambda a: np.ascontiguousarray(np.asarray(a, dtype=np.float32))
    tabs = _tables()
    w_qup = f(inputs["mla_w_qup"])
    perm = np.array([(d + 16) if (d % 32) < 16 else (d - 16) for d in range(64)])
    w_qup_sw = np.stack([np.concatenate([w_qup[i][:, h * 192 + 128 + perm] for h in range(8)], 1) for i in range(2)])
    w_kvup = f(inputs["mla_w_kvup"])
    w_ukT = np.stack([np.stack([w_kvup[i][:, h * 256:h * 256 + 128].T for h in range(8)]) for i in range(2)])
    shared = dict(
        norm_g=f(inputs["norm_g"]), w_ada=f(inputs["w_ada"]), b_ada=f(inputs["b_ada"]),
        mla_w_in=f(inputs["mla_w_in"]), mla_g_qa=f(inputs["mla_g_qa"]), mla_w_qup=w_qup,
        mla_w_qup_sw=np.ascontiguousarray(w_qup_sw), mla_g_kva=f(inputs["mla_g_kva"]), mla_w_kvup=w_kvup,
        mla_w_ukT=np.ascontiguousarray(w_ukT), mla_w_out=f(inputs["mla_w_out"]), fno_w_in=f(inputs["fno_w_in"]),
        fno_w_out=f(inputs["fno_w_out"]), final_g=f(inputs["final_g"]).reshape(1, D), **tabs)
    x, c, ctx, c_ctx = f(inputs["x"]), f(inputs["c"]), f(inputs["ctx"]), f(inputs["c_ctx"])
    maps = []
    for core in cores:
        b = core % 4
        cv = np.stack([c[b], c_ctx], 0)
        cvecT = np.ascontiguousarray(cv.reshape(2, 8, 128).transpose(2, 1, 0))
        m = dict(shared)
        m.update(x=x[b], ctx=ctx[b], cvecT=cvecT)
        maps.append(m)
    return maps


def kernel(**inputs):
    nc = Builder().build()
    cores = list(range(4))
    in_maps = make_in_maps(inputs, cores)
    res = run_bass_kernel_spmd(nc, in_maps, core_ids=cores)
    out = np.stack([np.asarray(res.results[b]["out"]) for b in range(4)], 0)
    return out.astype(np.float32)
```

```python
import contextlib
import math
import numpy as np
import ml_dtypes
import concourse.bass as bass
import concourse.mybir as mybir
from concourse.bass_utils import run_bass_kernel_spmd

F32 = mybir.dt.float32
BF16 = mybir.dt.bfloat16
AF = mybir.ActivationFunctionType
ALU = mybir.AluOpType

T = 8192
D = 1024
NCTX = 256
NK = T + NCTX
NKC = NK // 128
EPS = 1e-6
SCALE = 1.0 / math.sqrt(192.0)


class Op:
    __slots__ = ("eng", "fn", "deps", "dma", "semkey", "signal", "sem", "val", "key", "idx", "impl")

    def __init__(self, eng, fn, dma, semkey):
        self.eng = eng
        self.fn = fn
        self.deps = []
        self.dma = dma
        self.semkey = semkey
        self.signal = False
        self.sem = None
        self.val = 0
        self.key = 0.0
        self.idx = 0
        self.impl = []


class Phase:
    ENGS = ("sp", "pe", "act", "dve", "pool")

    def __init__(self, nc, name):
        self.nc = nc
        self.name = name
        self.ops = []
        self.last_writer = {}
        self.readers = {}
        self.stack = contextlib.ExitStack()
        self.n_alloc = 0
        self.cur_key = None

    def sb(self, shape, dtype, name="t"):
        self.n_alloc += 1
        t = self.stack.enter_context(self.nc.sbuf_tensor(f"{self.name}_{name}{self.n_alloc}", list(shape), dtype))
        return t, f"{name}{self.n_alloc}"

    def ps(self, shape, dtype, name="p"):
        self.n_alloc += 1
        t = self.stack.enter_context(self.nc.psum_tensor(f"{self.name}_{name}{self.n_alloc}", list(shape), dtype))
        return t, f"{name}{self.n_alloc}"

    def op(self, eng, fn, reads=(), writes=(), dma=False, semkey=None):
        o = Op(eng, fn, dma, semkey)
        o.idx = len(self.ops)
        o.key = self.cur_key if self.cur_key is not None else 0.0
        deps = []
        for t in list(reads) + list(writes):
            w = self.last_writer.get(t)
            if w is not None:
                deps.append(w)
        for t in writes:
            deps.extend(self.readers.get(t, ()))
        seen = set()
        for d in deps:
            if id(d) in seen or d is o:
                continue
            seen.add(id(d))
            if d.eng == "pe" and eng == "pe" and not d.dma and not dma:
                o.impl.append(d)
                continue
            o.deps.append(d)
            d.signal = True
        for t in writes:
            self.last_writer[t] = o
            self.readers[t] = []
        for t in reads:
            self.readers.setdefault(t, []).append(o)
        self.ops.append(o)
        return o

    def dma(self, queue, out, in_, reads=(), writes=(), semkey=None, slow=False):
        assert semkey is not None
        if slow:
            fn = lambda e: e.dma_start(out=out, in_=in_, allow_slow_non_contiguous=True)
        else:
            fn = lambda e: e.dma_start(out=out, in_=in_)
        return self.op(queue, fn, reads, writes, dma=True, semkey=semkey)

    def emit(self):
        nc = self.nc
        self.ops.sort(key=lambda o: (o.key, o.idx))
        pos = {id(o): i for i, o in enumerate(self.ops)}
        for o in self.ops:
            for d in o.deps + o.impl:
                assert pos[id(d)] < pos[id(o)], f"{self.name}: pipelining key inverts a dependency ({d.eng}->{o.eng})"
        keys = []
        for o in self.ops:
            if o.dma:
                o.signal = True
                if o.semkey not in keys:
                    keys.append(o.semkey)
        sems = {}
        for e in self.ENGS:
            sems[("eng", e)] = nc.alloc_semaphore(name=f"{self.name}_s_{e}")
        for i, k in enumerate(keys):
            sems[("dma", k)] = nc.alloc_semaphore(name=f"{self.name}_d{i}")
        counts = {k: 0 for k in sems}
        for o in self.ops:
            if not o.signal:
                continue
            k = ("dma", o.semkey) if o.dma else ("eng", o.eng)
            counts[k] += 16 if o.dma else 1
            o.sem = k
            o.val = counts[k]
        per_eng = {e: [] for e in self.ENGS}
        for o in self.ops:
            per_eng[o.eng].append(o)
        final_dma = {k: v for k, v in counts.items() if k[0] == "dma" and v > 0}

        def body(ename):
            def f(eng):
                waited = {}
                for o in per_eng[ename]:
                    need = {}
                    for d in o.deps:
                        if need.get(d.sem, 0) < d.val:
                            need[d.sem] = d.val
                    for k, v in need.items():
                        if waited.get(k, 0) >= v:
                            continue
                        eng.wait_ge(sems[k], v)
                        waited[k] = v
                    ins = o.fn(eng)
                    if o.signal:
                        ins.then_inc(sems[o.sem], 16 if o.dma else 1)
                if ename == "sp":
                    for k, v in final_dma.items():
                        eng.wait_ge(sems[k], v)
                    for e2 in ("pe", "act", "dve", "pool"):
                        v = counts[("eng", e2)]
                        if v > 0:
                            eng.wait_ge(sems[("eng", e2)], v)
            return f

        with nc.Block() as block:
            block.sync(body("sp"))
            block.tensor(body("pe"))
            block.scalar(body("act"))
            block.vector(body("dve"))
            block.gpsimd(body("pool"))
        nc.all_engine_barrier()
        nc.clear_and_free_semaphores(list(sems.values()))
        nc.all_engine_barrier()
        self.stack.close()
        return len(self.ops)


class Rot:
    def __init__(self, items):
        self.items = items
        self.i = 0

    def next(self):
        it = self.items[self.i % len(self.items)]
        self.i += 1
        return it


def rot_sb(ph, n, shape, dtype, name):
    return Rot([ph.sb(shape, dtype, name) for _ in range(n)])


def rot_ps(ph, n, shape, dtype, name):
    return Rot([ph.ps(shape, dtype, name) for _ in range(n)])


def OP_mm(ph, out, lhsT, rhs, start, stop, r, w):
    return ph.op("pe", lambda e: e.matmul(out, lhsT, rhs, start=start, stop=stop), r, w)


def OP_tr(ph, out, in_, ident, r, w):
    return ph.op("pe", lambda e: e.transpose(out, in_, ident), r, w)


def OP_act(ph, out, in_, func, r, w, bias=None, scale=None, accum=None):
    kw = {}
    if bias is not None:
        kw["bias"] = bias
    if scale is not None:
        kw["scale"] = scale
    if accum is not None:
        kw["accum_out"] = accum
    return ph.op("act", lambda e: e.activation(out, in_, func, **kw), r, w)


def OP_ts(ph, eng, out, in0, s1, s2, op0, op1, r, w):
    if op1 is None:
        return ph.op(eng, lambda e: e.tensor_scalar(out, in0, s1, None, op0), r, w)
    return ph.op(eng, lambda e: e.tensor_scalar(out, in0, s1, s2, op0, op1), r, w)


def OP_stt(ph, eng, out, in0, scalar, in1, op0, op1, r, w):
    return ph.op(eng, lambda e: e.scalar_tensor_tensor(out, in0, scalar, in1, op0, op1), r, w)


def OP_tt(ph, eng, out, in0, in1, op, r, w):
    return ph.op(eng, lambda e: e.tensor_tensor(out, in0, in1, op), r, w)


def OP_recip(ph, out, in_, r, w):
    return ph.op("dve", lambda e: e.reciprocal(out, in_), r, w)


def OP_rstd(ph, out, in_, k):
    OP_act(ph, out, in_, AF.Sqrt, [k], [k], bias=EPS)
    return OP_recip(ph, out, out, [k], [k])


def OP_cp(ph, eng, out, in_, r, w):
    if eng == "act":
        return ph.op(eng, lambda e: e.activation(out, in_, AF.Copy), r, w)
    return ph.op(eng, lambda e: e.tensor_copy(out, in_), r, w)


class Builder:
    def __init__(self, debug=False, upto=None, steps=None, inject=()):
        self.debug = debug
        self.upto = upto
        self.steps = steps
        self.inject = inject
        nc = bass.Bass("TRN2", target_bir_lowering=False)
        self.nc = nc
        self.I = {}
        self.S = {}
        din = lambda n, s, dt=F32: self.I.__setitem__(n, nc.dram_tensor(n, list(s), dt, kind="ExternalInput").ap())
        din("x", [T, D]); din("ctx", [NCTX, D]); din("cvecT", [128, 8, 2])
        din("norm_g", [4, D]); din("w_ada", [4, D, 3 * D]); din("b_ada", [4, 3 * D])
        din("mla_w_in", [2, D, 1472]); din("mla_g_qa", [2, 256]); din("mla_w_qup", [2, 256, 1536])
        din("mla_w_qup_sw", [2, 256, 512]); din("mla_g_kva", [2, 128]); din("mla_w_kvup", [2, 128, 2048])
        din("mla_w_ukT", [2, 8, 128, 128]); din("mla_w_out", [2, D, D])
        din("fno_w_in", [2, D, 2 * D]); din("fno_w_out", [2, D, D]); din("final_g", [1, D])
        din("rope_k", [T, 2, 64]); din("rope_q", [2, 64, T])
        din("ident", [128, 128], BF16); din("dft_t1", [128, 2, 128], BF16); din("dft_tw", [128, 2, 64])
        din("dft_r2", [128, 128], BF16); din("dft_c1", [128, 2, 128], BF16); din("dft_ctx", [128, 2, 512], BF16)
        self.out = nc.dram_tensor("out", [T, D], F32, kind="ExternalOutput").ap()
        kind = "ExternalOutput" if debug else "Internal"
        dsc = lambda n, s, dt: self.S.__setitem__(n, nc.dram_tensor("s_" + n, list(s), dt, kind=kind).ap())
        dsc("h", [T, D], F32); dsc("hc", [NCTX, D], F32); dsc("modd", [4, 2, 3, D], F32)
        dsc("QA", [8, 128, NK], BF16); dsc("QR", [8, 64, NK], BF16)
        dsc("KLT", [128, NK], BF16); dsc("KRT", [64, NK], BF16); dsc("V", [128, NKC, 128], BF16)
        dsc("GATE", [D, NK], BF16); dsc("Bd", [2, 64, 128, D], BF16)

        for n in inject:
            a = self.S[n]
            self.I["inj_" + n] = nc.dram_tensor("inj_" + n, list(a.shape), a.dtype, kind="ExternalInput").ap()

    def phase_inject(self):
        ph = Phase(self.nc, "inj")
        for n in self.inject:
            ph.dma("sp", self.S[n], self.I["inj_" + n], semkey="inj_" + n)
        ph.emit()

    def phase_mod(self):
        nc, I, S = self.nc, self.I, self.S
        ph = Phase(nc, "p0")
        actT, actT_k = ph.sb([128, 8, 2], F32, "actT")
        ph.dma("sp", actT[:], I["cvecT"], writes=[actT_k], semkey=actT_k)
        OP_act(ph, actT[:], actT[:], AF.Silu, [actT_k], [actT_k])
        wrot = rot_sb(ph, 3, [128, 3 * D], F32, "wada")
        psb = [ph.ps([128, 512], F32, "pm") for _ in range(6)]
        for i in range(4):
            bt, bt_k = ph.sb([2, 3 * D], F32, "bada")
            gt, gt_k = ph.sb([2, D], F32, "ng")
            for s in range(2):
                ph.dma("sp", bt[s:s + 1, :], I["b_ada"][i:i + 1, :], writes=[bt_k], semkey=bt_k)
                ph.dma("sp", gt[s:s + 1, :], I["norm_g"][i:i + 1, :], writes=[gt_k], semkey=gt_k)
            for c in range(8):
                wt, wt_k = wrot.next()
                ph.dma("sp", wt[:], I["w_ada"][i, c * 128:(c + 1) * 128, :], writes=[wt_k], semkey=wt_k)
                for n in range(6):
                    OP_mm(ph, psb[n][0][0:2, :], actT[:, c, :], wt[:, n * 512:(n + 1) * 512], c == 0, c == 7,
                          [actT_k, wt_k], [psb[n][1]])
            md, md_k = ph.sb([2, 3 * D], F32, "mod")
            for n in range(6):
                OP_tt(ph, "dve", md[:, n * 512:(n + 1) * 512], psb[n][0][0:2, :], bt[:, n * 512:(n + 1) * 512],
                      ALU.add, [psb[n][1], bt_k], [md_k])
            OP_stt(ph, "dve", md[:, D:2 * D], md[:, D:2 * D], 1.0, gt[:], ALU.add, ALU.mult, [md_k, gt_k], [md_k])
            for s in range(2):
                ph.dma("sp", S["modd"][i, s:s + 1, 0, :], md[s:s + 1, D:2 * D], reads=[md_k], semkey=md_k + "o")
                ph.dma("sp", S["modd"][i, s:s + 1, 1, :], md[s:s + 1, 0:D], reads=[md_k], semkey=md_k + "o")
                ph.dma("sp", S["modd"][i, s:s + 1, 2, :], md[s:s + 1, 2 * D:3 * D], reads=[md_k], semkey=md_k + "o")
        ph.emit()

    def load_bcast(self, ph, dram_row_ap, n, name):
        t, k = ph.sb([128, n], F32, name)
        ph.dma("sp", t[:], dram_row_ap.partition_broadcast(128), writes=[k], semkey=k)
        return t, k

    def load_w_bf16(self, ph, dst, dst_k, src, stage):
        st, st_k = stage.next()
        n = src.shape[-1]
        ph.dma("sp", st[:, 0:n], src, writes=[st_k], semkey=st_k)
        OP_cp(ph, "pool", dst, st[:, 0:n], [st_k], [dst_k])

    def u_tile(self, ph, R, src_ap, G1b, SHb, mod_keys, uT, uT_k, slot, ident, ident_k, key2=None):
        xt, xt_k = R["x"].next()
        if isinstance(src_ap, (list, tuple)):
            hp = 128 // len(src_ap)
            for j, a in enumerate(src_ap):
                ph.dma("sp", xt[j * hp:(j + 1) * hp, :], a, writes=[xt_k], semkey=xt_k)
        else:
            ph.dma("sp", xt[:], src_ap, writes=[xt_k], semkey=xt_k)
        junk, junk_k = R["junk"].next()
        ss, ss_k = R["ss"].next()
        OP_act(ph, junk[:], xt[:], AF.Square, [xt_k], [junk_k, ss_k], scale=1.0 / 32.0, accum=ss[:, 0:1])
        OP_rstd(ph, ss[:, 1:2], ss[:, 0:1], ss_k)
        un, un_k = R["un"].next()
        OP_stt(ph, "dve", un[:], xt[:], ss[:, 1:2], G1b[:], ALU.mult, ALU.mult, [xt_k, ss_k, mod_keys[0]], [un_k])
        ub, ub_k = R["ub"].next()
        OP_tt(ph, "pool", ub[:], un[:], SHb[:], ALU.add, [un_k, mod_keys[1]], [ub_k])
        if key2 is not None:
            ph.cur_key = key2
        pT, pT_k = R["pT"].next()
        for c in range(8):
            OP_tr(ph, pT[:, c * 128:(c + 1) * 128], ub[:, c * 128:(c + 1) * 128], ident[:], [ub_k, ident_k], [pT_k])
        OP_cp(ph, "act", uT[:, :, slot * 128:(slot + 1) * 128], pT[:].rearrange("p (c t) -> p c t", c=8),
              [pT_k], [uT_k + f"_{slot}"])
        return xt, xt_k

    def phase_mla_a(self, li, src_lat, src_ctx, ctx_q):
        nc, I, S = self.nc, self.I, self.S
        idx = li // 2
        ph = Phase(nc, f"a{li}")
        ident, ident_k = ph.sb([128, 128], BF16, "ident")
        ph.dma("sp", ident[:], I["ident"], writes=[ident_k], semkey=ident_k)
        mods = {}
        for s in range(2):
            mods[s] = (self.load_bcast(ph, S["modd"][li, s:s + 1, 0, :], D, "g1b"),
                       self.load_bcast(ph, S["modd"][li, s:s + 1, 1, :], D, "shb"))
        gqa, gqa_k = self.load_bcast(ph, I["mla_g_qa"][idx:idx + 1, :], 256, "gqa")
        gkva, gkva_k = self.load_bcast(ph, I["mla_g_kva"][idx:idx + 1, :], 128, "gkva")
        stage = rot_sb(ph, 2, [128, 1536], F32, "wst")
        w_in, w_in_k = ph.sb([128, 8, 1472], BF16, "w_in")
        for c in range(8):
            self.load_w_bf16(ph, w_in[:, c, :], w_in_k, I["mla_w_in"][idx, c * 128:(c + 1) * 128, :], stage)
        w_q, w_q_k = ph.sb([128, 2, 1536], BF16, "w_q")
        w_qs, w_qs_k = ph.sb([128, 2, 512], BF16, "w_qs")
        for c in range(2):
            self.load_w_bf16(ph, w_q[:, c, :], w_q_k, I["mla_w_qup"][idx, c * 128:(c + 1) * 128, :], stage)
            self.load_w_bf16(ph, w_qs[:, c, :], w_qs_k, I["mla_w_qup_sw"][idx, c * 128:(c + 1) * 128, :], stage)
        w_uk, w_uk_k = ph.sb([128, 8, 128], BF16, "w_uk")
        for h in range(8):
            self.load_w_bf16(ph, w_uk[:, h, :], w_uk_k, I["mla_w_ukT"][idx, h], stage)

        R = {"x": rot_sb(ph, 3, [128, D], F32, "x"), "junk": rot_sb(ph, 1, [128, D], BF16, "junk"),
             "ss": rot_sb(ph, 4, [128, 8], F32, "ss"), "un": rot_sb(ph, 3, [128, D], F32, "un"),
             "ub": rot_sb(ph, 3, [128, D], BF16, "ub"), "pT": rot_ps(ph, 2, [128, D], BF16, "pT")}
        ss2_r = rot_sb(ph, 3, [128, 8], F32, "ss2")
        junk2_r = rot_sb(ph, 1, [128, 512], BF16, "junk2")
        uTs = rot_sb(ph, 2, [128, 8, 512], BF16, "uT")
        cqTs = rot_sb(ph, 2, [128, 2, 512], BF16, "cqT")
        psm = rot_ps(ph, 1, [128, 512], F32, "psm")
        pT2 = rot_ps(ph, 1, [128, 512], BF16, "pT2")
        pbig = rot_ps(ph, 4, [128, 512], F32, "pbig")
        rk = rot_sb(ph, 2, [128, 2, 64], F32, "rk")
        sm = rot_sb(ph, 2, [128, 512], BF16, "sm")
        tmpk = rot_sb(ph, 2, [128, 3, 64], F32, "tmpk")
        kst = rot_sb(ph, 2, [128, 2, 128], BF16, "kst")
        gst = rot_sb(ph, 3, [128, 512], BF16, "gst")
        qn_r = rot_sb(ph, 2, [128, 512], BF16, "qn")
        qa_r = rot_sb(ph, 3, [128, 512], BF16, "qa")
        qr_r = rot_sb(ph, 3, [64, 512], BF16, "qr")
        rq_r = rot_sb(ph, 2, [64, 2, 512], F32, "rq")
        t12 = rot_sb(ph, 2, [64, 2, 512], F32, "t12")

        groups = [("lat", g * 512, 512) for g in range(T // 512)] + [("ctx", 0, NCTX)]
        tg = 0
        ph.cur_key = -1.0
        for kind, t0, ntok in groups:
            s = 0 if kind == "lat" else 1
            (G1b, G1b_k), (SHb, SHb_k) = mods[s]
            nt = ntok // 128
            uT, uT_k = uTs.next()
            cqT, cqT_k = cqTs.next()
            gcol0 = t0 if kind == "lat" else T
            kcol0 = (NCTX + t0) if kind == "lat" else 0
            uT_toks = [uT_k + f"_{j}" for j in range(nt)]
            cq_toks = [cqT_k + f"_{j}" for j in range(nt)]
            for j in range(nt):
                r0 = t0 + j * 128
                src = (src_lat if kind == "lat" else src_ctx)[r0:r0 + 128, :]
                ph.cur_key = float(tg)
                self.u_tile(ph, R, src, G1b, SHb, (G1b_k, SHb_k), uT, uT_k, j, ident, ident_k, key2=tg + 1.5)
                ph.cur_key = tg + 1.5
                tg += 1
                pm, pm_k = psm.next()
                for c in range(8):
                    OP_mm(ph, pm[:, 0:448], uT[:, c, j * 128:(j + 1) * 128], w_in[:, c, 0:448], c == 0, c == 7,
                          [uT_toks[j], w_in_k], [pm_k])
                ss, ss_k = ss2_r.next()
                junk, junk_k = junk2_r.next()
                OP_act(ph, junk[:, 0:256], pm[:, 0:256], AF.Square, [pm_k], [junk_k, ss_k], scale=1.0 / 16.0,
                       accum=ss[:, 0:1])
                OP_act(ph, junk[:, 256:384], pm[:, 256:384], AF.Square, [pm_k], [junk_k, ss_k],
                       scale=1.0 / math.sqrt(128.0), accum=ss[:, 1:2])
                OP_rstd(ph, ss[:, 2:4], ss[:, 0:2], ss_k)
                smt, smt_k = sm.next()
                OP_stt(ph, "dve", smt[:, 0:256], pm[:, 0:256], ss[:, 2:3], gqa[:], ALU.mult, ALU.mult,
                       [pm_k, ss_k, gqa_k], [smt_k])
                OP_stt(ph, "dve", smt[:, 256:384], pm[:, 256:384], ss[:, 3:4], gkva[:], ALU.mult, ALU.mult,
                       [pm_k, ss_k, gkva_k], [smt_k])
                if kind == "lat":
                    rkt, rkt_k = rk.next()
                    ph.dma("sp", rkt[:], I["rope_k"][r0:r0 + 128], writes=[rkt_k], semkey=rkt_k)
                    tk, tk_k = tmpk.next()
                    OP_tt(ph, "dve", tk[:, 0, :], pm[:, 384:448], rkt[:, 0, :], ALU.mult, [pm_k, rkt_k], [tk_k])
                    pv = pm[:, 384:448].rearrange("p (b h d) -> p b h d", b=2, h=2)
                    sv = rkt[:, 1, :].rearrange("p (b h d) -> p b h d", b=2, h=2)
                    dv = tk[:, 1, :].rearrange("p (b h d) -> p b h d", b=2, h=2)
                    OP_tt(ph, "dve", dv[:, :, 0, :], pv[:, :, 1, :], sv[:, :, 0, :], ALU.mult, [pm_k, rkt_k], [tk_k])
                    OP_tt(ph, "dve", dv[:, :, 1, :], pv[:, :, 0, :], sv[:, :, 1, :], ALU.mult, [pm_k, rkt_k], [tk_k])
                    OP_tt(ph, "dve", smt[:, 384:448], tk[:, 0, :], tk[:, 1, :], ALU.add, [tk_k], [smt_k])
                else:
                    OP_cp(ph, "dve", smt[:, 384:448], pm[:, 384:448], [pm_k], [smt_k])
                ph.dma("pool", S["V"][:, (kcol0 + j * 128) // 128, :], smt[:, 256:384], reads=[smt_k],
                       semkey=smt_k + "v")
                p2, p2_k = pT2.next()
                OP_tr(ph, p2[:, 0:128], smt[:, 0:128], ident[:], [smt_k, ident_k], [p2_k])
                OP_tr(ph, p2[:, 128:256], smt[:, 128:256], ident[:], [smt_k, ident_k], [p2_k])
                OP_tr(ph, p2[:, 256:384], smt[:, 256:384], ident[:], [smt_k, ident_k], [p2_k])
                OP_tr(ph, p2[0:64, 384:512], smt[:, 384:448], ident[:], [smt_k, ident_k], [p2_k])
                OP_cp(ph, "act", cqT[:, :, j * 128:(j + 1) * 128], p2[:, 0:256].rearrange("p (c t) -> p c t", c=2),
                      [p2_k], [cq_toks[j]])
                ks, ks_k = kst.next()
                OP_cp(ph, "act", ks[:, 0, :], p2[:, 256:384], [p2_k], [ks_k])
                OP_cp(ph, "act", ks[0:64, 1, :], p2[0:64, 384:512], [p2_k], [ks_k])
                ph.dma("pool", S["KLT"][:, kcol0 + j * 128:kcol0 + (j + 1) * 128], ks[:, 0, :], reads=[ks_k],
                       semkey=ks_k + "o")
                ph.dma("pool", S["KRT"][:, kcol0 + j * 128:kcol0 + (j + 1) * 128], ks[0:64, 1, :], reads=[ks_k],
                       semkey=ks_k + "o")
            ph.cur_key = tg - 1 + 3.5
            for n in range(8):
                pb, pb_k = pbig.next()
                for c in range(8):
                    OP_mm(ph, pb[:, 0:ntok], w_in[:, c, 448 + n * 128:448 + (n + 1) * 128], uT[:, c, 0:ntok],
                          c == 0, c == 7, uT_toks + [w_in_k], [pb_k])
                g, g_k = gst.next()
                OP_act(ph, g[:, 0:ntok], pb[:, 0:ntok], AF.Silu, [pb_k], [g_k])
                ph.dma("pool", S["GATE"][n * 128:(n + 1) * 128, gcol0:gcol0 + ntok], g[:, 0:ntok], reads=[g_k],
                       semkey=g_k + "o")
            if kind == "ctx" and not ctx_q:
                continue
            if kind == "lat":
                rq, rq_k = rq_r.next()
                ph.dma("sp", rq[:, :, 0:ntok], I["rope_q"][:, :, t0:t0 + ntok].rearrange("a d t -> d a t"),
                       writes=[rq_k], semkey=rq_k)
            for h in range(8):
                pb, pb_k = pbig.next()
                for c in range(2):
                    OP_mm(ph, pb[:, 0:ntok], w_q[:, c, h * 192:h * 192 + 128], cqT[:, c, 0:ntok], c == 0, c == 1,
                          cq_toks + [w_q_k], [pb_k])
                qn, qn_k = qn_r.next()
                OP_cp(ph, "act", qn[:, 0:ntok], pb[:, 0:ntok], [pb_k], [qn_k])
                pb2, pb2_k = pbig.next()
                OP_mm(ph, pb2[:, 0:ntok], w_uk[:, h, :], qn[:, 0:ntok], True, True, [qn_k, w_uk_k], [pb2_k])
                qa, qa_k = qa_r.next()
                OP_cp(ph, "dve", qa[:, 0:ntok], pb2[:, 0:ntok], [pb2_k], [qa_k])
                ph.dma("pool", S["QA"][h, :, gcol0:gcol0 + ntok], qa[:, 0:ntok], reads=[qa_k], semkey=qa_k + "o")
                pr, pr_k = pbig.next()
                for c in range(2):
                    OP_mm(ph, pr[0:64, 0:ntok], w_q[:, c, h * 192 + 128:h * 192 + 192], cqT[:, c, 0:ntok], c == 0,
                          c == 1, cq_toks + [w_q_k], [pr_k])
                qr, qr_k = qr_r.next()
                if kind == "lat":
                    pss, pss_k = pbig.next()
                    for c in range(2):
                        OP_mm(ph, pss[0:64, 0:ntok], w_qs[:, c, h * 64:(h + 1) * 64], cqT[:, c, 0:ntok], c == 0,
                              c == 1, cq_toks + [w_qs_k], [pss_k])
                    tt, tt_k = t12.next()
                    OP_tt(ph, "dve", tt[:, 0, 0:ntok], pr[0:64, 0:ntok], rq[:, 0, 0:ntok], ALU.mult, [pr_k, rq_k],
                          [tt_k])
                    OP_tt(ph, "dve", tt[:, 1, 0:ntok], pss[0:64, 0:ntok], rq[:, 1, 0:ntok], ALU.mult,
                          [pss_k, rq_k], [tt_k])
                    OP_tt(ph, "pool", qr[:, 0:ntok], tt[:, 0, 0:ntok], tt[:, 1, 0:ntok], ALU.add, [tt_k], [qr_k])
                else:
                    OP_cp(ph, "dve", qr[:, 0:ntok], pr[0:64, 0:ntok], [pr_k], [qr_k])
                ph.dma("pool", S["QR"][h, :, gcol0:gcol0 + ntok], qr[:, 0:ntok], reads=[qr_k], semkey=qr_k + "o")
        ph.emit()

    def phase_mla_b(self, li, src_lat, src_ctx, ctx_q):
        nc, I, S = self.nc, self.I, self.S
        idx = li // 2
        ph = Phase(nc, f"b{li}")
        KLT, KLT_k = ph.sb([128, NK], BF16, "KLT")
        KRT, KRT_k = ph.sb([128, NK], BF16, "KRT")
        V, V_k = ph.sb([128, NKC, 128], BF16, "V")
        ph.op("pool", lambda e: e.memset(KRT[64:128, :], 0.0), [], [KRT_k])
        for q4 in range(4):
            c0, c1 = q4 * (NK // 4), (q4 + 1) * (NK // 4)
            ph.dma("sp", KLT[:, c0:c1], S["KLT"][:, c0:c1], writes=[KLT_k], semkey=KLT_k)
        ph.dma("sp", KRT[0:64, :], S["KRT"], writes=[KRT_k], semkey=KRT_k)
        for q4 in range(3):
            ph.dma("sp", V[:, q4 * 22:(q4 + 1) * 22, :], S["V"][:, q4 * 22:(q4 + 1) * 22, :], writes=[V_k], semkey=V_k)
        ones, ones_k = ph.sb([128, 128], BF16, "ones")
        ph.op("pool", lambda e: e.memset(ones[:], 1.0), [], [ones_k])
        stage = rot_sb(ph, 2, [128, 1024], F32, "wst")
        w_uv, w_uv_k = ph.sb([128, 8, 128], BF16, "w_uv")
        w_o, w_o_k = ph.sb([128, 8, D], BF16, "w_o")
        for h in range(8):
            self.load_w_bf16(ph, w_uv[:, h, :], w_uv_k, I["mla_w_kvup"][idx, :, h * 256 + 128:h * 256 + 256], stage)
            self.load_w_bf16(ph, w_o[:, h, :], w_o_k, I["mla_w_out"][idx, h * 128:(h + 1) * 128, :], stage)
        GT = {}
        for s in range(2):
            GT[s] = self.load_bcast(ph, S["modd"][li, s:s + 1, 2, :], D, "gtb")

        qa_r = rot_sb(ph, 2, [128, 512], BF16, "qa")
        qr_r = rot_sb(ph, 2, [128, 512], BF16, "qr")
        for qr_t, qr_tk in qr_r.items:
            ph.op("pool", (lambda t: (lambda e: e.memset(t[64:128, :], 0.0)))(qr_t), [], [qr_tk])
        gt_r = rot_sb(ph, 4, [128, 512], BF16, "gate")
        ps_s = rot_ps(ph, 3, [128, 1024], F32, "pss")
        ps_o = rot_ps(ph, 1, [128, 512], F32, "pso")
        ps_u = rot_ps(ph, 1, [128, 512], F32, "psu")
        ps_l = ps_u
        posb_r = rot_sb(ph, 2, [128, 512], F32, "posb")
        pt_r = rot_sb(ph, 6, [128, 1024], BF16, "pt")
        tmp_r = rot_sb(ph, 2, [128, 1024], BF16, "ptsum")
        accA_r = rot_sb(ph, 4, [128, 1024], F32, "accA")
        accB_r = rot_sb(ph, 1, [128, 8], F32, "accB")
        t2_r = rot_sb(ph, 2, [128, 512], BF16, "tot2")
        rs_r = rot_sb(ph, 2, [128, 512], F32, "rs")
        op_r = rot_sb(ph, 2, [128, 512], BF16, "op")
        go_r = rot_sb(ph, 2, [128, 8, 512], BF16, "go")
        h_r = rot_sb(ph, 2, [128, D], F32, "h")
        t_r = rot_sb(ph, 2, [128, D], F32, "t")
        LAG = 2
        pending = []

        def run_pending():
            while pending:
                f = pending.pop(0)
                if f is not None:
                    f()

        tiles = [("lat", g * 512, 512) for g in range(T // 512)]
        if ctx_q:
            tiles.append(("ctx", 0, NCTX))

        class HC:
            pass

        heads = []
        for kind, t0, ntok in tiles:
            kcs = list(range(NKC)) if kind == "lat" else [0, 1]
            tile_ctx = {}
            for h in range(8):
                hc = HC()
                hc.kind, hc.t0, hc.ntok, hc.h = kind, t0, ntok, h
                hc.s = 0 if kind == "lat" else 1
                hc.gcol0 = t0 if kind == "lat" else T
                hc.pairs = [(kcs[2 * j], kcs[2 * j + 1]) for j in range(len(kcs) // 2)]
                hc.npair = len(hc.pairs)
                hc.tile_ctx = tile_ctx
                heads.append(hc)
        jobs = [(hc, j) for hc in heads for j in range(hc.npair)]

        def head_begin(hc):
            if hc.npair < 12:
                run_pending()
            if hc.h == 0:
                hc.tile_ctx["go"] = go_r.next()
            hc.go, hc.go_k = hc.tile_ctx["go"]
            ntok = hc.ntok
            hc.qa, hc.qa_k = qa_r.next()
            hc.qr, hc.qr_k = qr_r.next()
            hc.gt, hc.gt_k = gt_r.next()
            ph.dma("sp", hc.qa[:, 0:ntok], S["QA"][hc.h, :, hc.gcol0:hc.gcol0 + ntok], writes=[hc.qa_k],
                   semkey=hc.qa_k)
            ph.dma("sp", hc.qr[0:64, 0:ntok], S["QR"][hc.h, :, hc.gcol0:hc.gcol0 + ntok], writes=[hc.qr_k],
                   semkey=hc.qr_k)
            ph.dma("sp", hc.gt[:, 0:ntok], S["GATE"][hc.h * 128:(hc.h + 1) * 128, hc.gcol0:hc.gcol0 + ntok],
                   writes=[hc.gt_k], semkey=hc.gt_k)
            hc.po, hc.po_k = ps_o.next()
            hc.accA, hc.accA_k = accA_r.next()
            hc.usedA = False
            hc.pts = []

        def emit_qk(hc, j):
            ntok = hc.ntok
            pS, pS_k = ps_s.next()
            for half, kc in enumerate(hc.pairs[j]):
                o_ap = pS[:, half * 512:half * 512 + ntok]
                OP_mm(ph, o_ap, KLT[:, kc * 128:(kc + 1) * 128], hc.qa[:, 0:ntok], True, False, [KLT_k, hc.qa_k],
                      [pS_k])
                OP_mm(ph, o_ap, KRT[:, kc * 128:(kc + 1) * 128], hc.qr[:, 0:ntok], False, True, [KRT_k, hc.qr_k],
                      [pS_k])
            pt, pt_k = pt_r.next()
            if ntok == 512:
                sv, pv = pS[:], pt[:]
            else:
                sv = pS[:].rearrange("p (a t) -> p a t", a=2)[:, :, 0:ntok]
                pv = pt[:].rearrange("p (a t) -> p a t", a=2)[:, :, 0:ntok]
            OP_act(ph, pv, sv, AF.Exp, [pS_k], [pt_k], scale=SCALE)
            hc.pts.append((pt, pt_k))

            def accum(src_ap, src_k):
                accA, accA_k = hc.accA, hc.accA_k
                av = accA[:] if ntok == 512 else accA[:].rearrange("p (a t) -> p a t", a=2)[:, :, 0:ntok]
                if not hc.usedA:
                    OP_cp(ph, "dve", av, src_ap, [src_k], [accA_k])
                else:
                    OP_tt(ph, "dve", av, av, src_ap, ALU.add, [src_k, accA_k], [accA_k])
                hc.usedA = True

            if ntok == 512 and j % 2 == 1:
                tmp, tmp_k = tmp_r.next()
                pprev, pprev_k = hc.pts[j - 1]
                OP_tt(ph, "dve", tmp[:], pprev[:], pt[:], ALU.add, [pprev_k, pt_k], [tmp_k])
                accum(tmp[:], tmp_k)
            elif j == hc.npair - 1:
                accum(pv, pt_k)

        def emit_pv(hc, jj):
            ntok = hc.ntok
            ptj, ptj_k = hc.pts[jj]
            for half, kc in enumerate(hc.pairs[jj]):
                OP_mm(ph, hc.po[:, 0:ntok], V[:, kc, :], ptj[:, half * 512:half * 512 + ntok],
                      jj == 0 and half == 0, jj == hc.npair - 1 and half == 1, [V_k, ptj_k], [hc.po_k])

        def ep_stages(hc):
            h, po, po_k, accA, accA_k = hc.h, hc.posb, hc.posb_k, hc.accA, hc.accA_k
            gt, gt_k, go, go_k, ntok = hc.gt, hc.gt_k, hc.go, hc.go_k, hc.ntok
            st = {}

            def s0():
                st["t2"] = t2_r.next()
                t2, t2_k = st["t2"]
                OP_tt(ph, "dve", t2[:, 0:ntok], accA[:, 0:ntok], accA[:, 512:512 + ntok], ALU.add, [accA_k], [t2_k])

            def s1():
                t2, t2_k = st["t2"]
                st["pl"] = ps_l.next()
                pl, pl_k = st["pl"]
                OP_mm(ph, pl[:, 0:ntok], ones[:], t2[:, 0:ntok], True, True, [ones_k, t2_k], [pl_k])

            def s2():
                pl, pl_k = st["pl"]
                st["rs"] = rs_r.next()
                rs, rs_k = st["rs"]
                OP_act(ph, rs[:, 0:ntok], pl[:, 0:ntok], AF.Ln, [pl_k], [rs_k])
                OP_act(ph, rs[:, 0:ntok], rs[:, 0:ntok], AF.Exp, [rs_k], [rs_k], scale=-1.0)

            def s2b():
                rs, rs_k = st["rs"]
                st["opn"] = op_r.next()
                opn, opn_k = st["opn"]
                OP_tt(ph, "dve", opn[:, 0:ntok], po[:, 0:ntok], rs[:, 0:ntok], ALU.mult, [po_k, rs_k], [opn_k])

            def s3():
                opn, opn_k = st["opn"]
                st["pu"] = ps_u.next()
                pu, pu_k = st["pu"]
                OP_mm(ph, pu[:, 0:ntok], w_uv[:, h, :], opn[:, 0:ntok], True, True, [w_uv_k, opn_k], [pu_k])

            def s4():
                pu, pu_k = st["pu"]
                OP_tt(ph, "dve", go[:, h, 0:ntok], pu[:, 0:ntok], gt[:, 0:ntok], ALU.mult, [pu_k, gt_k],
                      [go_k + f"_{h}"])

            return [s0, None, s1, None, s2, None, None, s2b, None, None, s3, None, None, s4, None]

        def tile_out_stage(hc, ts):
            kind, t0, go, go_k, s = hc.kind, hc.t0, hc.go, hc.go_k, hc.s
            go_toks = [go_k + f"_{h}" for h in range(8)]

            def f():
                (GTb, GTb_k) = GT[s]
                r0 = t0 + ts * 128
                ht, ht_k = h_r.next()
                src = (src_lat if kind == "lat" else src_ctx)[r0:r0 + 128, :]
                dst = (S["h"] if kind == "lat" else S["hc"])[r0:r0 + 128, :]
                ph.dma("sp", ht[:], src, writes=[ht_k], semkey=ht_k)
                tt, tt_k = t_r.next()
                for half in range(2):
                    py, py_k = ps_u.next()
                    for h in range(8):
                        OP_mm(ph, py[:], go[:, h, ts * 128:(ts + 1) * 128], w_o[:, h, half * 512:(half + 1) * 512],
                              h == 0, h == 7, go_toks + [w_o_k], [py_k])
                    OP_tt(ph, "dve", tt[:, half * 512:(half + 1) * 512], py[:], GTb[:, half * 512:(half + 1) * 512],
                          ALU.mult, [py_k, GTb_k], [tt_k])
                OP_tt(ph, "pool", tt[:], tt[:], ht[:], ALU.add, [tt_k, ht_k], [tt_k])
                ph.dma("pool", dst, tt[:], reads=[tt_k], semkey=tt_k + "o")
            return f

        for g in range(len(jobs) + LAG):
            if g < len(jobs):
                hc, j = jobs[g]
                if j == 0:
                    head_begin(hc)
                if hc.npair >= 12 and j >= 2 and pending:
                    f = pending.pop(0)
                    if f is not None:
                        f()
                emit_qk(hc, j)
            if g >= LAG:
                hc2, j2 = jobs[g - LAG]
                emit_pv(hc2, j2)
                if j2 == hc2.npair - 1:
                    hc2.posb, hc2.posb_k = posb_r.next()
                    OP_cp(ph, "dve", hc2.posb[:, 0:hc2.ntok], hc2.po[:, 0:hc2.ntok], [hc2.po_k], [hc2.posb_k])
                    pending.extend(ep_stages(hc2))
                    if hc2.h == 7:
                        for ts in range(hc2.ntok // 128):
                            pending.append(None)
                            pending.append(tile_out_stage(hc2, ts))
                    if hc2.npair < 12:
                        run_pending()
        run_pending()
        ph.emit()

    def phase_f1(self, li, do_ctx):
        nc, I, S = self.nc, self.I, self.S
        idx = li // 2
        ph = Phase(nc, f"f{li}")
        ident, ident_k = ph.sb([128, 128], BF16, "ident")
        ph.dma("sp", ident[:], I["ident"], writes=[ident_k], semkey=ident_k)
        mods = {}
        for s in range(2):
            mods[s] = (self.load_bcast(ph, S["modd"][li, s:s + 1, 0, :], D, "g1b"),
                       self.load_bcast(ph, S["modd"][li, s:s + 1, 1, :], D, "shb"))
        stage = rot_sb(ph, 2, [128, 1024], F32, "wst")
        self.f_stage = stage
        w_in, w_in_k = ph.sb([128, 8, 2048], BF16, "w_in")
        for c in range(8):
            for hf in range(2):
                self.load_w_bf16(ph, w_in[:, c, hf * 1024:(hf + 1) * 1024], w_in_k,
                                 I["fno_w_in"][idx, c * 128:(c + 1) * 128, hf * 1024:(hf + 1) * 1024], stage)
        t1m, t1m_k = ph.sb([128, 2, 128], BF16, "t1m")
        ph.dma("sp", t1m[:], I["dft_t1"], writes=[t1m_k], semkey=t1m_k)
        tw, tw_k = ph.sb([128, 2, 64], F32, "tw")
        ph.dma("sp", tw[:], I["dft_tw"], writes=[tw_k], semkey=tw_k)
        R = {"x": rot_sb(ph, 3, [128, D], F32, "x"), "junk": rot_sb(ph, 1, [128, D], BF16, "junk"),
             "ss": rot_sb(ph, 4, [128, 8], F32, "ss"), "un": rot_sb(ph, 3, [128, D], F32, "un"),
             "ub": rot_sb(ph, 3, [128, D], BF16, "ub"), "pT": rot_ps(ph, 1, [128, D], BF16, "pT")}
        uTs = rot_sb(ph, 2, [128, 8, 512], BF16, "uT")
        pz = rot_ps(ph, 1, [128, 1024], F32, "pz")
        pa = rot_ps(ph, 1, [128, 2, 1024], F32, "pa")
        pg = rot_ps(ph, 1, [128, 512], F32, "pg")
        zt_r = rot_sb(ph, 2, [128, D], BF16, "zt")
        tm_r = rot_sb(ph, 1, [128, 2, D], F32, "tm")
        b_r = rot_sb(ph, 2, [128, 2, D], BF16, "bt")
        gst = rot_sb(ph, 3, [128, 512], BF16, "gst")
        hv = S["h"].rearrange("(t1 t2) d -> t2 t1 d", t2=64)
        if do_ctx:
            zc, zc_k = ph.sb([128, 2, D], BF16, "zc")
            gc, gc_k = ph.sb([128, 8, NCTX], BF16, "gc")
        groups = [("lat", g * 4, 4) for g in range(16)] + ([("ctx", 0, 2)] if do_ctx else [])
        tg = 0
        ph.cur_key = -1.0
        for kind, j0, nt in groups:
            s = 0 if kind == "lat" else 1
            (G1b, G1b_k), (SHb, SHb_k) = mods[s]
            uT, uT_k = uTs.next()
            ntok = nt * 128
            uT_toks = [uT_k + f"_{j}" for j in range(nt)]
            for j in range(nt):
                if kind == "lat":
                    t2 = j0 + j
                    src = hv[t2]
                else:
                    src = S["hc"][j * 128:(j + 1) * 128, :]
                ph.cur_key = float(tg)
                self.u_tile(ph, R, src, G1b, SHb, (G1b_k, SHb_k), uT, uT_k, j, ident, ident_k, key2=tg + 1.5)
                ph.cur_key = tg + 1.5
                tg += 1
                z, z_k = pz.next()
                for half in range(2):
                    for c in range(8):
                        OP_mm(ph, z[:, half * 512:(half + 1) * 512], uT[:, c, j * 128:(j + 1) * 128],
                              w_in[:, c, half * 512:(half + 1) * 512], c == 0, c == 7, [uT_toks[j], w_in_k], [z_k])
                if kind == "ctx":
                    OP_cp(ph, "dve", zc[:, j, :], z[:], [z_k], [zc_k + f"_{j}"])
                    continue
                zt, zt_k = zt_r.next()
                OP_cp(ph, "dve", zt[:], z[:], [z_k], [zt_k])
                a, a_k = pa.next()
                for comp in range(2):
                    for half in range(2):
                        OP_mm(ph, a[:, comp, half * 512:(half + 1) * 512], t1m[:, comp, :],
                              zt[:, half * 512:(half + 1) * 512], True, True, [t1m_k, zt_k], [a_k])
                tm, tm_k = tm_r.next()
                OP_act(ph, tm[:, 0, :], a[:, 1, :], AF.Copy, [a_k, tw_k], [tm_k], scale=tw[:, 1, t2:t2 + 1])
                OP_act(ph, tm[:, 1, :], a[:, 1, :], AF.Copy, [a_k, tw_k], [tm_k], scale=tw[:, 0, t2:t2 + 1])
                bt, bt_k = b_r.next()
                OP_stt(ph, "dve", bt[:, 0, :], a[:, 0, :], tw[:, 0, t2:t2 + 1], tm[:, 0, :], ALU.mult, ALU.subtract,
                       [a_k, tw_k, tm_k], [bt_k])
                OP_stt(ph, "dve", bt[:, 1, :], a[:, 0, :], tw[:, 1, t2:t2 + 1], tm[:, 1, :], ALU.mult, ALU.add,
                       [a_k, tw_k, tm_k], [bt_k])
                for comp in range(2):
                    ph.dma("pool", S["Bd"][comp, t2], bt[:, comp, :], reads=[bt_k], semkey=bt_k + "o")
            ph.cur_key = tg - 1 + 3.5
            for n in range(8):
                g_ps, g_ps_k = pg.next()
                for c in range(8):
                    OP_mm(ph, g_ps[:, 0:ntok], w_in[:, c, D + n * 128:D + (n + 1) * 128], uT[:, c, 0:ntok], c == 0,
                          c == 7, uT_toks + [w_in_k], [g_ps_k])
                if kind == "lat":
                    g, g_k = gst.next()
                    OP_act(ph, g[:, 0:ntok], g_ps[:, 0:ntok], AF.Silu, [g_ps_k], [g_k])
                    ph.dma("pool", S["GATE"][n * 128:(n + 1) * 128, j0 * 128:j0 * 128 + ntok], g[:, 0:ntok],
                           reads=[g_k], semkey=g_k + "o")
                else:
                    OP_act(ph, gc[:, n, :], g_ps[:, 0:ntok], AF.Silu, [g_ps_k], [gc_k + f"_{n}"])
        ph.cur_key = 1e6
        if do_ctx:
            self.f_ctx_tail(ph, li, zc, [zc_k + "_0", zc_k + "_1"], gc, [gc_k + f"_{n}" for n in range(8)], pz, pa, pg)
        ph.emit()

    def f_ctx_tail(self, ph, li, zc, zc_toks, gc, gc_toks, pz, pa, pg):
        nc, I, S = self.nc, self.I, self.S
        idx = li // 2
        dc, dc_k = ph.sb([128, 2, 512], BF16, "dctx")
        ph.dma("sp", dc[:], I["dft_ctx"], writes=[dc_k], semkey=dc_k)
        c1, c1_k = ph.sb([128, 2, 128], BF16, "c1")
        ph.dma("sp", c1[:], I["dft_c1"], writes=[c1_k], semkey=c1_k)
        stage = self.f_stage
        w_o, w_o_k = ph.sb([128, 8, D], BF16, "w_o")
        for c in range(8):
            self.load_w_bf16(ph, w_o[:, c, :], w_o_k, I["fno_w_out"][idx, c * 128:(c + 1) * 128, :], stage)
        GTb, GTb_k = self.load_bcast(ph, S["modd"][li, 1:2, 2, :], D, "gtc")
        xt, xt_k = ph.sb([128, 8, 512], BF16, "xT")
        gy, gy_k = ph.sb([128, 8, NCTX], BF16, "gy")
        for g in range(8):
            p, p_k = pg.next()
            for tc in range(2):
                OP_mm(ph, p[:], zc[:, tc, g * 128:(g + 1) * 128], dc[:, tc, :], tc == 0, tc == 1,
                      zc_toks + [dc_k], [p_k])
            OP_cp(ph, "dve", xt[:, g, :], p[:], [p_k], [xt_k + f"_{g}"])
            p2, p2_k = pg.next()
            OP_mm(ph, p2[:, 0:NCTX], c1[:, 0, :], xt[:, g, 0:NCTX], True, False, [c1_k, xt_k + f"_{g}"], [p2_k])
            OP_mm(ph, p2[:, 0:NCTX], c1[:, 1, :], xt[:, g, NCTX:2 * NCTX], False, True, [c1_k, xt_k + f"_{g}"], [p2_k])
            OP_tt(ph, "dve", gy[:, g, :], p2[:, 0:NCTX], gc[:, g, :], ALU.mult, [p2_k, gc_toks[g]], [gy_k + f"_{g}"])
        gy_toks = [gy_k + f"_{g}" for g in range(8)]
        h_r = rot_sb(ph, 2, [128, D], F32, "hcx")
        t_r = rot_sb(ph, 2, [128, D], F32, "tcx")
        for ts in range(2):
            ht, ht_k = h_r.next()
            ph.dma("sp", ht[:], S["hc"][ts * 128:(ts + 1) * 128, :], writes=[ht_k], semkey=ht_k)
            tt, tt_k = t_r.next()
            for half in range(2):
                py, py_k = pg.next()
                for g in range(8):
                    OP_mm(ph, py[:], gy[:, g, ts * 128:(ts + 1) * 128], w_o[:, g, half * 512:(half + 1) * 512],
                          g == 0, g == 7, gy_toks + [w_o_k], [py_k])
                OP_tt(ph, "dve", tt[:, half * 512:(half + 1) * 512], py[:], GTb[:, half * 512:(half + 1) * 512],
                      ALU.mult, [py_k, GTb_k], [tt_k])
            OP_tt(ph, "pool", tt[:], tt[:], ht[:], ALU.add, [tt_k, ht_k], [tt_k])
            ph.dma("pool", S["hc"][ts * 128:(ts + 1) * 128, :], tt[:], reads=[tt_k], semkey=tt_k + "o")

    def phase_f2(self, li, final):
        nc, I, S = self.nc, self.I, self.S
        idx = li // 2
        ph = Phase(nc, f"g{li}")
        r2, r2_k = ph.sb([128, 128], BF16, "r2")
        ph.dma("sp", r2[:], I["dft_r2"], writes=[r2_k], semkey=r2_k)
        c1, c1_k = ph.sb([128, 2, 128], BF16, "c1")
        ph.dma("sp", c1[:], I["dft_c1"], writes=[c1_k], semkey=c1_k)
        stage = rot_sb(ph, 2, [128, 1024], F32, "wst")
        w_o, w_o_k = ph.sb([128, 8, D], BF16, "w_o")
        for c in range(8):
            self.load_w_bf16(ph, w_o[:, c, :], w_o_k, I["fno_w_out"][idx, c * 128:(c + 1) * 128, :], stage)
        GTb, GTb_k = self.load_bcast(ph, S["modd"][li, 0:1, 2, :], D, "gtb")
        if final:
            FGb, FGb_k = self.load_bcast(ph, I["final_g"], D, "fg")
        KB = 8
        b_r = rot_sb(ph, 2, [128, KB, D], BF16, "bblk")
        g_r = rot_sb(ph, 2, [128, 8, KB * 128], BF16, "gblk")
        px = rot_ps(ph, 2, [128, 512], F32, "px")
        pyy = rot_ps(ph, 2, [128, 512], F32, "pyy")
        po = rot_ps(ph, 2, [128, 512], F32, "po")
        xt_r = rot_sb(ph, 2, [128, 8, KB * 128], BF16, "xT")
        gy_r = rot_sb(ph, 2, [128, 8, KB * 64], BF16, "gy")
        h_r = rot_sb(ph, 2, [128, D], F32, "h")
        t_r = rot_sb(ph, 2, [128, D], F32, "t")
        ss_r = rot_sb(ph, 4, [128, 8], F32, "ss")
        junk, junk_k = ph.sb([128, D], BF16, "junk")
        Bv = S["Bd"].rearrange("c t k d -> (c t) k d")
        hv = S["h"].rearrange("(k2 k1) d -> k1 k2 d", k1=128)
        ov = self.out.rearrange("(k2 k1) d -> k1 k2 d", k1=128)
        ph.cur_key = -1.0
        for blk in range(128 // KB):
            k10 = blk * KB
            ph.cur_key = float(blk)
            bb, bb_k = b_r.next()
            for q in range(2):
                ph.dma("sp", bb[:, q * 4:(q + 1) * 4, :], Bv[:, k10 + q * 4:k10 + (q + 1) * 4, :], writes=[bb_k],
                       semkey=bb_k)
            gb, gb_k = g_r.next()
            t20 = k10 % 64
            hi = k10 // 64
            for cc in range(8):
                ph.dma("sp", gb[:, cc, :], S["GATE"][cc * 128:(cc + 1) * 128, t20 * 128:(t20 + KB) * 128],
                       writes=[gb_k], semkey=gb_k)
            xt, xt_k = xt_r.next()
            gy, gy_k = gy_r.next()
            for cc in range(8):
                for q in range(KB // 4):
                    p, p_k = px.next()
                    for kk in range(4):
                        k1 = q * 4 + kk
                        OP_mm(ph, p[:, kk * 128:(kk + 1) * 128], bb[:, k1, cc * 128:(cc + 1) * 128], r2[:], True, True,
                              [bb_k, r2_k], [p_k])
                    xw = xt[:, cc, :].rearrange("p (c k j) -> p k c j", c=2, k=KB)[:, q * 4:(q + 1) * 4, :, :]
                    OP_cp(ph, "act", xw, p[:].rearrange("p (k c j) -> p k c j", k=4, c=2), [p_k], [xt_k + f"_{cc}"])
                ph.cur_key = blk + 1.5
                p2, p2_k = pyy.next()
                p2v = p2[:].rearrange("p (k j) -> p k j", k=KB)
                OP_mm(ph, p2[:], c1[:, 0, :], xt[:, cc, 0:KB * 64], True, False, [c1_k, xt_k + f"_{cc}"], [p2_k])
                OP_mm(ph, p2[:], c1[:, 1, :], xt[:, cc, KB * 64:KB * 128], False, True, [c1_k, xt_k + f"_{cc}"], [p2_k])
                gv = gb[:, cc, :].rearrange("p (t a two) -> p t a two", t=KB, two=2)[:, :, :, hi]
                OP_tt(ph, "dve", gy[:, cc, :].rearrange("p (k j) -> p k j", k=KB), p2v, gv, ALU.mult,
                      [p2_k, gb_k], [gy_k + f"_{cc}"])
                ph.cur_key = float(blk)
            gy_toks = [gy_k + f"_{cc}" for cc in range(8)]
            ph.cur_key = blk + 2.5
            for ts in range(KB // 2):
                ht, ht_k = h_r.next()
                k1a = k10 + 2 * ts
                ph.dma("sp", ht[0:64, :], hv[k1a], writes=[ht_k], semkey=ht_k)
                ph.dma("sp", ht[64:128, :], hv[k1a + 1], writes=[ht_k], semkey=ht_k)
                tt, tt_k = t_r.next()
                for half in range(2):
                    py, py_k = po.next()
                    for cc in range(8):
                        OP_mm(ph, py[:], gy[:, cc, ts * 128:(ts + 1) * 128], w_o[:, cc, half * 512:(half + 1) * 512],
                              cc == 0, cc == 7, gy_toks + [w_o_k], [py_k])
                    OP_tt(ph, "dve", tt[:, half * 512:(half + 1) * 512], py[:], GTb[:, half * 512:(half + 1) * 512],
                          ALU.mult, [py_k, GTb_k], [tt_k])
                OP_tt(ph, "pool", tt[:], tt[:], ht[:], ALU.add, [tt_k, ht_k], [tt_k])
                if final:
                    ss, ss_k = ss_r.next()
                    OP_act(ph, junk[:], tt[:], AF.Square, [tt_k], [junk_k, ss_k], scale=1.0 / 32.0, accum=ss[:, 0:1])
                    OP_rstd(ph, ss[:, 1:2], ss[:, 0:1], ss_k)
                    OP_stt(ph, "dve", tt[:], tt[:], ss[:, 1:2], FGb[:], ALU.mult, ALU.mult, [tt_k, ss_k, FGb_k], [tt_k])
                    dsts = (ov[k1a], ov[k1a + 1])
                else:
                    dsts = (hv[k1a], hv[k1a + 1])
                ph.dma("pool", dsts[0], tt[0:64, :], reads=[tt_k], semkey=tt_k + "o")
                ph.dma("pool", dsts[1], tt[64:128, :], reads=[tt_k], semkey=tt_k + "o")
        ph.emit()

    def build(self):
        S, I = self.S, self.I
        steps = [
            lambda: self.phase_mod(),
            lambda: self.phase_mla_a(0, I["x"], I["ctx"], True),
            lambda: self.phase_mla_b(0, I["x"], I["ctx"], True),
            lambda: self.phase_f1(1, True),
            lambda: self.phase_f2(1, False),
            lambda: self.phase_mla_a(2, S["h"], S["hc"], False),
            lambda: self.phase_mla_b(2, S["h"], S["hc"], False),
            lambda: self.phase_f1(3, False),
            lambda: self.phase_f2(3, True),
        ]
        if self.inject:
            self.phase_inject()
        if self.steps is not None:
            for i in self.steps:
                steps[i]()
            return self.nc
        n = len(steps) if self.upto is None else self.upto
        for i, st in enumerate(steps[:n]):
            with self.nc.named_scope(f"step{i}"):
                st()
        return self.nc


def _tables():
    bf = ml_dtypes.bfloat16
    row = np.repeat(np.arange(128), 64).astype(np.float32)
    col = np.tile(np.arange(64), 128).astype(np.float32)
    inv = (1.0 / (10000.0 ** (np.arange(0, 32, 2, dtype=np.float32) / 32.0))).astype(np.float32)
    ar = row[:, None] * inv[None, :]
    ac = col[:, None] * inv[None, :]
    cos_full = np.concatenate([np.cos(ar), np.cos(ar), np.cos(ac), np.cos(ac)], -1).astype(np.float32)
    sin_sgn = np.concatenate([-np.sin(ar), np.sin(ar), -np.sin(ac), np.sin(ac)], -1).astype(np.float32)
    rope_k = np.ascontiguousarray(np.stack([cos_full, sin_sgn], 1))
    rope_q = np.ascontiguousarray(np.stack([cos_full.T, sin_sgn.T], 0))
    ident = np.eye(128, dtype=np.float32).astype(bf)
    n128 = np.arange(128, dtype=np.float64)
    a1 = 2 * np.pi * np.outer(n128, n128) / 128.0
    sc = 1.0 / math.sqrt(128.0)
    dft_t1 = np.stack([np.cos(a1) * sc, np.sin(a1) * sc], 1).astype(np.float32).astype(bf)
    atw = 2 * np.pi * np.outer(n128, np.arange(64)) / 8192.0
    dft_tw = np.ascontiguousarray(np.stack([np.cos(atw), np.sin(atw)], 1).astype(np.float32))
    n64 = np.arange(64, dtype=np.float64)
    a2 = 2 * np.pi * np.outer(n64, n64) / 64.0
    C2, S2 = np.cos(a2) / 8.0, np.sin(a2) / 8.0
    dft_r2 = np.block([[C2, S2], [-S2, C2]]).astype(np.float32).astype(bf)
    dft_c1 = np.stack([np.cos(a1) * sc, -np.sin(a1) * sc], 1).astype(np.float32).astype(bf)
    n256 = np.arange(256, dtype=np.float64)
    a3 = 2 * np.pi * np.outer(n256, n256) / 256.0
    CS = np.concatenate([np.cos(a3), np.sin(a3)], 1) / 16.0
    dft_ctx = np.ascontiguousarray(CS.reshape(2, 128, 512).transpose(1, 0, 2)).astype(np.float32).astype(bf)
    return dict(rope_k=rope_k, rope_q=rope_q, ident=ident, dft_t1=dft_t1, dft_tw=dft_tw, dft_r2=dft_r2,
                dft_c1=dft_c1, dft_ctx=dft_ctx)


CORE_BATCH = {0: 0, 1: 1, 4: 2, 5: 3}


def make_in_maps(inputs, cores):
    f = lambda a: np.ascontiguousarray(np.asarray(a, dtype=np.float32))
    tabs = _tables()
    w_qup = f(inputs["mla_w_qup"])
    perm = np.array([(d + 16) if (d % 32) < 16 else (d - 16) for d in range(64)])
    w_qup_sw = np.stack([np.concatenate([w_qup[i][:, h * 192 + 128 + perm] for h in range(8)], 1) for i in range(2)])
    w_kvup = f(inputs["mla_w_kvup"])
    w_ukT = np.stack([np.stack([w_kvup[i][:, h * 256:h * 256 + 128].T for h in range(8)]) for i in range(2)])
    shared = dict(
        norm_g=f(inputs["norm_g"]), w_ada=f(inputs["w_ada"]), b_ada=f(inputs["b_ada"]),
        mla_w_in=f(inputs["mla_w_in"]), mla_g_qa=f(inputs["mla_g_qa"]), mla_w_qup=w_qup,
        mla_w_qup_sw=np.ascontiguousarray(w_qup_sw), mla_g_kva=f(inputs["mla_g_kva"]), mla_w_kvup=w_kvup,
        mla_w_ukT=np.ascontiguousarray(w_ukT), mla_w_out=f(inputs["mla_w_out"]), fno_w_in=f(inputs["fno_w_in"]),
        fno_w_out=f(inputs["fno_w_out"]), final_g=f(inputs["final_g"]).reshape(1, D), **tabs)
    x, c, ctx, c_ctx = f(inputs["x"]), f(inputs["c"]), f(inputs["ctx"]), f(inputs["c_ctx"])
    maps = []
    zero = None
    for core in cores:
        b = CORE_BATCH.get(core)
        if b is None:
            if zero is None:
                zero = {k: np.zeros_like(v) for k, v in maps[0].items()}
            maps.append(zero)
            continue
        cv = np.stack([c[b], c_ctx], 0)
        cvecT = np.ascontiguousarray(cv.reshape(2, 8, 128).transpose(2, 1, 0))
        m = dict(shared)
        m.update(x=x[b], ctx=ctx[b], cvecT=cvecT)
        maps.append(m)
    return maps


def kernel(**inputs):
    nc = Builder().build()
    cores = list(range(8))
    in_maps = make_in_maps(inputs, cores)
    res = run_bass_kernel_spmd(nc, in_maps, core_ids=cores)
    inv = {b: c for c, b in CORE_BATCH.items()}
    out = np.stack([np.asarray(res.results[inv[b]]["out"]) for b in range(4)], 0)
    return out.astype(np.float32)
```

```python
import contextlib
import math
import numpy as np
import ml_dtypes
import concourse.bass as bass
import concourse.mybir as mybir
from concourse.bass_utils import run_bass_kernel_spmd

F32 = mybir.dt.float32
BF16 = mybir.dt.bfloat16
AF = mybir.ActivationFunctionType
ALU = mybir.AluOpType

T = 8192
D = 1024
NCTX = 256
NK = T + NCTX
NKC = NK // 128
EPS = 1e-6
SCALE = 1.0 / math.sqrt(192.0)


class Op:
    __slots__ = ("eng", "fn", "deps", "dma", "semkey", "signal", "sem", "val", "key", "idx", "impl")

    def __init__(self, eng, fn, dma, semkey):
        self.eng = eng
        self.fn = fn
        self.deps = []
        self.dma = dma
        self.semkey = semkey
        self.signal = False
        self.sem = None
        self.val = 0
        self.key = 0.0
        self.idx = 0
        self.impl = []


class Phase:
    ENGS = ("sp", "pe", "act", "dve", "pool")

    def __init__(self, nc, name):
        self.nc = nc
        self.name = name
        self.ops = []
        self.last_writer = {}
        self.readers = {}
        self.stack = contextlib.ExitStack()
        self.n_alloc = 0
        self.cur_key = None

    def sb(self, shape, dtype, name="t"):
        self.n_alloc += 1
        t = self.stack.enter_context(self.nc.sbuf_tensor(f"{self.name}_{name}{self.n_alloc}", list(shape), dtype))
        return t, f"{name}{self.n_alloc}"

    def ps(self, shape, dtype, name="p"):
        self.n_alloc += 1
        t = self.stack.enter_context(self.nc.psum_tensor(f"{self.name}_{name}{self.n_alloc}", list(shape), dtype))
        return t, f"{name}{self.n_alloc}"

    def op(self, eng, fn, reads=(), writes=(), dma=False, semkey=None):
        o = Op(eng, fn, dma, semkey)
        o.idx = len(self.ops)
        o.key = self.cur_key if self.cur_key is not None else 0.0
        deps = []
        for t in list(reads) + list(writes):
            w = self.last_writer.get(t)
            if w is not None:
                deps.append(w)
        for t in writes:
            deps.extend(self.readers.get(t, ()))
        seen = set()
        for d in deps:
            if id(d) in seen or d is o:
                continue
            seen.add(id(d))
            if d.eng == "pe" and eng == "pe" and not d.dma and not dma:
                o.impl.append(d)
                continue
            o.deps.append(d)
            d.signal = True
        for t in writes:
            self.last_writer[t] = o
            self.readers[t] = []
        for t in reads:
            self.readers.setdefault(t, []).append(o)
        self.ops.append(o)
        return o

    def dma(self, queue, out, in_, reads=(), writes=(), semkey=None, slow=False):
        assert semkey is not None
        if slow:
            fn = lambda e: e.dma_start(out=out, in_=in_, allow_slow_non_contiguous=True)
        else:
            fn = lambda e: e.dma_start(out=out, in_=in_)
        return self.op(queue, fn, reads, writes, dma=True, semkey=semkey)

    def emit(self):
        nc = self.nc
        self.ops.sort(key=lambda o: (o.key, o.idx))
        pos = {id(o): i for i, o in enumerate(self.ops)}
        for o in self.ops:
            for d in o.deps + o.impl:
                assert pos[id(d)] < pos[id(o)], f"{self.name}: pipelining key inverts a dependency ({d.eng}->{o.eng})"
        keys = []
        for o in self.ops:
            if o.dma:
                o.signal = True
                if o.semkey not in keys:
                    keys.append(o.semkey)
        sems = {}
        for e in self.ENGS:
            sems[("eng", e)] = nc.alloc_semaphore(name=f"{self.name}_s_{e}")
        for i, k in enumerate(keys):
            sems[("dma", k)] = nc.alloc_semaphore(name=f"{self.name}_d{i}")
        counts = {k: 0 for k in sems}
        for o in self.ops:
            if not o.signal:
                continue
            k = ("dma", o.semkey) if o.dma else ("eng", o.eng)
            counts[k] += 16 if o.dma else 1
            o.sem = k
            o.val = counts[k]
        per_eng = {e: [] for e in self.ENGS}
        for o in self.ops:
            per_eng[o.eng].append(o)
        final_dma = {k: v for k, v in counts.items() if k[0] == "dma" and v > 0}

        def body(ename):
            def f(eng):
                waited = {}
                for o in per_eng[ename]:
                    need = {}
                    for d in o.deps:
                        if need.get(d.sem, 0) < d.val:
                            need[d.sem] = d.val
                    for k, v in need.items():
                        if waited.get(k, 0) >= v:
                            continue
                        eng.wait_ge(sems[k], v)
                        waited[k] = v
                    ins = o.fn(eng)
                    if o.signal:
                        ins.then_inc(sems[o.sem], 16 if o.dma else 1)
                if ename == "sp":
                    for k, v in final_dma.items():
                        eng.wait_ge(sems[k], v)
                    for e2 in ("pe", "act", "dve", "pool"):
                        v = counts[("eng", e2)]
                        if v > 0:
                            eng.wait_ge(sems[("eng", e2)], v)
            return f

        with nc.Block() as block:
            block.sync(body("sp"))
            block.tensor(body("pe"))
            block.scalar(body("act"))
            block.vector(body("dve"))
            block.gpsimd(body("pool"))
        nc.all_engine_barrier()
        nc.clear_and_free_semaphores(list(sems.values()))
        nc.all_engine_barrier()
        self.stack.close()
        return len(self.ops)


class Rot:
    def __init__(self, items):
        self.items = items
        self.i = 0

    def next(self):
        it = self.items[self.i % len(self.items)]
        self.i += 1
        return it


def rot_sb(ph, n, shape, dtype, name):
    return Rot([ph.sb(shape, dtype, name) for _ in range(n)])


def rot_ps(ph, n, shape, dtype, name):
    return Rot([ph.ps(shape, dtype, name) for _ in range(n)])


def OP_mm(ph, out, lhsT, rhs, start, stop, r, w):
    return ph.op("pe", lambda e: e.matmul(out, lhsT, rhs, start=start, stop=stop), r, w)


def OP_tr(ph, out, in_, ident, r, w):
    return ph.op("pe", lambda e: e.transpose(out, in_, ident), r, w)


def OP_act(ph, out, in_, func, r, w, bias=None, scale=None, accum=None):
    kw = {}
    if bias is not None:
        kw["bias"] = bias
    if scale is not None:
        kw["scale"] = scale
    if accum is not None:
        kw["accum_out"] = accum
    return ph.op("act", lambda e: e.activation(out, in_, func, **kw), r, w)


def OP_ts(ph, eng, out, in0, s1, s2, op0, op1, r, w):
    if op1 is None:
        return ph.op(eng, lambda e: e.tensor_scalar(out, in0, s1, None, op0), r, w)
    return ph.op(eng, lambda e: e.tensor_scalar(out, in0, s1, s2, op0, op1), r, w)


def OP_stt(ph, eng, out, in0, scalar, in1, op0, op1, r, w):
    return ph.op(eng, lambda e: e.scalar_tensor_tensor(out, in0, scalar, in1, op0, op1), r, w)


def OP_tt(ph, eng, out, in0, in1, op, r, w):
    return ph.op(eng, lambda e: e.tensor_tensor(out, in0, in1, op), r, w)


def OP_recip(ph, out, in_, r, w):
    return ph.op("dve", lambda e: e.reciprocal(out, in_), r, w)


def OP_rstd(ph, out, in_, k):
    OP_act(ph, out, in_, AF.Sqrt, [k], [k], bias=EPS)
    return OP_recip(ph, out, out, [k], [k])


def OP_cp(ph, eng, out, in_, r, w):
    if eng == "act":
        return ph.op(eng, lambda e: e.activation(out, in_, AF.Copy), r, w)
    return ph.op(eng, lambda e: e.tensor_copy(out, in_), r, w)


class Builder:
    def __init__(self, debug=False, upto=None, steps=None, inject=()):
        self.debug = debug
        self.upto = upto
        self.steps = steps
        self.inject = inject
        nc = bass.Bass("TRN2", target_bir_lowering=False)
        self.nc = nc
        self.I = {}
        self.S = {}
        din = lambda n, s, dt=F32: self.I.__setitem__(n, nc.dram_tensor(n, list(s), dt, kind="ExternalInput").ap())
        din("x", [T, D]); din("ctx", [NCTX, D]); din("cvecT", [128, 8, 2])
        din("norm_g", [4, D]); din("w_ada", [4, D, 3 * D]); din("b_ada", [4, 3 * D])
        din("mla_w_in", [2, D, 1472]); din("mla_g_qa", [2, 256]); din("mla_w_qup", [2, 256, 1536])
        din("mla_w_qup_sw", [2, 256, 512]); din("mla_g_kva", [2, 128]); din("mla_w_kvup", [2, 128, 2048])
        din("mla_w_ukT", [2, 8, 128, 128]); din("mla_w_out", [2, D, D])
        din("fno_w_in", [2, D, 2 * D]); din("fno_w_out", [2, D, D]); din("final_g", [1, D])
        din("rope_k", [T, 2, 64]); din("rope_q", [2, 64, T])
        din("ident", [128, 128], BF16); din("dft_t1", [128, 2, 128], BF16); din("dft_tw", [128, 2, 64])
        din("dft_r2", [128, 128], BF16); din("dft_c1", [128, 2, 128], BF16); din("dft_ctx", [128, 2, 512], BF16)
        self.out = nc.dram_tensor("out", [T, D], F32, kind="ExternalOutput").ap()
        kind = "ExternalOutput" if debug else "Internal"
        dsc = lambda n, s, dt: self.S.__setitem__(n, nc.dram_tensor("s_" + n, list(s), dt, kind=kind).ap())
        dsc("h", [T, D], F32); dsc("hc", [NCTX, D], F32); dsc("modd", [4, 2, 3, D], F32)
        dsc("QA", [8, 128, NK], BF16); dsc("QR", [8, 64, NK], BF16)
        dsc("KLT", [128, NK], BF16); dsc("KRT", [64, NK], BF16); dsc("V", [128, NKC, 128], BF16)
        dsc("GATE", [D, NK], BF16); dsc("Bd", [2, 64, 128, D], BF16)

        for n in inject:
            a = self.S[n]
            self.I["inj_" + n] = nc.dram_tensor("inj_" + n, list(a.shape), a.dtype, kind="ExternalInput").ap()

    def phase_inject(self):
        ph = Phase(self.nc, "inj")
        for n in self.inject:
            ph.dma("sp", self.S[n], self.I["inj_" + n], semkey="inj_" + n)
        ph.emit()

    def phase_mod(self):
        nc, I, S = self.nc, self.I, self.S
        ph = Phase(nc, "p0")
        actT, actT_k = ph.sb([128, 8, 2], F32, "actT")
        ph.dma("sp", actT[:], I["cvecT"], writes=[actT_k], semkey=actT_k)
        OP_act(ph, actT[:], actT[:], AF.Silu, [actT_k], [actT_k])
        wrot = rot_sb(ph, 3, [128, 3 * D], F32, "wada")
        psb = [ph.ps([128, 512], F32, "pm") for _ in range(6)]
        for i in range(4):
            bt, bt_k = ph.sb([2, 3 * D], F32, "bada")
            gt, gt_k = ph.sb([2, D], F32, "ng")
            for s in range(2):
                ph.dma("sp", bt[s:s + 1, :], I["b_ada"][i:i + 1, :], writes=[bt_k], semkey=bt_k)
                ph.dma("sp", gt[s:s + 1, :], I["norm_g"][i:i + 1, :], writes=[gt_k], semkey=gt_k)
            for c in range(8):
                wt, wt_k = wrot.next()
                ph.dma("sp", wt[:], I["w_ada"][i, c * 128:(c + 1) * 128, :], writes=[wt_k], semkey=wt_k)
                for n in range(6):
                    OP_mm(ph, psb[n][0][0:2, :], actT[:, c, :], wt[:, n * 512:(n + 1) * 512], c == 0, c == 7,
                          [actT_k, wt_k], [psb[n][1]])
            md, md_k = ph.sb([2, 3 * D], F32, "mod")
            for n in range(6):
                OP_tt(ph, "dve", md[:, n * 512:(n + 1) * 512], psb[n][0][0:2, :], bt[:, n * 512:(n + 1) * 512],
                      ALU.add, [psb[n][1], bt_k], [md_k])
            OP_stt(ph, "dve", md[:, D:2 * D], md[:, D:2 * D], 1.0, gt[:], ALU.add, ALU.mult, [md_k, gt_k], [md_k])
            for s in range(2):
                ph.dma("sp", S["modd"][i, s:s + 1, 0, :], md[s:s + 1, D:2 * D], reads=[md_k], semkey=md_k + "o")
                ph.dma("sp", S["modd"][i, s:s + 1, 1, :], md[s:s + 1, 0:D], reads=[md_k], semkey=md_k + "o")
                ph.dma("sp", S["modd"][i, s:s + 1, 2, :], md[s:s + 1, 2 * D:3 * D], reads=[md_k], semkey=md_k + "o")
        ph.emit()

    def load_bcast(self, ph, dram_row_ap, n, name):
        t, k = ph.sb([128, n], F32, name)
        ph.dma("sp", t[:], dram_row_ap.partition_broadcast(128), writes=[k], semkey=k)
        return t, k

    def load_w_bf16(self, ph, dst, dst_k, src, stage):
        st, st_k = stage.next()
        n = src.shape[-1]
        ph.dma("sp", st[:, 0:n], src, writes=[st_k], semkey=st_k)
        OP_cp(ph, "pool", dst, st[:, 0:n], [st_k], [dst_k])

    def u_tile(self, ph, R, src_ap, G1b, SHb, mod_keys, uT, uT_k, slot, ident, ident_k, key2=None):
        xt, xt_k = R["x"].next()
        if isinstance(src_ap, (list, tuple)):
            hp = 128 // len(src_ap)
            for j, a in enumerate(src_ap):
                ph.dma("sp", xt[j * hp:(j + 1) * hp, :], a, writes=[xt_k], semkey=xt_k)
        else:
            ph.dma("sp", xt[:], src_ap, writes=[xt_k], semkey=xt_k)
        junk, junk_k = R["junk"].next()
        ss, ss_k = R["ss"].next()
        OP_act(ph, junk[:], xt[:], AF.Square, [xt_k], [junk_k, ss_k], scale=1.0 / 32.0, accum=ss[:, 0:1])
        OP_rstd(ph, ss[:, 1:2], ss[:, 0:1], ss_k)
        un, un_k = R["un"].next()
        OP_stt(ph, "dve", un[:], xt[:], ss[:, 1:2], G1b[:], ALU.mult, ALU.mult, [xt_k, ss_k, mod_keys[0]], [un_k])
        ub, ub_k = R["ub"].next()
        OP_tt(ph, "pool", ub[:], un[:], SHb[:], ALU.add, [un_k, mod_keys[1]], [ub_k])
        if key2 is not None:
            ph.cur_key = key2
        pT, pT_k = R["pT"].next()
        for c in range(8):
            OP_tr(ph, pT[:, c * 128:(c + 1) * 128], ub[:, c * 128:(c + 1) * 128], ident[:], [ub_k, ident_k], [pT_k])
        OP_cp(ph, "act", uT[:, :, slot * 128:(slot + 1) * 128], pT[:].rearrange("p (c t) -> p c t", c=8),
              [pT_k], [uT_k + f"_{slot}"])
        return xt, xt_k

    def phase_mla_a(self, li, src_lat, src_ctx, ctx_q):
        nc, I, S = self.nc, self.I, self.S
        idx = li // 2
        ph = Phase(nc, f"a{li}")
        ident, ident_k = ph.sb([128, 128], BF16, "ident")
        ph.dma("sp", ident[:], I["ident"], writes=[ident_k], semkey=ident_k)
        mods = {}
        for s in range(2):
            mods[s] = (self.load_bcast(ph, S["modd"][li, s:s + 1, 0, :], D, "g1b"),
                       self.load_bcast(ph, S["modd"][li, s:s + 1, 1, :], D, "shb"))
        gqa, gqa_k = self.load_bcast(ph, I["mla_g_qa"][idx:idx + 1, :], 256, "gqa")
        gkva, gkva_k = self.load_bcast(ph, I["mla_g_kva"][idx:idx + 1, :], 128, "gkva")
        stage = rot_sb(ph, 2, [128, 1536], F32, "wst")
        w_in, w_in_k = ph.sb([128, 8, 1472], BF16, "w_in")
        for c in range(8):
            self.load_w_bf16(ph, w_in[:, c, :], w_in_k, I["mla_w_in"][idx, c * 128:(c + 1) * 128, :], stage)
        w_q, w_q_k = ph.sb([128, 2, 1536], BF16, "w_q")
        w_qs, w_qs_k = ph.sb([128, 2, 512], BF16, "w_qs")
        for c in range(2):
            self.load_w_bf16(ph, w_q[:, c, :], w_q_k, I["mla_w_qup"][idx, c * 128:(c + 1) * 128, :], stage)
            self.load_w_bf16(ph, w_qs[:, c, :], w_qs_k, I["mla_w_qup_sw"][idx, c * 128:(c + 1) * 128, :], stage)
        w_uk, w_uk_k = ph.sb([128, 8, 128], BF16, "w_uk")
        for h in range(8):
            self.load_w_bf16(ph, w_uk[:, h, :], w_uk_k, I["mla_w_ukT"][idx, h], stage)

        R = {"x": rot_sb(ph, 3, [128, D], F32, "x"), "junk": rot_sb(ph, 1, [128, D], BF16, "junk"),
             "ss": rot_sb(ph, 4, [128, 8], F32, "ss"), "un": rot_sb(ph, 3, [128, D], F32, "un"),
             "ub": rot_sb(ph, 3, [128, D], BF16, "ub"), "pT": rot_ps(ph, 2, [128, D], BF16, "pT")}
        ss2_r = rot_sb(ph, 3, [128, 8], F32, "ss2")
        junk2_r = rot_sb(ph, 1, [128, 512], BF16, "junk2")
        uTs = rot_sb(ph, 2, [128, 8, 512], BF16, "uT")
        cqTs = rot_sb(ph, 2, [128, 2, 512], BF16, "cqT")
        psm = rot_ps(ph, 1, [128, 512], F32, "psm")
        pT2 = rot_ps(ph, 1, [128, 512], BF16, "pT2")
        pbig = rot_ps(ph, 4, [128, 512], F32, "pbig")
        rk = rot_sb(ph, 2, [128, 2, 64], F32, "rk")
        sm = rot_sb(ph, 2, [128, 512], BF16, "sm")
        tmpk = rot_sb(ph, 2, [128, 3, 64], F32, "tmpk")
        kst = rot_sb(ph, 2, [128, 2, 128], BF16, "kst")
        gst = rot_sb(ph, 3, [128, 512], BF16, "gst")
        qn_r = rot_sb(ph, 2, [128, 512], BF16, "qn")
        qa_r = rot_sb(ph, 3, [128, 512], BF16, "qa")
        qr_r = rot_sb(ph, 3, [64, 512], BF16, "qr")
        rq_r = rot_sb(ph, 2, [64, 2, 512], F32, "rq")
        t12 = rot_sb(ph, 2, [64, 2, 512], F32, "t12")

        groups = [("lat", g * 512, 512) for g in range(T // 512)] + [("ctx", 0, NCTX)]
        tg = 0
        ph.cur_key = -1.0
        for kind, t0, ntok in groups:
            s = 0 if kind == "lat" else 1
            (G1b, G1b_k), (SHb, SHb_k) = mods[s]
            nt = ntok // 128
            uT, uT_k = uTs.next()
            cqT, cqT_k = cqTs.next()
            gcol0 = t0 if kind == "lat" else T
            kcol0 = (NCTX + t0) if kind == "lat" else 0
            uT_toks = [uT_k + f"_{j}" for j in range(nt)]
            cq_toks = [cqT_k + f"_{j}" for j in range(nt)]
            for j in range(nt):
                r0 = t0 + j * 128
                src = (src_lat if kind == "lat" else src_ctx)[r0:r0 + 128, :]
                ph.cur_key = float(tg)
                self.u_tile(ph, R, src, G1b, SHb, (G1b_k, SHb_k), uT, uT_k, j, ident, ident_k, key2=tg + 1.5)
                ph.cur_key = tg + 1.5
                tg += 1
                pm, pm_k = psm.next()
                for c in range(8):
                    OP_mm(ph, pm[:, 0:448], uT[:, c, j * 128:(j + 1) * 128], w_in[:, c, 0:448], c == 0, c == 7,
                          [uT_toks[j], w_in_k], [pm_k])
                ss, ss_k = ss2_r.next()
                junk, junk_k = junk2_r.next()
                OP_act(ph, junk[:, 0:256], pm[:, 0:256], AF.Square, [pm_k], [junk_k, ss_k], scale=1.0 / 16.0,
                       accum=ss[:, 0:1])
                OP_act(ph, junk[:, 256:384], pm[:, 256:384], AF.Square, [pm_k], [junk_k, ss_k],
                       scale=1.0 / math.sqrt(128.0), accum=ss[:, 1:2])
                OP_rstd(ph, ss[:, 2:4], ss[:, 0:2], ss_k)
                smt, smt_k = sm.next()
                OP_stt(ph, "dve", smt[:, 0:256], pm[:, 0:256], ss[:, 2:3], gqa[:], ALU.mult, ALU.mult,
                       [pm_k, ss_k, gqa_k], [smt_k])
                OP_stt(ph, "dve", smt[:, 256:384], pm[:, 256:384], ss[:, 3:4], gkva[:], ALU.mult, ALU.mult,
                       [pm_k, ss_k, gkva_k], [smt_k])
                if kind == "lat":
                    rkt, rkt_k = rk.next()
                    ph.dma("sp", rkt[:], I["rope_k"][r0:r0 + 128], writes=[rkt_k], semkey=rkt_k)
                    tk, tk_k = tmpk.next()
                    OP_tt(ph, "dve", tk[:, 0, :], pm[:, 384:448], rkt[:, 0, :], ALU.mult, [pm_k, rkt_k], [tk_k])
                    pv = pm[:, 384:448].rearrange("p (b h d) -> p b h d", b=2, h=2)
                    sv = rkt[:, 1, :].rearrange("p (b h d) -> p b h d", b=2, h=2)
                    dv = tk[:, 1, :].rearrange("p (b h d) -> p b h d", b=2, h=2)
                    OP_tt(ph, "dve", dv[:, :, 0, :], pv[:, :, 1, :], sv[:, :, 0, :], ALU.mult, [pm_k, rkt_k], [tk_k])
                    OP_tt(ph, "dve", dv[:, :, 1, :], pv[:, :, 0, :], sv[:, :, 1, :], ALU.mult, [pm_k, rkt_k], [tk_k])
                    OP_tt(ph, "dve", smt[:, 384:448], tk[:, 0, :], tk[:, 1, :], ALU.add, [tk_k], [smt_k])
                else:
                    OP_cp(ph, "dve", smt[:, 384:448], pm[:, 384:448], [pm_k], [smt_k])
                ph.dma("pool", S["V"][:, (kcol0 + j * 128) // 128, :], smt[:, 256:384], reads=[smt_k],
                       semkey=smt_k + "v")
                p2, p2_k = pT2.next()
                OP_tr(ph, p2[:, 0:128], smt[:, 0:128], ident[:], [smt_k, ident_k], [p2_k])
                OP_tr(ph, p2[:, 128:256], smt[:, 128:256], ident[:], [smt_k, ident_k], [p2_k])
                OP_tr(ph, p2[:, 256:384], smt[:, 256:384], ident[:], [smt_k, ident_k], [p2_k])
                OP_tr(ph, p2[0:64, 384:512], smt[:, 384:448], ident[:], [smt_k, ident_k], [p2_k])
                OP_cp(ph, "act", cqT[:, :, j * 128:(j + 1) * 128], p2[:, 0:256].rearrange("p (c t) -> p c t", c=2),
                      [p2_k], [cq_toks[j]])
                ks, ks_k = kst.next()
                OP_cp(ph, "act", ks[:, 0, :], p2[:, 256:384], [p2_k], [ks_k])
                OP_cp(ph, "act", ks[0:64, 1, :], p2[0:64, 384:512], [p2_k], [ks_k])
                ph.dma("pool", S["KLT"][:, kcol0 + j * 128:kcol0 + (j + 1) * 128], ks[:, 0, :], reads=[ks_k],
                       semkey=ks_k + "o")
                ph.dma("pool", S["KRT"][:, kcol0 + j * 128:kcol0 + (j + 1) * 128], ks[0:64, 1, :], reads=[ks_k],
                       semkey=ks_k + "o")
            ph.cur_key = tg - 1 + 3.5
            for n in range(8):
                pb, pb_k = pbig.next()
                for c in range(8):
                    OP_mm(ph, pb[:, 0:ntok], w_in[:, c, 448 + n * 128:448 + (n + 1) * 128], uT[:, c, 0:ntok],
                          c == 0, c == 7, uT_toks + [w_in_k], [pb_k])
                g, g_k = gst.next()
                OP_act(ph, g[:, 0:ntok], pb[:, 0:ntok], AF.Silu, [pb_k], [g_k])
                ph.dma("pool", S["GATE"][n * 128:(n + 1) * 128, gcol0:gcol0 + ntok], g[:, 0:ntok], reads=[g_k],
                       semkey=g_k + "o")
            if kind == "ctx" and not ctx_q:
                continue
            if kind == "lat":
                rq, rq_k = rq_r.next()
                ph.dma("sp", rq[:, :, 0:ntok], I["rope_q"][:, :, t0:t0 + ntok].rearrange("a d t -> d a t"),
                       writes=[rq_k], semkey=rq_k)
            for h in range(8):
                pb, pb_k = pbig.next()
                for c in range(2):
                    OP_mm(ph, pb[:, 0:ntok], w_q[:, c, h * 192:h * 192 + 128], cqT[:, c, 0:ntok], c == 0, c == 1,
                          cq_toks + [w_q_k], [pb_k])
                qn, qn_k = qn_r.next()
                OP_cp(ph, "act", qn[:, 0:ntok], pb[:, 0:ntok], [pb_k], [qn_k])
                pb2, pb2_k = pbig.next()
                OP_mm(ph, pb2[:, 0:ntok], w_uk[:, h, :], qn[:, 0:ntok], True, True, [qn_k, w_uk_k], [pb2_k])
                qa, qa_k = qa_r.next()
                OP_cp(ph, "dve", qa[:, 0:ntok], pb2[:, 0:ntok], [pb2_k], [qa_k])
                ph.dma("pool", S["QA"][h, :, gcol0:gcol0 + ntok], qa[:, 0:ntok], reads=[qa_k], semkey=qa_k + "o")
                pr, pr_k = pbig.next()
                for c in range(2):
                    OP_mm(ph, pr[0:64, 0:ntok], w_q[:, c, h * 192 + 128:h * 192 + 192], cqT[:, c, 0:ntok], c == 0,
                          c == 1, cq_toks + [w_q_k], [pr_k])
                qr, qr_k = qr_r.next()
                if kind == "lat":
                    pss, pss_k = pbig.next()
                    for c in range(2):
                        OP_mm(ph, pss[0:64, 0:ntok], w_qs[:, c, h * 64:(h + 1) * 64], cqT[:, c, 0:ntok], c == 0,
                              c == 1, cq_toks + [w_qs_k], [pss_k])
                    tt, tt_k = t12.next()
                    OP_tt(ph, "dve", tt[:, 0, 0:ntok], pr[0:64, 0:ntok], rq[:, 0, 0:ntok], ALU.mult, [pr_k, rq_k],
                          [tt_k])
                    OP_tt(ph, "dve", tt[:, 1, 0:ntok], pss[0:64, 0:ntok], rq[:, 1, 0:ntok], ALU.mult,
                          [pss_k, rq_k], [tt_k])
                    OP_tt(ph, "pool", qr[:, 0:ntok], tt[:, 0, 0:ntok], tt[:, 1, 0:ntok], ALU.add, [tt_k], [qr_k])
                else:
                    OP_cp(ph, "dve", qr[:, 0:ntok], pr[0:64, 0:ntok], [pr_k], [qr_k])
                ph.dma("pool", S["QR"][h, :, gcol0:gcol0 + ntok], qr[:, 0:ntok], reads=[qr_k], semkey=qr_k + "o")
        ph.emit()

    def phase_mla_b(self, li, src_lat, src_ctx, ctx_q):
        nc, I, S = self.nc, self.I, self.S
        idx = li // 2
        ph = Phase(nc, f"b{li}")
        KLT, KLT_k = ph.sb([128, NK], BF16, "KLT")
        KRT, KRT_k = ph.sb([128, NK], BF16, "KRT")
        V, V_k = ph.sb([128, NKC, 128], BF16, "V")
        ph.op("pool", lambda e: e.memset(KRT[64:128, :], 0.0), [], [KRT_k])
        for q4 in range(4):
            c0, c1 = q4 * (NK // 4), (q4 + 1) * (NK // 4)
            ph.dma("sp", KLT[:, c0:c1], S["KLT"][:, c0:c1], writes=[KLT_k], semkey=KLT_k)
        ph.dma("sp", KRT[0:64, :], S["KRT"], writes=[KRT_k], semkey=KRT_k)
        for q4 in range(3):
            ph.dma("sp", V[:, q4 * 22:(q4 + 1) * 22, :], S["V"][:, q4 * 22:(q4 + 1) * 22, :], writes=[V_k], semkey=V_k)
        ones, ones_k = ph.sb([128, 128], BF16, "ones")
        ph.op("pool", lambda e: e.memset(ones[:], 1.0), [], [ones_k])
        stage = rot_sb(ph, 2, [128, 1024], F32, "wst")
        w_uv, w_uv_k = ph.sb([128, 8, 128], BF16, "w_uv")
        w_o, w_o_k = ph.sb([128, 8, D], BF16, "w_o")
        for h in range(8):
            self.load_w_bf16(ph, w_uv[:, h, :], w_uv_k, I["mla_w_kvup"][idx, :, h * 256 + 128:h * 256 + 256], stage)
            self.load_w_bf16(ph, w_o[:, h, :], w_o_k, I["mla_w_out"][idx, h * 128:(h + 1) * 128, :], stage)
        GT = {}
        for s in range(2):
            GT[s] = self.load_bcast(ph, S["modd"][li, s:s + 1, 2, :], D, "gtb")

        qa_r = rot_sb(ph, 2, [128, 512], BF16, "qa")
        qr_r = rot_sb(ph, 2, [128, 512], BF16, "qr")
        for qr_t, qr_tk in qr_r.items:
            ph.op("pool", (lambda t: (lambda e: e.memset(t[64:128, :], 0.0)))(qr_t), [], [qr_tk])
        gt_r = rot_sb(ph, 4, [128, 512], BF16, "gate")
        ps_s = rot_ps(ph, 3, [128, 1024], F32, "pss")
        ps_o = rot_ps(ph, 1, [128, 512], F32, "pso")
        ps_u = rot_ps(ph, 1, [128, 512], F32, "psu")
        ps_l = ps_u
        posb_r = rot_sb(ph, 2, [128, 512], F32, "posb")
        pt_r = rot_sb(ph, 6, [128, 1024], BF16, "pt")
        tmp_r = rot_sb(ph, 2, [128, 1024], BF16, "ptsum")
        accA_r = rot_sb(ph, 4, [128, 1024], F32, "accA")
        accB_r = rot_sb(ph, 1, [128, 8], F32, "accB")
        t2_r = rot_sb(ph, 2, [128, 512], BF16, "tot2")
        rs_r = rot_sb(ph, 2, [128, 512], F32, "rs")
        op_r = rot_sb(ph, 2, [128, 512], BF16, "op")
        go_r = rot_sb(ph, 2, [128, 8, 512], BF16, "go")
        h_r = rot_sb(ph, 2, [128, D], F32, "h")
        t_r = rot_sb(ph, 2, [128, D], F32, "t")
        LAG = 2
        pending = []

        def run_pending():
            while pending:
                f = pending.pop(0)
                if f is not None:
                    f()

        tiles = [("lat", g * 512, 512) for g in range(T // 512)]
        if ctx_q:
            tiles.append(("ctx", 0, NCTX))

        class HC:
            pass

        heads = []
        for kind, t0, ntok in tiles:
            kcs = list(range(NKC)) if kind == "lat" else [0, 1]
            tile_ctx = {}
            for h in range(8):
                hc = HC()
                hc.kind, hc.t0, hc.ntok, hc.h = kind, t0, ntok, h
                hc.s = 0 if kind == "lat" else 1
                hc.gcol0 = t0 if kind == "lat" else T
                hc.pairs = [(kcs[2 * j], kcs[2 * j + 1]) for j in range(len(kcs) // 2)]
                hc.npair = len(hc.pairs)
                hc.tile_ctx = tile_ctx
                heads.append(hc)
        jobs = [(hc, j) for hc in heads for j in range(hc.npair)]

        def head_begin(hc):
            if hc.npair < 12:
                run_pending()
            if hc.h == 0:
                hc.tile_ctx["go"] = go_r.next()
            hc.go, hc.go_k = hc.tile_ctx["go"]
            ntok = hc.ntok
            hc.qa, hc.qa_k = qa_r.next()
            hc.qr, hc.qr_k = qr_r.next()
            hc.gt, hc.gt_k = gt_r.next()
            ph.dma("sp", hc.qa[:, 0:ntok], S["QA"][hc.h, :, hc.gcol0:hc.gcol0 + ntok], writes=[hc.qa_k],
                   semkey=hc.qa_k)
            ph.dma("sp", hc.qr[0:64, 0:ntok], S["QR"][hc.h, :, hc.gcol0:hc.gcol0 + ntok], writes=[hc.qr_k],
                   semkey=hc.qr_k)
            ph.dma("sp", hc.gt[:, 0:ntok], S["GATE"][hc.h * 128:(hc.h + 1) * 128, hc.gcol0:hc.gcol0 + ntok],
                   writes=[hc.gt_k], semkey=hc.gt_k)
            hc.po, hc.po_k = ps_o.next()
            hc.accA, hc.accA_k = accA_r.next()
            hc.usedA = False
            hc.pts = []

        def emit_qk(hc, j):
            ntok = hc.ntok
            pS, pS_k = ps_s.next()
            for half, kc in enumerate(hc.pairs[j]):
                o_ap = pS[:, half * 512:half * 512 + ntok]
                OP_mm(ph, o_ap, KLT[:, kc * 128:(kc + 1) * 128], hc.qa[:, 0:ntok], True, False, [KLT_k, hc.qa_k],
                      [pS_k])
                OP_mm(ph, o_ap, KRT[:, kc * 128:(kc + 1) * 128], hc.qr[:, 0:ntok], False, True, [KRT_k, hc.qr_k],
                      [pS_k])
            pt, pt_k = pt_r.next()
            if ntok == 512:
                sv, pv = pS[:], pt[:]
            else:
                sv = pS[:].rearrange("p (a t) -> p a t", a=2)[:, :, 0:ntok]
                pv = pt[:].rearrange("p (a t) -> p a t", a=2)[:, :, 0:ntok]
            OP_act(ph, pv, sv, AF.Exp, [pS_k], [pt_k], scale=SCALE)
            hc.pts.append((pt, pt_k))

            def accum(src_ap, src_k):
                accA, accA_k = hc.accA, hc.accA_k
                av = accA[:] if ntok == 512 else accA[:].rearrange("p (a t) -> p a t", a=2)[:, :, 0:ntok]
                if not hc.usedA:
                    OP_cp(ph, "dve", av, src_ap, [src_k], [accA_k])
                else:
                    OP_tt(ph, "dve", av, av, src_ap, ALU.add, [src_k, accA_k], [accA_k])
                hc.usedA = True

            if ntok == 512 and j % 2 == 1:
                tmp, tmp_k = tmp_r.next()
                pprev, pprev_k = hc.pts[j - 1]
                OP_tt(ph, "dve", tmp[:], pprev[:], pt[:], ALU.add, [pprev_k, pt_k], [tmp_k])
                accum(tmp[:], tmp_k)
            elif j == hc.npair - 1:
                accum(pv, pt_k)

        def emit_pv(hc, jj):
            ntok = hc.ntok
            ptj, ptj_k = hc.pts[jj]
            for half, kc in enumerate(hc.pairs[jj]):
                OP_mm(ph, hc.po[:, 0:ntok], V[:, kc, :], ptj[:, half * 512:half * 512 + ntok],
                      jj == 0 and half == 0, jj == hc.npair - 1 and half == 1, [V_k, ptj_k], [hc.po_k])

        def ep_stages(hc):
            h, po, po_k, accA, accA_k = hc.h, hc.posb, hc.posb_k, hc.accA, hc.accA_k
            gt, gt_k, go, go_k, ntok = hc.gt, hc.gt_k, hc.go, hc.go_k, hc.ntok
            st = {}

            def s0():
                st["t2"] = t2_r.next()
                t2, t2_k = st["t2"]
                OP_tt(ph, "dve", t2[:, 0:ntok], accA[:, 0:ntok], accA[:, 512:512 + ntok], ALU.add, [accA_k], [t2_k])

            def s1():
                t2, t2_k = st["t2"]
                st["pl"] = ps_l.next()
                pl, pl_k = st["pl"]
                OP_mm(ph, pl[:, 0:ntok], ones[:], t2[:, 0:ntok], True, True, [ones_k, t2_k], [pl_k])

            def s2():
                pl, pl_k = st["pl"]
                st["rs"] = rs_r.next()
                rs, rs_k = st["rs"]
                OP_act(ph, rs[:, 0:ntok], pl[:, 0:ntok], AF.Ln, [pl_k], [rs_k])
                OP_act(ph, rs[:, 0:ntok], rs[:, 0:ntok], AF.Exp, [rs_k], [rs_k], scale=-1.0)

            def s2b():
                rs, rs_k = st["rs"]
                st["opn"] = op_r.next()
                opn, opn_k = st["opn"]
                OP_tt(ph, "dve", opn[:, 0:ntok], po[:, 0:ntok], rs[:, 0:ntok], ALU.mult, [po_k, rs_k], [opn_k])

            def s3():
                opn, opn_k = st["opn"]
                st["pu"] = ps_u.next()
                pu, pu_k = st["pu"]
                OP_mm(ph, pu[:, 0:ntok], w_uv[:, h, :], opn[:, 0:ntok], True, True, [w_uv_k, opn_k], [pu_k])

            def s4():
                pu, pu_k = st["pu"]
                OP_tt(ph, "dve", go[:, h, 0:ntok], pu[:, 0:ntok], gt[:, 0:ntok], ALU.mult, [pu_k, gt_k],
                      [go_k + f"_{h}"])

            return [s0, None, s1, None, s2, None, None, s2b, None, None, s3, None, None, s4, None]

        def tile_out_stage(hc, ts):
            kind, t0, go, go_k, s = hc.kind, hc.t0, hc.go, hc.go_k, hc.s
            go_toks = [go_k + f"_{h}" for h in range(8)]

            def f():
                (GTb, GTb_k) = GT[s]
                r0 = t0 + ts * 128
                ht, ht_k = h_r.next()
                src = (src_lat if kind == "lat" else src_ctx)[r0:r0 + 128, :]
                dst = (S["h"] if kind == "lat" else S["hc"])[r0:r0 + 128, :]
                ph.dma("sp", ht[:], src, writes=[ht_k], semkey=ht_k)
                tt, tt_k = t_r.next()
                for half in range(2):
                    py, py_k = ps_u.next()
                    for h in range(8):
                        OP_mm(ph, py[:], go[:, h, ts * 128:(ts + 1) * 128], w_o[:, h, half * 512:(half + 1) * 512],
                              h == 0, h == 7, go_toks + [w_o_k], [py_k])
                    OP_tt(ph, "dve", tt[:, half * 512:(half + 1) * 512], py[:], GTb[:, half * 512:(half + 1) * 512],
                          ALU.mult, [py_k, GTb_k], [tt_k])
                OP_tt(ph, "pool", tt[:], tt[:], ht[:], ALU.add, [tt_k, ht_k], [tt_k])
                ph.dma("pool", dst, tt[:], reads=[tt_k], semkey=tt_k + "o")
            return f

        for g in range(len(jobs) + LAG):
            if g < len(jobs):
                hc, j = jobs[g]
                if j == 0:
                    head_begin(hc)
                if hc.npair >= 12 and j >= 2 and pending:
                    f = pending.pop(0)
                    if f is not None:
                        f()
                emit_qk(hc, j)
            if g >= LAG:
                hc2, j2 = jobs[g - LAG]
                emit_pv(hc2, j2)
                if j2 == hc2.npair - 1:
                    hc2.posb, hc2.posb_k = posb_r.next()
                    OP_cp(ph, "dve", hc2.posb[:, 0:hc2.ntok], hc2.po[:, 0:hc2.ntok], [hc2.po_k], [hc2.posb_k])
                    pending.extend(ep_stages(hc2))
                    if hc2.h == 7:
                        for ts in range(hc2.ntok // 128):
                            pending.append(None)
                            pending.append(tile_out_stage(hc2, ts))
                    if hc2.npair < 12:
                        run_pending()
        run_pending()
        ph.emit()

    def phase_f1(self, li, do_ctx):
        nc, I, S = self.nc, self.I, self.S
        idx = li // 2
        ph = Phase(nc, f"f{li}")
        ident, ident_k = ph.sb([128, 128], BF16, "ident")
        ph.dma("sp", ident[:], I["ident"], writes=[ident_k], semkey=ident_k)
        mods = {}
        for s in range(2):
            mods[s] = (self.load_bcast(ph, S["modd"][li, s:s + 1, 0, :], D, "g1b"),
                       self.load_bcast(ph, S["modd"][li, s:s + 1, 1, :], D, "shb"))
        stage = rot_sb(ph, 2, [128, 1024], F32, "wst")
        self.f_stage = stage
        w_in, w_in_k = ph.sb([128, 8, 2048], BF16, "w_in")
        for c in range(8):
            for hf in range(2):
                self.load_w_bf16(ph, w_in[:, c, hf * 1024:(hf + 1) * 1024], w_in_k,
                                 I["fno_w_in"][idx, c * 128:(c + 1) * 128, hf * 1024:(hf + 1) * 1024], stage)
        t1m, t1m_k = ph.sb([128, 2, 128], BF16, "t1m")
        ph.dma("sp", t1m[:], I["dft_t1"], writes=[t1m_k], semkey=t1m_k)
        tw, tw_k = ph.sb([128, 2, 64], F32, "tw")
        ph.dma("sp", tw[:], I["dft_tw"], writes=[tw_k], semkey=tw_k)
        R = {"x": rot_sb(ph, 3, [128, D], F32, "x"), "junk": rot_sb(ph, 1, [128, D], BF16, "junk"),
             "ss": rot_sb(ph, 4, [128, 8], F32, "ss"), "un": rot_sb(ph, 3, [128, D], F32, "un"),
             "ub": rot_sb(ph, 3, [128, D], BF16, "ub"), "pT": rot_ps(ph, 1, [128, D], BF16, "pT")}
        uTs = rot_sb(ph, 2, [128, 8, 512], BF16, "uT")
        pz = rot_ps(ph, 1, [128, 1024], F32, "pz")
        pa = rot_ps(ph, 1, [128, 2, 1024], F32, "pa")
        pg = rot_ps(ph, 1, [128, 512], F32, "pg")
        zt_r = rot_sb(ph, 2, [128, D], BF16, "zt")
        tm_r = rot_sb(ph, 1, [128, 2, D], F32, "tm")
        b_r = rot_sb(ph, 2, [128, 2, D], BF16, "bt")
        gst = rot_sb(ph, 3, [128, 512], BF16, "gst")
        hv = S["h"].rearrange("(t1 t2) d -> t2 t1 d", t2=64)
        if do_ctx:
            zc, zc_k = ph.sb([128, 2, D], BF16, "zc")
            gc, gc_k = ph.sb([128, 8, NCTX], BF16, "gc")
        groups = [("lat", g * 4, 4) for g in range(16)] + ([("ctx", 0, 2)] if do_ctx else [])
        tg = 0
        ph.cur_key = -1.0
        for kind, j0, nt in groups:
            s = 0 if kind == "lat" else 1
            (G1b, G1b_k), (SHb, SHb_k) = mods[s]
            uT, uT_k = uTs.next()
            ntok = nt * 128
            uT_toks = [uT_k + f"_{j}" for j in range(nt)]
            for j in range(nt):
                if kind == "lat":
                    t2 = j0 + j
                    src = hv[t2]
                else:
                    src = S["hc"][j * 128:(j + 1) * 128, :]
                ph.cur_key = float(tg)
                self.u_tile(ph, R, src, G1b, SHb, (G1b_k, SHb_k), uT, uT_k, j, ident, ident_k, key2=tg + 1.5)
                ph.cur_key = tg + 1.5
                tg += 1
                z, z_k = pz.next()
                for half in range(2):
                    for c in range(8):
                        OP_mm(ph, z[:, half * 512:(half + 1) * 512], uT[:, c, j * 128:(j + 1) * 128],
                              w_in[:, c, half * 512:(half + 1) * 512], c == 0, c == 7, [uT_toks[j], w_in_k], [z_k])
                if kind == "ctx":
                    OP_cp(ph, "dve", zc[:, j, :], z[:], [z_k], [zc_k + f"_{j}"])
                    continue
                zt, zt_k = zt_r.next()
                OP_cp(ph, "dve", zt[:], z[:], [z_k], [zt_k])
                a, a_k = pa.next()
                for comp in range(2):
                    for half in range(2):
                        OP_mm(ph, a[:, comp, half * 512:(half + 1) * 512], t1m[:, comp, :],
                              zt[:, half * 512:(half + 1) * 512], True, True, [t1m_k, zt_k], [a_k])
                tm, tm_k = tm_r.next()
                OP_act(ph, tm[:, 0, :], a[:, 1, :], AF.Copy, [a_k, tw_k], [tm_k], scale=tw[:, 1, t2:t2 + 1])
                OP_act(ph, tm[:, 1, :], a[:, 1, :], AF.Copy, [a_k, tw_k], [tm_k], scale=tw[:, 0, t2:t2 + 1])
                bt, bt_k = b_r.next()
                OP_stt(ph, "dve", bt[:, 0, :], a[:, 0, :], tw[:, 0, t2:t2 + 1], tm[:, 0, :], ALU.mult, ALU.subtract,
                       [a_k, tw_k, tm_k], [bt_k])
                OP_stt(ph, "dve", bt[:, 1, :], a[:, 0, :], tw[:, 1, t2:t2 + 1], tm[:, 1, :], ALU.mult, ALU.add,
                       [a_k, tw_k, tm_k], [bt_k])
                for comp in range(2):
                    ph.dma("pool", S["Bd"][comp, t2], bt[:, comp, :], reads=[bt_k], semkey=bt_k + "o")
            ph.cur_key = tg - 1 + 3.5
            for n in range(8):
                g_ps, g_ps_k = pg.next()
                for c in range(8):
                    OP_mm(ph, g_ps[:, 0:ntok], w_in[:, c, D + n * 128:D + (n + 1) * 128], uT[:, c, 0:ntok], c == 0,
                          c == 7, uT_toks + [w_in_k], [g_ps_k])
                if kind == "lat":
                    g, g_k = gst.next()
                    OP_act(ph, g[:, 0:ntok], g_ps[:, 0:ntok], AF.Silu, [g_ps_k], [g_k])
                    ph.dma("pool", S["GATE"][n * 128:(n + 1) * 128, j0 * 128:j0 * 128 + ntok], g[:, 0:ntok],
                           reads=[g_k], semkey=g_k + "o")
                else:
                    OP_act(ph, gc[:, n, :], g_ps[:, 0:ntok], AF.Silu, [g_ps_k], [gc_k + f"_{n}"])
        ph.cur_key = 1e6
        if do_ctx:
            self.f_ctx_tail(ph, li, zc, [zc_k + "_0", zc_k + "_1"], gc, [gc_k + f"_{n}" for n in range(8)], pz, pa, pg)
        ph.emit()

    def f_ctx_tail(self, ph, li, zc, zc_toks, gc, gc_toks, pz, pa, pg):
        nc, I, S = self.nc, self.I, self.S
        idx = li // 2
        dc, dc_k = ph.sb([128, 2, 512], BF16, "dctx")
        ph.dma("sp", dc[:], I["dft_ctx"], writes=[dc_k], semkey=dc_k)
        c1, c1_k = ph.sb([128, 2, 128], BF16, "c1")
        ph.dma("sp", c1[:], I["dft_c1"], writes=[c1_k], semkey=c1_k)
        stage = self.f_stage
        w_o, w_o_k = ph.sb([128, 8, D], BF16, "w_o")
        for c in range(8):
            self.load_w_bf16(ph, w_o[:, c, :], w_o_k, I["fno_w_out"][idx, c * 128:(c + 1) * 128, :], stage)
        GTb, GTb_k = self.load_bcast(ph, S["modd"][li, 1:2, 2, :], D, "gtc")
        xt, xt_k = ph.sb([128, 8, 512], BF16, "xT")
        gy, gy_k = ph.sb([128, 8, NCTX], BF16, "gy")
        for g in range(8):
            p, p_k = pg.next()
            for tc in range(2):
                OP_mm(ph, p[:], zc[:, tc, g * 128:(g + 1) * 128], dc[:, tc, :], tc == 0, tc == 1,
                      zc_toks + [dc_k], [p_k])
            OP_cp(ph, "dve", xt[:, g, :], p[:], [p_k], [xt_k + f"_{g}"])
            p2, p2_k = pg.next()
            OP_mm(ph, p2[:, 0:NCTX], c1[:, 0, :], xt[:, g, 0:NCTX], True, False, [c1_k, xt_k + f"_{g}"], [p2_k])
            OP_mm(ph, p2[:, 0:NCTX], c1[:, 1, :], xt[:, g, NCTX:2 * NCTX], False, True, [c1_k, xt_k + f"_{g}"], [p2_k])
            OP_tt(ph, "dve", gy[:, g, :], p2[:, 0:NCTX], gc[:, g, :], ALU.mult, [p2_k, gc_toks[g]], [gy_k + f"_{g}"])
        gy_toks = [gy_k + f"_{g}" for g in range(8)]
        h_r = rot_sb(ph, 2, [128, D], F32, "hcx")
        t_r = rot_sb(ph, 2, [128, D], F32, "tcx")
        for ts in range(2):
            ht, ht_k = h_r.next()
            ph.dma("sp", ht[:], S["hc"][ts * 128:(ts + 1) * 128, :], writes=[ht_k], semkey=ht_k)
            tt, tt_k = t_r.next()
            for half in range(2):
                py, py_k = pg.next()
                for g in range(8):
                    OP_mm(ph, py[:], gy[:, g, ts * 128:(ts + 1) * 128], w_o[:, g, half * 512:(half + 1) * 512],
                          g == 0, g == 7, gy_toks + [w_o_k], [py_k])
                OP_tt(ph, "dve", tt[:, half * 512:(half + 1) * 512], py[:], GTb[:, half * 512:(half + 1) * 512],
                      ALU.mult, [py_k, GTb_k], [tt_k])
            OP_tt(ph, "pool", tt[:], tt[:], ht[:], ALU.add, [tt_k, ht_k], [tt_k])
            ph.dma("pool", S["hc"][ts * 128:(ts + 1) * 128, :], tt[:], reads=[tt_k], semkey=tt_k + "o")

    def phase_f2(self, li, final):
        nc, I, S = self.nc, self.I, self.S
        idx = li // 2
        ph = Phase(nc, f"g{li}")
        r2, r2_k = ph.sb([128, 128], BF16, "r2")
        ph.dma("sp", r2[:], I["dft_r2"], writes=[r2_k], semkey=r2_k)
        c1, c1_k = ph.sb([128, 2, 128], BF16, "c1")
        ph.dma("sp", c1[:], I["dft_c1"], writes=[c1_k], semkey=c1_k)
        stage = rot_sb(ph, 2, [128, 1024], F32, "wst")
        w_o, w_o_k = ph.sb([128, 8, D], BF16, "w_o")
        for c in range(8):
            self.load_w_bf16(ph, w_o[:, c, :], w_o_k, I["fno_w_out"][idx, c * 128:(c + 1) * 128, :], stage)
        GTb, GTb_k = self.load_bcast(ph, S["modd"][li, 0:1, 2, :], D, "gtb")
        if final:
            FGb, FGb_k = self.load_bcast(ph, I["final_g"], D, "fg")
        KB = 8
        b_r = rot_sb(ph, 2, [128, KB, D], BF16, "bblk")
        g_r = rot_sb(ph, 2, [128, 8, KB * 128], BF16, "gblk")
        px = rot_ps(ph, 2, [128, 512], F32, "px")
        pyy = rot_ps(ph, 2, [128, 512], F32, "pyy")
        po = rot_ps(ph, 2, [128, 512], F32, "po")
        xt_r = rot_sb(ph, 2, [128, 8, KB * 128], BF16, "xT")
        gy_r = rot_sb(ph, 2, [128, 8, KB * 64], BF16, "gy")
        h_r = rot_sb(ph, 2, [128, D], F32, "h")
        t_r = rot_sb(ph, 2, [128, D], F32, "t")
        ss_r = rot_sb(ph, 4, [128, 8], F32, "ss")
        junk, junk_k = ph.sb([128, D], BF16, "junk")
        Bv = S["Bd"].rearrange("c t k d -> (c t) k d")
        hv = S["h"].rearrange("(k2 k1) d -> k1 k2 d", k1=128)
        ov = self.out.rearrange("(k2 k1) d -> k1 k2 d", k1=128)
        ph.cur_key = -1.0
        for blk in range(128 // KB):
            k10 = blk * KB
            ph.cur_key = float(blk)
            bb, bb_k = b_r.next()
            for q in range(2):
                ph.dma("sp", bb[:, q * 4:(q + 1) * 4, :], Bv[:, k10 + q * 4:k10 + (q + 1) * 4, :], writes=[bb_k],
                       semkey=bb_k)
            gb, gb_k = g_r.next()
            t20 = k10 % 64
            hi = k10 // 64
            for cc in range(8):
                ph.dma("sp", gb[:, cc, :], S["GATE"][cc * 128:(cc + 1) * 128, t20 * 128:(t20 + KB) * 128],
                       writes=[gb_k], semkey=gb_k)
            xt, xt_k = xt_r.next()
            gy, gy_k = gy_r.next()
            for cc in range(8):
                for q in range(KB // 4):
                    p, p_k = px.next()
                    for kk in range(4):
                        k1 = q * 4 + kk
                        OP_mm(ph, p[:, kk * 128:(kk + 1) * 128], bb[:, k1, cc * 128:(cc + 1) * 128], r2[:], True, True,
                              [bb_k, r2_k], [p_k])
                    xw = xt[:, cc, :].rearrange("p (c k j) -> p k c j", c=2, k=KB)[:, q * 4:(q + 1) * 4, :, :]
                    OP_cp(ph, "act", xw, p[:].rearrange("p (k c j) -> p k c j", k=4, c=2), [p_k], [xt_k + f"_{cc}"])
                ph.cur_key = blk + 1.5
                p2, p2_k = pyy.next()
                p2v = p2[:].rearrange("p (k j) -> p k j", k=KB)
                OP_mm(ph, p2[:], c1[:, 0, :], xt[:, cc, 0:KB * 64], True, False, [c1_k, xt_k + f"_{cc}"], [p2_k])
                OP_mm(ph, p2[:], c1[:, 1, :], xt[:, cc, KB * 64:KB * 128], False, True, [c1_k, xt_k + f"_{cc}"], [p2_k])
                gv = gb[:, cc, :].rearrange("p (t a two) -> p t a two", t=KB, two=2)[:, :, :, hi]
                OP_tt(ph, "dve", gy[:, cc, :].rearrange("p (k j) -> p k j", k=KB), p2v, gv, ALU.mult,
                      [p2_k, gb_k], [gy_k + f"_{cc}"])
                ph.cur_key = float(blk)
            gy_toks = [gy_k + f"_{cc}" for cc in range(8)]
            ph.cur_key = blk + 2.5
            for ts in range(KB // 2):
                ht, ht_k = h_r.next()
                k1a = k10 + 2 * ts
                ph.dma("sp", ht[0:64, :], hv[k1a], writes=[ht_k], semkey=ht_k)
                ph.dma("sp", ht[64:128, :], hv[k1a + 1], writes=[ht_k], semkey=ht_k)
                tt, tt_k = t_r.next()
                for half in range(2):
                    py, py_k = po.next()
                    for cc in range(8):
                        OP_mm(ph, py[:], gy[:, cc, ts * 128:(ts + 1) * 128], w_o[:, cc, half * 512:(half + 1) * 512],
                              cc == 0, cc == 7, gy_toks + [w_o_k], [py_k])
                    OP_tt(ph, "dve", tt[:, half * 512:(half + 1) * 512], py[:], GTb[:, half * 512:(half + 1) * 512],
                          ALU.mult, [py_k, GTb_k], [tt_k])
                OP_tt(ph, "pool", tt[:], tt[:], ht[:], ALU.add, [tt_k, ht_k], [tt_k])
                if final:
                    ss, ss_k = ss_r.next()
                    OP_act(ph, junk[:], tt[:], AF.Square, [tt_k], [junk_k, ss_k], scale=1.0 / 32.0, accum=ss[:, 0:1])
                    OP_rstd(ph, ss[:, 1:2], ss[:, 0:1], ss_k)
                    OP_stt(ph, "dve", tt[:], tt[:], ss[:, 1:2], FGb[:], ALU.mult, ALU.mult, [tt_k, ss_k, FGb_k], [tt_k])
                    dsts = (ov[k1a], ov[k1a + 1])
                else:
                    dsts = (hv[k1a], hv[k1a + 1])
                ph.dma("pool", dsts[0], tt[0:64, :], reads=[tt_k], semkey=tt_k + "o")
                ph.dma("pool", dsts[1], tt[64:128, :], reads=[tt_k], semkey=tt_k + "o")
        ph.emit()

    def build(self):
        S, I = self.S, self.I
        steps = [
            lambda: self.phase_mod(),
            lambda: self.phase_mla_a(0, I["x"], I["ctx"], True),
            lambda: self.phase_mla_b(0, I["x"], I["ctx"], True),
            lambda: self.phase_f1(1, True),
            lambda: self.phase_f2(1, False),
            lambda: self.phase_mla_a(2, S["h"], S["hc"], False),
            lambda: self.phase_mla_b(2, S["h"], S["hc"], False),
            lambda: self.phase_f1(3, False),
            lambda: self.phase_f2(3, True),
        ]
        if self.inject:
            self.phase_inject()
        if self.steps is not None:
            for i in self.steps:
                steps[i]()
            return self.nc
        n = len(steps) if self.upto is None else self.upto
        for i, st in enumerate(steps[:n]):
            with self.nc.named_scope(f"step{i}"):
                st()
        return self.nc


def _tables():
    bf = ml_dtypes.bfloat16
    row = np.repeat(np.arange(128), 64).astype(np.float32)
    col = np.tile(np.arange(64), 128).astype(np.float32)
    inv = (1.0 / (10000.0 ** (np.arange(0, 32, 2, dtype=np.float32) / 32.0))).astype(np.float32)
    ar = row[:, None] * inv[None, :]
    ac = col[:, None] * inv[None, :]
    cos_full = np.concatenate([np.cos(ar), np.cos(ar), np.cos(ac), np.cos(ac)], -1).astype(np.float32)
    sin_sgn = np.concatenate([-np.sin(ar), np.sin(ar), -np.sin(ac), np.sin(ac)], -1).astype(np.float32)
    rope_k = np.ascontiguousarray(np.stack([cos_full, sin_sgn], 1))
    rope_q = np.ascontiguousarray(np.stack([cos_full.T, sin_sgn.T], 0))
    ident = np.eye(128, dtype=np.float32).astype(bf)
    n128 = np.arange(128, dtype=np.float64)
    a1 = 2 * np.pi * np.outer(n128, n128) / 128.0
    sc = 1.0 / math.sqrt(128.0)
    dft_t1 = np.stack([np.cos(a1) * sc, np.sin(a1) * sc], 1).astype(np.float32).astype(bf)
    atw = 2 * np.pi * np.outer(n128, np.arange(64)) / 8192.0
    dft_tw = np.ascontiguousarray(np.stack([np.cos(atw), np.sin(atw)], 1).astype(np.float32))
    n64 = np.arange(64, dtype=np.float64)
    a2 = 2 * np.pi * np.outer(n64, n64) / 64.0
    C2, S2 = np.cos(a2) / 8.0, np.sin(a2) / 8.0
    dft_r2 = np.block([[C2, S2], [-S2, C2]]).astype(np.float32).astype(bf)
    dft_c1 = np.stack([np.cos(a1) * sc, -np.sin(a1) * sc], 1).astype(np.float32).astype(bf)
    n256 = np.arange(256, dtype=np.float64)
    a3 = 2 * np.pi * np.outer(n256, n256) / 256.0
    CS = np.concatenate([np.cos(a3), np.sin(a3)], 1) / 16.0
    dft_ctx = np.ascontiguousarray(CS.reshape(2, 128, 512).transpose(1, 0, 2)).astype(np.float32).astype(bf)
    return dict(rope_k=rope_k, rope_q=rope_q, ident=ident, dft_t1=dft_t1, dft_tw=dft_tw, dft_r2=dft_r2,
                dft_c1=dft_c1, dft_ctx=dft_ctx)


CORE_BATCH = {0: 0, 1: 1, 4: 2, 5: 3}


def make_in_maps(inputs, cores):
    f = lambda a: np.ascontiguousarray(np.asarray(a, dtype=np.float32))
    tabs = _tables()
    w_qup = f(inputs["mla_w_qup"])
    perm = np.array([(d + 16) if (d % 32) < 16 else (d - 16) for d in range(64)])
    w_qup_sw = np.stack([np.concatenate([w_qup[i][:, h * 192 + 128 + perm] for h in range(8)], 1) for i in range(2)])
    w_kvup = f(inputs["mla_w_kvup"])
    w_ukT = np.stack([np.stack([w_kvup[i][:, h * 256:h * 256 + 128].T for h in range(8)]) for i in range(2)])
    shared = dict(
        norm_g=f(inputs["norm_g"]), w_ada=f(inputs["w_ada"]), b_ada=f(inputs["b_ada"]),
        mla_w_in=f(inputs["mla_w_in"]), mla_g_qa=f(inputs["mla_g_qa"]), mla_w_qup=w_qup,
        mla_w_qup_sw=np.ascontiguousarray(w_qup_sw), mla_g_kva=f(inputs["mla_g_kva"]), mla_w_kvup=w_kvup,
        mla_w_ukT=np.ascontiguousarray(w_ukT), mla_w_out=f(inputs["mla_w_out"]), fno_w_in=f(inputs["fno_w_in"]),
        fno_w_out=f(inputs["fno_w_out"]), final_g=f(inputs["final_g"]).reshape(1, D), **tabs)
    x, c, ctx, c_ctx = f(inputs["x"]), f(inputs["c"]), f(inputs["ctx"]), f(inputs["c_ctx"])
    maps = []
    zero = None
    for core in cores:
        b = CORE_BATCH.get(core)
        if b is None:
            if zero is None:
                zero = {k: (v if k in tabs else np.zeros_like(v)) for k, v in maps[0].items()}
            maps.append(zero)
            continue
        cv = np.stack([c[b], c_ctx], 0)
        cvecT = np.ascontiguousarray(cv.reshape(2, 8, 128).transpose(2, 1, 0))
        m = dict(shared)
        m.update(x=x[b], ctx=ctx[b], cvecT=cvecT)
        maps.append(m)
    return maps


def kernel(**inputs):
    nc = Builder().build()
    cores = list(range(8))
    in_maps = make_in_maps(inputs, cores)
    res = run_bass_kernel_spmd(nc, in_maps, core_ids=cores)
    inv = {b: c for c, b in CORE_BATCH.items()}
    out = np.stack([np.asarray(res.results[inv[b]]["out"]) for b in range(4)], 0)
    return out.astype(np.float32)
```

```python
import contextlib
import math
import numpy as np
import ml_dtypes
import concourse.bass as bass
import concourse.mybir as mybir
from concourse.bass_utils import run_bass_kernel_spmd

F32 = mybir.dt.float32
BF16 = mybir.dt.bfloat16
AF = mybir.ActivationFunctionType
ALU = mybir.AluOpType

T = 8192
D = 1024
NCTX = 256
NK = T + NCTX
NKC = NK // 128
EPS = 1e-6
SCALE = 1.0 / math.sqrt(192.0)


class Op:
    __slots__ = ("eng", "fn", "deps", "dma", "semkey", "signal", "sem", "val", "key", "idx", "impl")

    def __init__(self, eng, fn, dma, semkey):
        self.eng = eng
        self.fn = fn
        self.deps = []
        self.dma = dma
        self.semkey = semkey
        self.signal = False
        self.sem = None
        self.val = 0
        self.key = 0.0
        self.idx = 0
        self.impl = []


class Phase:
    ENGS = ("sp", "pe", "act", "dve", "pool")

    def __init__(self, nc, name):
        self.nc = nc
        self.name = name
        self.ops = []
        self.last_writer = {}
        self.readers = {}
        self.stack = contextlib.ExitStack()
        self.n_alloc = 0
        self.cur_key = None

    def sb(self, shape, dtype, name="t"):
        self.n_alloc += 1
        t = self.stack.enter_context(self.nc.sbuf_tensor(f"{self.name}_{name}{self.n_alloc}", list(shape), dtype))
        return t, f"{name}{self.n_alloc}"

    def ps(self, shape, dtype, name="p"):
        self.n_alloc += 1
        t = self.stack.enter_context(self.nc.psum_tensor(f"{self.name}_{name}{self.n_alloc}", list(shape), dtype))
        return t, f"{name}{self.n_alloc}"

    def op(self, eng, fn, reads=(), writes=(), dma=False, semkey=None):
        o = Op(eng, fn, dma, semkey)
        o.idx = len(self.ops)
        o.key = self.cur_key if self.cur_key is not None else 0.0
        deps = []
        for t in list(reads) + list(writes):
            w = self.last_writer.get(t)
            if w is not None:
                deps.append(w)
        for t in writes:
            deps.extend(self.readers.get(t, ()))
        seen = set()
        for d in deps:
            if id(d) in seen or d is o:
                continue
            seen.add(id(d))
            if d.eng == "pe" and eng == "pe" and not d.dma and not dma:
                o.impl.append(d)
                continue
            o.deps.append(d)
            d.signal = True
        for t in writes:
            self.last_writer[t] = o
            self.readers[t] = []
        for t in reads:
            self.readers.setdefault(t, []).append(o)
        self.ops.append(o)
        return o

    def dma(self, queue, out, in_, reads=(), writes=(), semkey=None, slow=False):
        assert semkey is not None
        if slow:
            fn = lambda e: e.dma_start(out=out, in_=in_, allow_slow_non_contiguous=True)
        else:
            fn = lambda e: e.dma_start(out=out, in_=in_)
        return self.op(queue, fn, reads, writes, dma=True, semkey=semkey)

    def emit(self):
        nc = self.nc
        self.ops.sort(key=lambda o: (o.key, o.idx))
        pos = {id(o): i for i, o in enumerate(self.ops)}
        for o in self.ops:
            for d in o.deps + o.impl:
                assert pos[id(d)] < pos[id(o)], f"{self.name}: pipelining key inverts a dependency ({d.eng}->{o.eng})"
        keys = []
        for o in self.ops:
            if o.dma:
                o.signal = True
                if o.semkey not in keys:
                    keys.append(o.semkey)
        sems = {}
        for e in self.ENGS:
            sems[("eng", e)] = nc.alloc_semaphore(name=f"{self.name}_s_{e}")
        for i, k in enumerate(keys):
            sems[("dma", k)] = nc.alloc_semaphore(name=f"{self.name}_d{i}")
        counts = {k: 0 for k in sems}
        for o in self.ops:
            if not o.signal:
                continue
            k = ("dma", o.semkey) if o.dma else ("eng", o.eng)
            counts[k] += 16 if o.dma else 1
            o.sem = k
            o.val = counts[k]
        per_eng = {e: [] for e in self.ENGS}
        for o in self.ops:
            per_eng[o.eng].append(o)
        final_dma = {k: v for k, v in counts.items() if k[0] == "dma" and v > 0}

        def body(ename):
            def f(eng):
                waited = {}
                for o in per_eng[ename]:
                    need = {}
                    for d in o.deps:
                        if need.get(d.sem, 0) < d.val:
                            need[d.sem] = d.val
                    for k, v in need.items():
                        if waited.get(k, 0) >= v:
                            continue
                        eng.wait_ge(sems[k], v)
                        waited[k] = v
                    ins = o.fn(eng)
                    if o.signal:
                        ins.then_inc(sems[o.sem], 16 if o.dma else 1)
                if ename == "sp":
                    for k, v in final_dma.items():
                        eng.wait_ge(sems[k], v)
                    for e2 in ("pe", "act", "dve", "pool"):
                        v = counts[("eng", e2)]
                        if v > 0:
                            eng.wait_ge(sems[("eng", e2)], v)
            return f

        with nc.Block() as block:
            block.sync(body("sp"))
            block.tensor(body("pe"))
            block.scalar(body("act"))
            block.vector(body("dve"))
            block.gpsimd(body("pool"))
        nc.all_engine_barrier()
        nc.clear_and_free_semaphores(list(sems.values()))
        nc.all_engine_barrier()
        self.stack.close()
        return len(self.ops)


class Rot:
    def __init__(self, items):
        self.items = items
        self.i = 0

    def next(self):
        it = self.items[self.i % len(self.items)]
        self.i += 1
        return it


def rot_sb(ph, n, shape, dtype, name):
    return Rot([ph.sb(shape, dtype, name) for _ in range(n)])


def rot_ps(ph, n, shape, dtype, name):
    return Rot([ph.ps(shape, dtype, name) for _ in range(n)])


def OP_mm(ph, out, lhsT, rhs, start, stop, r, w):
    return ph.op("pe", lambda e: e.matmul(out, lhsT, rhs, start=start, stop=stop), r, w)


def OP_tr(ph, out, in_, ident, r, w):
    return ph.op("pe", lambda e: e.transpose(out, in_, ident), r, w)


def OP_act(ph, out, in_, func, r, w, bias=None, scale=None, accum=None):
    kw = {}
    if bias is not None:
        kw["bias"] = bias
    if scale is not None:
        kw["scale"] = scale
    if accum is not None:
        kw["accum_out"] = accum
    return ph.op("act", lambda e: e.activation(out, in_, func, **kw), r, w)


def OP_ts(ph, eng, out, in0, s1, s2, op0, op1, r, w):
    if op1 is None:
        return ph.op(eng, lambda e: e.tensor_scalar(out, in0, s1, None, op0), r, w)
    return ph.op(eng, lambda e: e.tensor_scalar(out, in0, s1, s2, op0, op1), r, w)


def OP_stt(ph, eng, out, in0, scalar, in1, op0, op1, r, w):
    return ph.op(eng, lambda e: e.scalar_tensor_tensor(out, in0, scalar, in1, op0, op1), r, w)


def OP_tt(ph, eng, out, in0, in1, op, r, w):
    return ph.op(eng, lambda e: e.tensor_tensor(out, in0, in1, op), r, w)


def OP_recip(ph, out, in_, r, w):
    return ph.op("dve", lambda e: e.reciprocal(out, in_), r, w)


def OP_rstd(ph, out, in_, k):
    OP_act(ph, out, in_, AF.Sqrt, [k], [k], bias=EPS)
    return OP_recip(ph, out, out, [k], [k])


def OP_cp(ph, eng, out, in_, r, w):
    if eng == "act":
        return ph.op(eng, lambda e: e.activation(out, in_, AF.Copy), r, w)
    return ph.op(eng, lambda e: e.tensor_copy(out, in_), r, w)


class Builder:
    def __init__(self, debug=False, upto=None, steps=None, inject=()):
        self.debug = debug
        self.upto = upto
        self.steps = steps
        self.inject = inject
        nc = bass.Bass("TRN2", target_bir_lowering=False)
        self.nc = nc
        self.I = {}
        self.S = {}
        din = lambda n, s, dt=F32: self.I.__setitem__(n, nc.dram_tensor(n, list(s), dt, kind="ExternalInput").ap())
        din("x", [T, D]); din("ctx", [NCTX, D]); din("cvecT", [128, 8, 2])
        din("norm_g", [4, D]); din("w_ada", [4, D, 3 * D]); din("b_ada", [4, 3 * D])
        din("mla_w_in", [2, D, 1472]); din("mla_g_qa", [2, 256]); din("mla_w_qup", [2, 256, 1536])
        din("mla_w_qup_sw", [2, 256, 512]); din("mla_g_kva", [2, 128]); din("mla_w_kvup", [2, 128, 2048])
        din("mla_w_ukT", [2, 8, 128, 128]); din("mla_w_out", [2, D, D])
        din("fno_w_in", [2, D, 2 * D]); din("fno_w_out", [2, D, D]); din("final_g", [1, D])
        din("rope_k", [T, 2, 64]); din("rope_q", [2, 64, T])
        din("ident", [128, 128], BF16); din("dft_t1", [128, 2, 128], BF16); din("dft_tw", [128, 2, 64])
        din("dft_r2", [128, 128], BF16); din("dft_c1", [128, 2, 128], BF16); din("dft_ctx", [128, 2, 512], BF16)
        self.out = nc.dram_tensor("out", [T, D], F32, kind="ExternalOutput").ap()
        kind = "ExternalOutput" if debug else "Internal"
        dsc = lambda n, s, dt: self.S.__setitem__(n, nc.dram_tensor("s_" + n, list(s), dt, kind=kind).ap())
        dsc("h", [T, D], F32); dsc("hc", [NCTX, D], F32); dsc("modd", [4, 2, 3, D], F32)
        dsc("QA", [8, 128, NK], BF16); dsc("QR", [8, 64, NK], BF16)
        dsc("KLT", [128, NK], BF16); dsc("KRT", [64, NK], BF16); dsc("V", [128, NKC, 128], BF16)
        dsc("GATE", [D, NK], BF16); dsc("Bd", [2, 64, 128, D], BF16)

        for n in inject:
            a = self.S[n]
            self.I["inj_" + n] = nc.dram_tensor("inj_" + n, list(a.shape), a.dtype, kind="ExternalInput").ap()

    def phase_inject(self):
        ph = Phase(self.nc, "inj")
        for n in self.inject:
            ph.dma("sp", self.S[n], self.I["inj_" + n], semkey="inj_" + n)
        ph.emit()

    def phase_mod(self):
        nc, I, S = self.nc, self.I, self.S
        ph = Phase(nc, "p0")
        actT, actT_k = ph.sb([128, 8, 2], F32, "actT")
        ph.dma("sp", actT[:], I["cvecT"], writes=[actT_k], semkey=actT_k)
        OP_act(ph, actT[:], actT[:], AF.Silu, [actT_k], [actT_k])
        wrot = rot_sb(ph, 3, [128, 3 * D], F32, "wada")
        psb = [ph.ps([128, 512], F32, "pm") for _ in range(6)]
        for i in range(4):
            bt, bt_k = ph.sb([2, 3 * D], F32, "bada")
            gt, gt_k = ph.sb([2, D], F32, "ng")
            for s in range(2):
                ph.dma("sp", bt[s:s + 1, :], I["b_ada"][i:i + 1, :], writes=[bt_k], semkey=bt_k)
                ph.dma("sp", gt[s:s + 1, :], I["norm_g"][i:i + 1, :], writes=[gt_k], semkey=gt_k)
            for c in range(8):
                wt, wt_k = wrot.next()
                ph.dma("sp", wt[:], I["w_ada"][i, c * 128:(c + 1) * 128, :], writes=[wt_k], semkey=wt_k)
                for n in range(6):
                    OP_mm(ph, psb[n][0][0:2, :], actT[:, c, :], wt[:, n * 512:(n + 1) * 512], c == 0, c == 7,
                          [actT_k, wt_k], [psb[n][1]])
            md, md_k = ph.sb([2, 3 * D], F32, "mod")
            for n in range(6):
                OP_tt(ph, "dve", md[:, n * 512:(n + 1) * 512], psb[n][0][0:2, :], bt[:, n * 512:(n + 1) * 512],
                      ALU.add, [psb[n][1], bt_k], [md_k])
            OP_stt(ph, "dve", md[:, D:2 * D], md[:, D:2 * D], 1.0, gt[:], ALU.add, ALU.mult, [md_k, gt_k], [md_k])
            for s in range(2):
                ph.dma("sp", S["modd"][i, s:s + 1, 0, :], md[s:s + 1, D:2 * D], reads=[md_k], semkey=md_k + "o")
                ph.dma("sp", S["modd"][i, s:s + 1, 1, :], md[s:s + 1, 0:D], reads=[md_k], semkey=md_k + "o")
                ph.dma("sp", S["modd"][i, s:s + 1, 2, :], md[s:s + 1, 2 * D:3 * D], reads=[md_k], semkey=md_k + "o")
        ph.emit()

    def load_bcast(self, ph, dram_row_ap, n, name):
        t, k = ph.sb([128, n], F32, name)
        ph.dma("sp", t[:], dram_row_ap.partition_broadcast(128), writes=[k], semkey=k)
        return t, k

    def load_w_bf16(self, ph, dst, dst_k, src, stage):
        st, st_k = stage.next()
        n = src.shape[-1]
        ph.dma("sp", st[:, 0:n], src, writes=[st_k], semkey=st_k)
        OP_cp(ph, "pool", dst, st[:, 0:n], [st_k], [dst_k])

    def u_tile(self, ph, R, src_ap, G1b, SHb, mod_keys, uT, uT_k, slot, ident, ident_k, key2=None):
        xt, xt_k = R["x"].next()
        if isinstance(src_ap, (list, tuple)):
            hp = 128 // len(src_ap)
            for j, a in enumerate(src_ap):
                ph.dma("sp", xt[j * hp:(j + 1) * hp, :], a, writes=[xt_k], semkey=xt_k)
        else:
            ph.dma("sp", xt[:], src_ap, writes=[xt_k], semkey=xt_k)
        junk, junk_k = R["junk"].next()
        ss, ss_k = R["ss"].next()
        OP_act(ph, junk[:], xt[:], AF.Square, [xt_k], [junk_k, ss_k], scale=1.0 / 32.0, accum=ss[:, 0:1])
        OP_rstd(ph, ss[:, 1:2], ss[:, 0:1], ss_k)
        un, un_k = R["un"].next()
        OP_stt(ph, "dve", un[:], xt[:], ss[:, 1:2], G1b[:], ALU.mult, ALU.mult, [xt_k, ss_k, mod_keys[0]], [un_k])
        ub, ub_k = R["ub"].next()
        OP_tt(ph, "pool", ub[:], un[:], SHb[:], ALU.add, [un_k, mod_keys[1]], [ub_k])
        if key2 is not None:
            ph.cur_key = key2
        pT, pT_k = R["pT"].next()
        for c in range(8):
            OP_tr(ph, pT[:, c * 128:(c + 1) * 128], ub[:, c * 128:(c + 1) * 128], ident[:], [ub_k, ident_k], [pT_k])
        OP_cp(ph, "act", uT[:, :, slot * 128:(slot + 1) * 128], pT[:].rearrange("p (c t) -> p c t", c=8),
              [pT_k], [uT_k + f"_{slot}"])
        return xt, xt_k

    def phase_mla_a(self, li, src_lat, src_ctx, ctx_q):
        nc, I, S = self.nc, self.I, self.S
        idx = li // 2
        ph = Phase(nc, f"a{li}")
        ident, ident_k = ph.sb([128, 128], BF16, "ident")
        ph.dma("sp", ident[:], I["ident"], writes=[ident_k], semkey=ident_k)
        mods = {}
        for s in range(2):
            mods[s] = (self.load_bcast(ph, S["modd"][li, s:s + 1, 0, :], D, "g1b"),
                       self.load_bcast(ph, S["modd"][li, s:s + 1, 1, :], D, "shb"))
        gqa, gqa_k = self.load_bcast(ph, I["mla_g_qa"][idx:idx + 1, :], 256, "gqa")
        gkva, gkva_k = self.load_bcast(ph, I["mla_g_kva"][idx:idx + 1, :], 128, "gkva")
        stage = rot_sb(ph, 2, [128, 1536], F32, "wst")
        w_in, w_in_k = ph.sb([128, 8, 1472], BF16, "w_in")
        for c in range(8):
            self.load_w_bf16(ph, w_in[:, c, :], w_in_k, I["mla_w_in"][idx, c * 128:(c + 1) * 128, :], stage)
        w_q, w_q_k = ph.sb([128, 2, 1536], BF16, "w_q")
        w_qs, w_qs_k = ph.sb([128, 2, 512], BF16, "w_qs")
        for c in range(2):
            self.load_w_bf16(ph, w_q[:, c, :], w_q_k, I["mla_w_qup"][idx, c * 128:(c + 1) * 128, :], stage)
            self.load_w_bf16(ph, w_qs[:, c, :], w_qs_k, I["mla_w_qup_sw"][idx, c * 128:(c + 1) * 128, :], stage)
        w_uk, w_uk_k = ph.sb([128, 8, 128], BF16, "w_uk")
        for h in range(8):
            self.load_w_bf16(ph, w_uk[:, h, :], w_uk_k, I["mla_w_ukT"][idx, h], stage)

        R = {"x": rot_sb(ph, 3, [128, D], F32, "x"), "junk": rot_sb(ph, 1, [128, D], BF16, "junk"),
             "ss": rot_sb(ph, 4, [128, 8], F32, "ss"), "un": rot_sb(ph, 3, [128, D], F32, "un"),
             "ub": rot_sb(ph, 3, [128, D], BF16, "ub"), "pT": rot_ps(ph, 2, [128, D], BF16, "pT")}
        ss2_r = rot_sb(ph, 3, [128, 8], F32, "ss2")
        junk2_r = rot_sb(ph, 1, [128, 512], BF16, "junk2")
        uTs = rot_sb(ph, 2, [128, 8, 512], BF16, "uT")
        cqTs = rot_sb(ph, 2, [128, 2, 512], BF16, "cqT")
        psm = rot_ps(ph, 1, [128, 512], F32, "psm")
        pT2 = rot_ps(ph, 1, [128, 512], BF16, "pT2")
        pbig = rot_ps(ph, 4, [128, 512], F32, "pbig")
        rk = rot_sb(ph, 2, [128, 2, 64], F32, "rk")
        sm = rot_sb(ph, 2, [128, 512], BF16, "sm")
        tmpk = rot_sb(ph, 2, [128, 3, 64], F32, "tmpk")
        kst = rot_sb(ph, 2, [128, 2, 128], BF16, "kst")
        gst = rot_sb(ph, 3, [128, 512], BF16, "gst")
        qn_r = rot_sb(ph, 2, [128, 512], BF16, "qn")
        qa_r = rot_sb(ph, 3, [128, 512], BF16, "qa")
        qr_r = rot_sb(ph, 3, [64, 512], BF16, "qr")
        rq_r = rot_sb(ph, 2, [64, 2, 512], F32, "rq")
        t12 = rot_sb(ph, 2, [64, 2, 512], F32, "t12")

        groups = [("lat", g * 512, 512) for g in range(T // 512)] + [("ctx", 0, NCTX)]
        tg = 0
        ph.cur_key = -1.0
        for kind, t0, ntok in groups:
            s = 0 if kind == "lat" else 1
            (G1b, G1b_k), (SHb, SHb_k) = mods[s]
            nt = ntok // 128
            uT, uT_k = uTs.next()
            cqT, cqT_k = cqTs.next()
            gcol0 = t0 if kind == "lat" else T
            kcol0 = (NCTX + t0) if kind == "lat" else 0
            uT_toks = [uT_k + f"_{j}" for j in range(nt)]
            cq_toks = [cqT_k + f"_{j}" for j in range(nt)]
            for j in range(nt):
                r0 = t0 + j * 128
                src = (src_lat if kind == "lat" else src_ctx)[r0:r0 + 128, :]
                ph.cur_key = float(tg)
                self.u_tile(ph, R, src, G1b, SHb, (G1b_k, SHb_k), uT, uT_k, j, ident, ident_k, key2=tg + 1.5)
                ph.cur_key = tg + 1.5
                tg += 1
                pm, pm_k = psm.next()
                for c in range(8):
                    OP_mm(ph, pm[:, 0:448], uT[:, c, j * 128:(j + 1) * 128], w_in[:, c, 0:448], c == 0, c == 7,
                          [uT_toks[j], w_in_k], [pm_k])
                ss, ss_k = ss2_r.next()
                junk, junk_k = junk2_r.next()
                OP_act(ph, junk[:, 0:256], pm[:, 0:256], AF.Square, [pm_k], [junk_k, ss_k], scale=1.0 / 16.0,
                       accum=ss[:, 0:1])
                OP_act(ph, junk[:, 256:384], pm[:, 256:384], AF.Square, [pm_k], [junk_k, ss_k],
                       scale=1.0 / math.sqrt(128.0), accum=ss[:, 1:2])
                OP_rstd(ph, ss[:, 2:4], ss[:, 0:2], ss_k)
                smt, smt_k = sm.next()
                OP_stt(ph, "dve", smt[:, 0:256], pm[:, 0:256], ss[:, 2:3], gqa[:], ALU.mult, ALU.mult,
                       [pm_k, ss_k, gqa_k], [smt_k])
                OP_stt(ph, "dve", smt[:, 256:384], pm[:, 256:384], ss[:, 3:4], gkva[:], ALU.mult, ALU.mult,
                       [pm_k, ss_k, gkva_k], [smt_k])
                if kind == "lat":
                    rkt, rkt_k = rk.next()
                    ph.dma("sp", rkt[:], I["rope_k"][r0:r0 + 128], writes=[rkt_k], semkey=rkt_k)
                    tk, tk_k = tmpk.next()
                    OP_tt(ph, "dve", tk[:, 0, :], pm[:, 384:448], rkt[:, 0, :], ALU.mult, [pm_k, rkt_k], [tk_k])
                    pv = pm[:, 384:448].rearrange("p (b h d) -> p b h d", b=2, h=2)
                    sv = rkt[:, 1, :].rearrange("p (b h d) -> p b h d", b=2, h=2)
                    dv = tk[:, 1, :].rearrange("p (b h d) -> p b h d", b=2, h=2)
                    OP_tt(ph, "dve", dv[:, :, 0, :], pv[:, :, 1, :], sv[:, :, 0, :], ALU.mult, [pm_k, rkt_k], [tk_k])
                    OP_tt(ph, "dve", dv[:, :, 1, :], pv[:, :, 0, :], sv[:, :, 1, :], ALU.mult, [pm_k, rkt_k], [tk_k])
                    OP_tt(ph, "dve", smt[:, 384:448], tk[:, 0, :], tk[:, 1, :], ALU.add, [tk_k], [smt_k])
                else:
                    OP_cp(ph, "dve", smt[:, 384:448], pm[:, 384:448], [pm_k], [smt_k])
                ph.dma("pool", S["V"][:, (kcol0 + j * 128) // 128, :], smt[:, 256:384], reads=[smt_k],
                       semkey=smt_k + "v")
                p2, p2_k = pT2.next()
                OP_tr(ph, p2[:, 0:128], smt[:, 0:128], ident[:], [smt_k, ident_k], [p2_k])
                OP_tr(ph, p2[:, 128:256], smt[:, 128:256], ident[:], [smt_k, ident_k], [p2_k])
                OP_tr(ph, p2[:, 256:384], smt[:, 256:384], ident[:], [smt_k, ident_k], [p2_k])
                OP_tr(ph, p2[0:64, 384:512], smt[:, 384:448], ident[:], [smt_k, ident_k], [p2_k])
                OP_cp(ph, "act", cqT[:, :, j * 128:(j + 1) * 128], p2[:, 0:256].rearrange("p (c t) -> p c t", c=2),
                      [p2_k], [cq_toks[j]])
                ks, ks_k = kst.next()
                OP_cp(ph, "act", ks[:, 0, :], p2[:, 256:384], [p2_k], [ks_k])
                OP_cp(ph, "act", ks[0:64, 1, :], p2[0:64, 384:512], [p2_k], [ks_k])
                ph.dma("pool", S["KLT"][:, kcol0 + j * 128:kcol0 + (j + 1) * 128], ks[:, 0, :], reads=[ks_k],
                       semkey=ks_k + "o")
                ph.dma("pool", S["KRT"][:, kcol0 + j * 128:kcol0 + (j + 1) * 128], ks[0:64, 1, :], reads=[ks_k],
                       semkey=ks_k + "o")
            ph.cur_key = tg - 1 + 3.5
            for n in range(8):
                pb, pb_k = pbig.next()
                for c in range(8):
                    OP_mm(ph, pb[:, 0:ntok], w_in[:, c, 448 + n * 128:448 + (n + 1) * 128], uT[:, c, 0:ntok],
                          c == 0, c == 7, uT_toks + [w_in_k], [pb_k])
                g, g_k = gst.next()
                OP_act(ph, g[:, 0:ntok], pb[:, 0:ntok], AF.Silu, [pb_k], [g_k])
                ph.dma("pool", S["GATE"][n * 128:(n + 1) * 128, gcol0:gcol0 + ntok], g[:, 0:ntok], reads=[g_k],
                       semkey=g_k + "o")
            if kind == "ctx" and not ctx_q:
                continue
            if kind == "lat":
                rq, rq_k = rq_r.next()
                ph.dma("sp", rq[:, :, 0:ntok], I["rope_q"][:, :, t0:t0 + ntok].rearrange("a d t -> d a t"),
                       writes=[rq_k], semkey=rq_k)
            for h in range(8):
                pb, pb_k = pbig.next()
                for c in range(2):
                    OP_mm(ph, pb[:, 0:ntok], w_q[:, c, h * 192:h * 192 + 128], cqT[:, c, 0:ntok], c == 0, c == 1,
                          cq_toks + [w_q_k], [pb_k])
                qn, qn_k = qn_r.next()
                OP_cp(ph, "act", qn[:, 0:ntok], pb[:, 0:ntok], [pb_k], [qn_k])
                pr, pr_k = pbig.next()
                for c in range(2):
                    OP_mm(ph, pr[0:64, 0:ntok], w_q[:, c, h * 192 + 128:h * 192 + 192], cqT[:, c, 0:ntok], c == 0,
                          c == 1, cq_toks + [w_q_k], [pr_k])
                qr, qr_k = qr_r.next()
                if kind == "lat":
                    pss, pss_k = pbig.next()
                    for c in range(2):
                        OP_mm(ph, pss[0:64, 0:ntok], w_qs[:, c, h * 64:(h + 1) * 64], cqT[:, c, 0:ntok], c == 0,
                              c == 1, cq_toks + [w_qs_k], [pss_k])
                    tt, tt_k = t12.next()
                    OP_tt(ph, "dve", tt[:, 0, 0:ntok], pr[0:64, 0:ntok], rq[:, 0, 0:ntok], ALU.mult, [pr_k, rq_k],
                          [tt_k])
                    OP_tt(ph, "dve", tt[:, 1, 0:ntok], pss[0:64, 0:ntok], rq[:, 1, 0:ntok], ALU.mult,
                          [pss_k, rq_k], [tt_k])
                    OP_tt(ph, "pool", qr[:, 0:ntok], tt[:, 0, 0:ntok], tt[:, 1, 0:ntok], ALU.add, [tt_k], [qr_k])
                else:
                    OP_cp(ph, "dve", qr[:, 0:ntok], pr[0:64, 0:ntok], [pr_k], [qr_k])
                ph.dma("pool", S["QR"][h, :, gcol0:gcol0 + ntok], qr[:, 0:ntok], reads=[qr_k], semkey=qr_k + "o")
                pb2, pb2_k = pbig.next()
                OP_mm(ph, pb2[:, 0:ntok], w_uk[:, h, :], qn[:, 0:ntok], True, True, [qn_k, w_uk_k], [pb2_k])
                qa, qa_k = qa_r.next()
                OP_cp(ph, "dve", qa[:, 0:ntok], pb2[:, 0:ntok], [pb2_k], [qa_k])
                ph.dma("pool", S["QA"][h, :, gcol0:gcol0 + ntok], qa[:, 0:ntok], reads=[qa_k], semkey=qa_k + "o")
        ph.emit()

    def phase_mla_b(self, li, src_lat, src_ctx, ctx_q):
        nc, I, S = self.nc, self.I, self.S
        idx = li // 2
        ph = Phase(nc, f"b{li}")
        KLT, KLT_k = ph.sb([128, NK], BF16, "KLT")
        KRT, KRT_k = ph.sb([128, NK], BF16, "KRT")
        V, V_k = ph.sb([128, NKC, 128], BF16, "V")
        ph.op("pool", lambda e: e.memset(KRT[64:128, :], 0.0), [], [KRT_k])
        for q4 in range(4):
            c0, c1 = q4 * (NK // 4), (q4 + 1) * (NK // 4)
            ph.dma("sp", KLT[:, c0:c1], S["KLT"][:, c0:c1], writes=[KLT_k], semkey=KLT_k)
        ph.dma("sp", KRT[0:64, :], S["KRT"], writes=[KRT_k], semkey=KRT_k)
        for q4 in range(3):
            ph.dma("sp", V[:, q4 * 22:(q4 + 1) * 22, :], S["V"][:, q4 * 22:(q4 + 1) * 22, :], writes=[V_k], semkey=V_k)
        ones, ones_k = ph.sb([128, 128], BF16, "ones")
        ph.op("pool", lambda e: e.memset(ones[:], 1.0), [], [ones_k])
        stage = rot_sb(ph, 2, [128, 1024], F32, "wst")
        w_uv, w_uv_k = ph.sb([128, 8, 128], BF16, "w_uv")
        w_o, w_o_k = ph.sb([128, 8, D], BF16, "w_o")
        for h in range(8):
            self.load_w_bf16(ph, w_uv[:, h, :], w_uv_k, I["mla_w_kvup"][idx, :, h * 256 + 128:h * 256 + 256], stage)
            self.load_w_bf16(ph, w_o[:, h, :], w_o_k, I["mla_w_out"][idx, h * 128:(h + 1) * 128, :], stage)
        GT = {}
        for s in range(2):
            GT[s] = self.load_bcast(ph, S["modd"][li, s:s + 1, 2, :], D, "gtb")

        qa_r = rot_sb(ph, 2, [128, 512], BF16, "qa")
        qr_r = rot_sb(ph, 2, [128, 512], BF16, "qr")
        for qr_t, qr_tk in qr_r.items:
            ph.op("pool", (lambda t: (lambda e: e.memset(t[64:128, :], 0.0)))(qr_t), [], [qr_tk])
        gt_r = rot_sb(ph, 4, [128, 512], BF16, "gate")
        ps_s = rot_ps(ph, 3, [128, 1024], F32, "pss")
        ps_o = rot_ps(ph, 1, [128, 512], F32, "pso")
        ps_u = rot_ps(ph, 1, [128, 512], F32, "psu")
        ps_l = ps_u
        posb_r = rot_sb(ph, 2, [128, 512], F32, "posb")
        pt_r = rot_sb(ph, 6, [128, 1024], BF16, "pt")
        tmp_r = rot_sb(ph, 2, [128, 1024], BF16, "ptsum")
        accA_r = rot_sb(ph, 4, [128, 1024], F32, "accA")
        accB_r = rot_sb(ph, 1, [128, 8], F32, "accB")
        t2_r = rot_sb(ph, 2, [128, 512], BF16, "tot2")
        rs_r = rot_sb(ph, 2, [128, 512], F32, "rs")
        op_r = rot_sb(ph, 2, [128, 512], BF16, "op")
        go_r = rot_sb(ph, 2, [128, 8, 512], BF16, "go")
        h_r = rot_sb(ph, 2, [128, D], F32, "h")
        t_r = rot_sb(ph, 2, [128, D], F32, "t")
        LAG = 2
        pending = []

        def run_pending():
            while pending:
                f = pending.pop(0)
                if f is not None:
                    f()

        tiles = [("lat", g * 512, 512) for g in range(T // 512)]
        if ctx_q:
            tiles.append(("ctx", 0, NCTX))

        class HC:
            pass

        heads = []
        for kind, t0, ntok in tiles:
            kcs = list(range(NKC)) if kind == "lat" else [0, 1]
            tile_ctx = {}
            for h in range(8):
                hc = HC()
                hc.kind, hc.t0, hc.ntok, hc.h = kind, t0, ntok, h
                hc.s = 0 if kind == "lat" else 1
                hc.gcol0 = t0 if kind == "lat" else T
                hc.pairs = [(kcs[2 * j], kcs[2 * j + 1]) for j in range(len(kcs) // 2)]
                hc.npair = len(hc.pairs)
                hc.tile_ctx = tile_ctx
                heads.append(hc)
        jobs = [(hc, j) for hc in heads for j in range(hc.npair)]

        def head_begin(hc):
            if hc.npair < 12:
                run_pending()
            if hc.h == 0:
                hc.tile_ctx["go"] = go_r.next()
            hc.go, hc.go_k = hc.tile_ctx["go"]
            ntok = hc.ntok
            hc.qa, hc.qa_k = qa_r.next()
            hc.qr, hc.qr_k = qr_r.next()
            hc.gt, hc.gt_k = gt_r.next()
            ph.dma("sp", hc.qa[:, 0:ntok], S["QA"][hc.h, :, hc.gcol0:hc.gcol0 + ntok], writes=[hc.qa_k],
                   semkey=hc.qa_k)
            ph.dma("sp", hc.qr[0:64, 0:ntok], S["QR"][hc.h, :, hc.gcol0:hc.gcol0 + ntok], writes=[hc.qr_k],
                   semkey=hc.qr_k)
            ph.dma("sp", hc.gt[:, 0:ntok], S["GATE"][hc.h * 128:(hc.h + 1) * 128, hc.gcol0:hc.gcol0 + ntok],
                   writes=[hc.gt_k], semkey=hc.gt_k)
            hc.po, hc.po_k = ps_o.next()
            hc.accA, hc.accA_k = accA_r.next()
            hc.usedA = False
            hc.pts = []

        def emit_qk(hc, j):
            ntok = hc.ntok
            pS, pS_k = ps_s.next()
            for half, kc in enumerate(hc.pairs[j]):
                o_ap = pS[:, half * 512:half * 512 + ntok]
                OP_mm(ph, o_ap, KLT[:, kc * 128:(kc + 1) * 128], hc.qa[:, 0:ntok], True, False, [KLT_k, hc.qa_k],
                      [pS_k])
                OP_mm(ph, o_ap, KRT[:, kc * 128:(kc + 1) * 128], hc.qr[:, 0:ntok], False, True, [KRT_k, hc.qr_k],
                      [pS_k])
            pt, pt_k = pt_r.next()
            if ntok == 512:
                sv, pv = pS[:], pt[:]
            else:
                sv = pS[:].rearrange("p (a t) -> p a t", a=2)[:, :, 0:ntok]
                pv = pt[:].rearrange("p (a t) -> p a t", a=2)[:, :, 0:ntok]
            OP_act(ph, pv, sv, AF.Exp, [pS_k], [pt_k], scale=SCALE)
            hc.pts.append((pt, pt_k))

            def accum(src_ap, src_k):
                accA, accA_k = hc.accA, hc.accA_k
                av = accA[:] if ntok == 512 else accA[:].rearrange("p (a t) -> p a t", a=2)[:, :, 0:ntok]
                if not hc.usedA:
                    OP_cp(ph, "dve", av, src_ap, [src_k], [accA_k])
                else:
                    OP_tt(ph, "dve", av, av, src_ap, ALU.add, [src_k, accA_k], [accA_k])
                hc.usedA = True

            if ntok == 512 and j % 2 == 1:
                tmp, tmp_k = tmp_r.next()
                pprev, pprev_k = hc.pts[j - 1]
                OP_tt(ph, "dve", tmp[:], pprev[:], pt[:], ALU.add, [pprev_k, pt_k], [tmp_k])
                accum(tmp[:], tmp_k)
            elif j == hc.npair - 1:
                accum(pv, pt_k)

        def emit_pv(hc, jj):
            ntok = hc.ntok
            ptj, ptj_k = hc.pts[jj]
            for half, kc in enumerate(hc.pairs[jj]):
                OP_mm(ph, hc.po[:, 0:ntok], V[:, kc, :], ptj[:, half * 512:half * 512 + ntok],
                      jj == 0 and half == 0, jj == hc.npair - 1 and half == 1, [V_k, ptj_k], [hc.po_k])

        def ep_stages(hc):
            h, po, po_k, accA, accA_k = hc.h, hc.posb, hc.posb_k, hc.accA, hc.accA_k
            gt, gt_k, go, go_k, ntok = hc.gt, hc.gt_k, hc.go, hc.go_k, hc.ntok
            st = {}

            def s0():
                st["t2"] = t2_r.next()
                t2, t2_k = st["t2"]
                OP_tt(ph, "dve", t2[:, 0:ntok], accA[:, 0:ntok], accA[:, 512:512 + ntok], ALU.add, [accA_k], [t2_k])

            def s1():
                t2, t2_k = st["t2"]
                st["pl"] = ps_l.next()
                pl, pl_k = st["pl"]
                OP_mm(ph, pl[:, 0:ntok], ones[:], t2[:, 0:ntok], True, True, [ones_k, t2_k], [pl_k])

            def s2():
                pl, pl_k = st["pl"]
                st["rs"] = rs_r.next()
                rs, rs_k = st["rs"]
                OP_act(ph, rs[:, 0:ntok], pl[:, 0:ntok], AF.Ln, [pl_k], [rs_k])
                OP_act(ph, rs[:, 0:ntok], rs[:, 0:ntok], AF.Exp, [rs_k], [rs_k], scale=-1.0)

            def s2b():
                rs, rs_k = st["rs"]
                st["opn"] = op_r.next()
                opn, opn_k = st["opn"]
                OP_tt(ph, "dve", opn[:, 0:ntok], po[:, 0:ntok], rs[:, 0:ntok], ALU.mult, [po_k, rs_k], [opn_k])

            def s3():
                opn, opn_k = st["opn"]
                st["pu"] = ps_u.next()
                pu, pu_k = st["pu"]
                OP_mm(ph, pu[:, 0:ntok], w_uv[:, h, :], opn[:, 0:ntok], True, True, [w_uv_k, opn_k], [pu_k])

            def s4():
                pu, pu_k = st["pu"]
                OP_tt(ph, "dve", go[:, h, 0:ntok], pu[:, 0:ntok], gt[:, 0:ntok], ALU.mult, [pu_k, gt_k],
                      [go_k + f"_{h}"])

            return [s0, None, s1, None, s2, None, None, s2b, None, None, s3, None, None, s4, None]

        def tile_out_stage(hc, ts):
            kind, t0, go, go_k, s = hc.kind, hc.t0, hc.go, hc.go_k, hc.s
            go_toks = [go_k + f"_{h}" for h in range(8)]

            def f():
                (GTb, GTb_k) = GT[s]
                r0 = t0 + ts * 128
                ht, ht_k = h_r.next()
                src = (src_lat if kind == "lat" else src_ctx)[r0:r0 + 128, :]
                dst = (S["h"] if kind == "lat" else S["hc"])[r0:r0 + 128, :]
                ph.dma("sp", ht[:], src, writes=[ht_k], semkey=ht_k)
                tt, tt_k = t_r.next()
                for half in range(2):
                    py, py_k = ps_u.next()
                    for h in range(8):
                        OP_mm(ph, py[:], go[:, h, ts * 128:(ts + 1) * 128], w_o[:, h, half * 512:(half + 1) * 512],
                              h == 0, h == 7, go_toks + [w_o_k], [py_k])
                    OP_tt(ph, "dve", tt[:, half * 512:(half + 1) * 512], py[:], GTb[:, half * 512:(half + 1) * 512],
                          ALU.mult, [py_k, GTb_k], [tt_k])
                OP_tt(ph, "pool", tt[:], tt[:], ht[:], ALU.add, [tt_k, ht_k], [tt_k])
                ph.dma("pool", dst, tt[:], reads=[tt_k], semkey=tt_k + "o")
            return f

        for g in range(len(jobs) + LAG):
            if g < len(jobs):
                hc, j = jobs[g]
                if j == 0:
                    head_begin(hc)
                if hc.npair >= 12 and j >= 2 and pending:
                    f = pending.pop(0)
                    if f is not None:
                        f()
                emit_qk(hc, j)
            if g >= LAG:
                hc2, j2 = jobs[g - LAG]
                emit_pv(hc2, j2)
                if j2 == hc2.npair - 1:
                    hc2.posb, hc2.posb_k = posb_r.next()
                    OP_cp(ph, "dve", hc2.posb[:, 0:hc2.ntok], hc2.po[:, 0:hc2.ntok], [hc2.po_k], [hc2.posb_k])
                    pending.extend(ep_stages(hc2))
                    if hc2.h == 7:
                        for ts in range(hc2.ntok // 128):
                            pending.append(None)
                            pending.append(tile_out_stage(hc2, ts))
                    if hc2.npair < 12:
                        run_pending()
        run_pending()
        ph.emit()

    def phase_f1(self, li, do_ctx):
        nc, I, S = self.nc, self.I, self.S
        idx = li // 2
        ph = Phase(nc, f"f{li}")
        ident, ident_k = ph.sb([128, 128], BF16, "ident")
        ph.dma("sp", ident[:], I["ident"], writes=[ident_k], semkey=ident_k)
        mods = {}
        for s in range(2):
            mods[s] = (self.load_bcast(ph, S["modd"][li, s:s + 1, 0, :], D, "g1b"),
                       self.load_bcast(ph, S["modd"][li, s:s + 1, 1, :], D, "shb"))
        stage = rot_sb(ph, 2, [128, 1024], F32, "wst")
        self.f_stage = stage
        w_in, w_in_k = ph.sb([128, 8, 2048], BF16, "w_in")
        for c in range(8):
            for hf in range(2):
                self.load_w_bf16(ph, w_in[:, c, hf * 1024:(hf + 1) * 1024], w_in_k,
                                 I["fno_w_in"][idx, c * 128:(c + 1) * 128, hf * 1024:(hf + 1) * 1024], stage)
        t1m, t1m_k = ph.sb([128, 2, 128], BF16, "t1m")
        ph.dma("sp", t1m[:], I["dft_t1"], writes=[t1m_k], semkey=t1m_k)
        tw, tw_k = ph.sb([128, 2, 64], F32, "tw")
        ph.dma("sp", tw[:], I["dft_tw"], writes=[tw_k], semkey=tw_k)
        R = {"x": rot_sb(ph, 3, [128, D], F32, "x"), "junk": rot_sb(ph, 1, [128, D], BF16, "junk"),
             "ss": rot_sb(ph, 4, [128, 8], F32, "ss"), "un": rot_sb(ph, 3, [128, D], F32, "un"),
             "ub": rot_sb(ph, 3, [128, D], BF16, "ub"), "pT": rot_ps(ph, 1, [128, D], BF16, "pT")}
        uTs = rot_sb(ph, 2, [128, 8, 512], BF16, "uT")
        pz = rot_ps(ph, 1, [128, 1024], F32, "pz")
        pa = rot_ps(ph, 1, [128, 2, 1024], F32, "pa")
        pg = rot_ps(ph, 1, [128, 512], F32, "pg")
        zt_r = rot_sb(ph, 2, [128, D], BF16, "zt")
        tm_r = rot_sb(ph, 1, [128, 2, D], F32, "tm")
        b_r = rot_sb(ph, 2, [128, 2, D], BF16, "bt")
        gst = rot_sb(ph, 3, [128, 512], BF16, "gst")
        hv = S["h"].rearrange("(t1 t2) d -> t2 t1 d", t2=64)
        if do_ctx:
            zc, zc_k = ph.sb([128, 2, D], BF16, "zc")
            gc, gc_k = ph.sb([128, 8, NCTX], BF16, "gc")
        groups = [("lat", g * 4, 4) for g in range(16)] + ([("ctx", 0, 2)] if do_ctx else [])
        tg = 0
        ph.cur_key = -1.0
        for kind, j0, nt in groups:
            s = 0 if kind == "lat" else 1
            (G1b, G1b_k), (SHb, SHb_k) = mods[s]
            uT, uT_k = uTs.next()
            ntok = nt * 128
            uT_toks = [uT_k + f"_{j}" for j in range(nt)]
            for j in range(nt):
                if kind == "lat":
                    t2 = j0 + j
                    src = hv[t2]
                else:
                    src = S["hc"][j * 128:(j + 1) * 128, :]
                ph.cur_key = float(tg)
                self.u_tile(ph, R, src, G1b, SHb, (G1b_k, SHb_k), uT, uT_k, j, ident, ident_k, key2=tg + 1.5)
                ph.cur_key = tg + 1.5
                tg += 1
                z, z_k = pz.next()
                for half in range(2):
                    for c in range(8):
                        OP_mm(ph, z[:, half * 512:(half + 1) * 512], uT[:, c, j * 128:(j + 1) * 128],
                              w_in[:, c, half * 512:(half + 1) * 512], c == 0, c == 7, [uT_toks[j], w_in_k], [z_k])
                if kind == "ctx":
                    OP_cp(ph, "dve", zc[:, j, :], z[:], [z_k], [zc_k + f"_{j}"])
                    continue
                zt, zt_k = zt_r.next()
                OP_cp(ph, "dve", zt[:], z[:], [z_k], [zt_k])
                ph.cur_key = (tg - 1) + 2.6
                a, a_k = pa.next()
                for comp in range(2):
                    for half in range(2):
                        OP_mm(ph, a[:, comp, half * 512:(half + 1) * 512], t1m[:, comp, :],
                              zt[:, half * 512:(half + 1) * 512], True, True, [t1m_k, zt_k], [a_k])
                tm, tm_k = tm_r.next()
                OP_act(ph, tm[:, 0, :], a[:, 1, :], AF.Copy, [a_k, tw_k], [tm_k], scale=tw[:, 1, t2:t2 + 1])
                OP_act(ph, tm[:, 1, :], a[:, 1, :], AF.Copy, [a_k, tw_k], [tm_k], scale=tw[:, 0, t2:t2 + 1])
                bt, bt_k = b_r.next()
                OP_stt(ph, "dve", bt[:, 0, :], a[:, 0, :], tw[:, 0, t2:t2 + 1], tm[:, 0, :], ALU.mult, ALU.subtract,
                       [a_k, tw_k, tm_k], [bt_k])
                OP_stt(ph, "dve", bt[:, 1, :], a[:, 0, :], tw[:, 1, t2:t2 + 1], tm[:, 1, :], ALU.mult, ALU.add,
                       [a_k, tw_k, tm_k], [bt_k])
                for comp in range(2):
                    ph.dma("pool", S["Bd"][comp, t2], bt[:, comp, :], reads=[bt_k], semkey=bt_k + "o")
            ph.cur_key = tg - 1 + 3.5
            for n in range(8):
                g_ps, g_ps_k = pg.next()
                for c in range(8):
                    OP_mm(ph, g_ps[:, 0:ntok], w_in[:, c, D + n * 128:D + (n + 1) * 128], uT[:, c, 0:ntok], c == 0,
                          c == 7, uT_toks + [w_in_k], [g_ps_k])
                if kind == "lat":
                    g, g_k = gst.next()
                    OP_act(ph, g[:, 0:ntok], g_ps[:, 0:ntok], AF.Silu, [g_ps_k], [g_k])
                    ph.dma("pool", S["GATE"][n * 128:(n + 1) * 128, j0 * 128:j0 * 128 + ntok], g[:, 0:ntok],
                           reads=[g_k], semkey=g_k + "o")
                else:
                    OP_act(ph, gc[:, n, :], g_ps[:, 0:ntok], AF.Silu, [g_ps_k], [gc_k + f"_{n}"])
        ph.cur_key = 1e6
        if do_ctx:
            self.f_ctx_tail(ph, li, zc, [zc_k + "_0", zc_k + "_1"], gc, [gc_k + f"_{n}" for n in range(8)], pz, pa, pg)
        ph.emit()

    def f_ctx_tail(self, ph, li, zc, zc_toks, gc, gc_toks, pz, pa, pg):
        nc, I, S = self.nc, self.I, self.S
        idx = li // 2
        dc, dc_k = ph.sb([128, 2, 512], BF16, "dctx")
        ph.dma("sp", dc[:], I["dft_ctx"], writes=[dc_k], semkey=dc_k)
        c1, c1_k = ph.sb([128, 2, 128], BF16, "c1")
        ph.dma("sp", c1[:], I["dft_c1"], writes=[c1_k], semkey=c1_k)
        stage = self.f_stage
        w_o, w_o_k = ph.sb([128, 8, D], BF16, "w_o")
        for c in range(8):
            self.load_w_bf16(ph, w_o[:, c, :], w_o_k, I["fno_w_out"][idx, c * 128:(c + 1) * 128, :], stage)
        GTb, GTb_k = self.load_bcast(ph, S["modd"][li, 1:2, 2, :], D, "gtc")
        xt, xt_k = ph.sb([128, 8, 512], BF16, "xT")
        gy, gy_k = ph.sb([128, 8, NCTX], BF16, "gy")
        for g in range(8):
            p, p_k = pg.next()
            for tc in range(2):
                OP_mm(ph, p[:], zc[:, tc, g * 128:(g + 1) * 128], dc[:, tc, :], tc == 0, tc == 1,
                      zc_toks + [dc_k], [p_k])
            OP_cp(ph, "dve", xt[:, g, :], p[:], [p_k], [xt_k + f"_{g}"])
            p2, p2_k = pg.next()
            OP_mm(ph, p2[:, 0:NCTX], c1[:, 0, :], xt[:, g, 0:NCTX], True, False, [c1_k, xt_k + f"_{g}"], [p2_k])
            OP_mm(ph, p2[:, 0:NCTX], c1[:, 1, :], xt[:, g, NCTX:2 * NCTX], False, True, [c1_k, xt_k + f"_{g}"], [p2_k])
            OP_tt(ph, "dve", gy[:, g, :], p2[:, 0:NCTX], gc[:, g, :], ALU.mult, [p2_k, gc_toks[g]], [gy_k + f"_{g}"])
        gy_toks = [gy_k + f"_{g}" for g in range(8)]
        h_r = rot_sb(ph, 2, [128, D], F32, "hcx")
        t_r = rot_sb(ph, 2, [128, D], F32, "tcx")
        for ts in range(2):
            ht, ht_k = h_r.next()
            ph.dma("sp", ht[:], S["hc"][ts * 128:(ts + 1) * 128, :], writes=[ht_k], semkey=ht_k)
            tt, tt_k = t_r.next()
            for half in range(2):
                py, py_k = pg.next()
                for g in range(8):
                    OP_mm(ph, py[:], gy[:, g, ts * 128:(ts + 1) * 128], w_o[:, g, half * 512:(half + 1) * 512],
                          g == 0, g == 7, gy_toks + [w_o_k], [py_k])
                OP_tt(ph, "dve", tt[:, half * 512:(half + 1) * 512], py[:], GTb[:, half * 512:(half + 1) * 512],
                      ALU.mult, [py_k, GTb_k], [tt_k])
            OP_tt(ph, "pool", tt[:], tt[:], ht[:], ALU.add, [tt_k, ht_k], [tt_k])
            ph.dma("pool", S["hc"][ts * 128:(ts + 1) * 128, :], tt[:], reads=[tt_k], semkey=tt_k + "o")

    def phase_f2(self, li, final):
        nc, I, S = self.nc, self.I, self.S
        idx = li // 2
        ph = Phase(nc, f"g{li}")
        r2, r2_k = ph.sb([128, 128], BF16, "r2")
        ph.dma("sp", r2[:], I["dft_r2"], writes=[r2_k], semkey=r2_k)
        c1, c1_k = ph.sb([128, 2, 128], BF16, "c1")
        ph.dma("sp", c1[:], I["dft_c1"], writes=[c1_k], semkey=c1_k)
        stage = rot_sb(ph, 2, [128, 1024], F32, "wst")
        w_o, w_o_k = ph.sb([128, 8, D], BF16, "w_o")
        for c in range(8):
            self.load_w_bf16(ph, w_o[:, c, :], w_o_k, I["fno_w_out"][idx, c * 128:(c + 1) * 128, :], stage)
        GTb, GTb_k = self.load_bcast(ph, S["modd"][li, 0:1, 2, :], D, "gtb")
        if final:
            FGb, FGb_k = self.load_bcast(ph, I["final_g"], D, "fg")
        KB = 8
        b_r = rot_sb(ph, 2, [128, KB, D], BF16, "bblk")
        g_r = rot_sb(ph, 2, [128, 8, KB * 128], BF16, "gblk")
        px = rot_ps(ph, 2, [128, 512], F32, "px")
        pyy = rot_ps(ph, 2, [128, 512], F32, "pyy")
        po = rot_ps(ph, 2, [128, 512], F32, "po")
        xt_r = rot_sb(ph, 2, [128, 8, KB * 128], BF16, "xT")
        gy_r = rot_sb(ph, 2, [128, 8, KB * 64], BF16, "gy")
        h_r = rot_sb(ph, 2, [128, D], F32, "h")
        t_r = rot_sb(ph, 2, [128, D], F32, "t")
        ss_r = rot_sb(ph, 4, [128, 8], F32, "ss")
        junk, junk_k = ph.sb([128, D], BF16, "junk")
        Bv = S["Bd"].rearrange("c t k d -> (c t) k d")
        hv = S["h"].rearrange("(k2 k1) d -> k1 k2 d", k1=128)
        ov = self.out.rearrange("(k2 k1) d -> k1 k2 d", k1=128)
        ph.cur_key = -1.0
        for blk in range(128 // KB):
            k10 = blk * KB
            ph.cur_key = float(blk)
            bb, bb_k = b_r.next()
            for q in range(2):
                ph.dma("sp", bb[:, q * 4:(q + 1) * 4, :], Bv[:, k10 + q * 4:k10 + (q + 1) * 4, :], writes=[bb_k],
                       semkey=bb_k)
            gb, gb_k = g_r.next()
            t20 = k10 % 64
            hi = k10 // 64
            for cc in range(8):
                ph.dma("sp", gb[:, cc, :], S["GATE"][cc * 128:(cc + 1) * 128, t20 * 128:(t20 + KB) * 128],
                       writes=[gb_k], semkey=gb_k)
            xt, xt_k = xt_r.next()
            gy, gy_k = gy_r.next()
            for cc in range(8):
                for q in range(KB // 4):
                    p, p_k = px.next()
                    for kk in range(4):
                        k1 = q * 4 + kk
                        OP_mm(ph, p[:, kk * 128:(kk + 1) * 128], bb[:, k1, cc * 128:(cc + 1) * 128], r2[:], True, True,
                              [bb_k, r2_k], [p_k])
                    xw = xt[:, cc, :].rearrange("p (c k j) -> p k c j", c=2, k=KB)[:, q * 4:(q + 1) * 4, :, :]
                    OP_cp(ph, "act", xw, p[:].rearrange("p (k c j) -> p k c j", k=4, c=2), [p_k], [xt_k + f"_{cc}"])
                ph.cur_key = blk + 1.5
                p2, p2_k = pyy.next()
                p2v = p2[:].rearrange("p (k j) -> p k j", k=KB)
                OP_mm(ph, p2[:], c1[:, 0, :], xt[:, cc, 0:KB * 64], True, False, [c1_k, xt_k + f"_{cc}"], [p2_k])
                OP_mm(ph, p2[:], c1[:, 1, :], xt[:, cc, KB * 64:KB * 128], False, True, [c1_k, xt_k + f"_{cc}"], [p2_k])
                gv = gb[:, cc, :].rearrange("p (t a two) -> p t a two", t=KB, two=2)[:, :, :, hi]
                OP_tt(ph, "dve", gy[:, cc, :].rearrange("p (k j) -> p k j", k=KB), p2v, gv, ALU.mult,
                      [p2_k, gb_k], [gy_k + f"_{cc}"])
                ph.cur_key = float(blk)
            gy_toks = [gy_k + f"_{cc}" for cc in range(8)]
            ph.cur_key = blk + 2.5
            for ts in range(KB // 2):
                ht, ht_k = h_r.next()
                k1a = k10 + 2 * ts
                ph.dma("sp", ht[0:64, :], hv[k1a], writes=[ht_k], semkey=ht_k)
                ph.dma("sp", ht[64:128, :], hv[k1a + 1], writes=[ht_k], semkey=ht_k)
                tt, tt_k = t_r.next()
                for half in range(2):
                    py, py_k = po.next()
                    for cc in range(8):
                        OP_mm(ph, py[:], gy[:, cc, ts * 128:(ts + 1) * 128], w_o[:, cc, half * 512:(half + 1) * 512],
                              cc == 0, cc == 7, gy_toks + [w_o_k], [py_k])
                    OP_tt(ph, "dve", tt[:, half * 512:(half + 1) * 512], py[:], GTb[:, half * 512:(half + 1) * 512],
                          ALU.mult, [py_k, GTb_k], [tt_k])
                OP_tt(ph, "pool", tt[:], tt[:], ht[:], ALU.add, [tt_k, ht_k], [tt_k])
                if final:
                    ss, ss_k = ss_r.next()
                    OP_act(ph, junk[:], tt[:], AF.Square, [tt_k], [junk_k, ss_k], scale=1.0 / 32.0, accum=ss[:, 0:1])
                    OP_rstd(ph, ss[:, 1:2], ss[:, 0:1], ss_k)
                    OP_stt(ph, "dve", tt[:], tt[:], ss[:, 1:2], FGb[:], ALU.mult, ALU.mult, [tt_k, ss_k, FGb_k], [tt_k])
                    dsts = (ov[k1a], ov[k1a + 1])
                else:
                    dsts = (hv[k1a], hv[k1a + 1])
                ph.dma("pool", dsts[0], tt[0:64, :], reads=[tt_k], semkey=tt_k + "o")
                ph.dma("pool", dsts[1], tt[64:128, :], reads=[tt_k], semkey=tt_k + "o")
        ph.emit()

    def build(self):
        S, I = self.S, self.I
        steps = [
            lambda: self.phase_mod(),
            lambda: self.phase_mla_a(0, I["x"], I["ctx"], True),
            lambda: self.phase_mla_b(0, I["x"], I["ctx"], True),
            lambda: self.phase_f1(1, True),
            lambda: self.phase_f2(1, False),
            lambda: self.phase_mla_a(2, S["h"], S["hc"], False),
            lambda: self.phase_mla_b(2, S["h"], S["hc"], False),
            lambda: self.phase_f1(3, False),
            lambda: self.phase_f2(3, True),
        ]
        if self.inject:
            self.phase_inject()
        if self.steps is not None:
            for i in self.steps:
                steps[i]()
            return self.nc
        n = len(steps) if self.upto is None else self.upto
        for i, st in enumerate(steps[:n]):
            with self.nc.named_scope(f"step{i}"):
                st()
        return self.nc


def _tables():
    bf = ml_dtypes.bfloat16
    row = np.repeat(np.arange(128), 64).astype(np.float32)
    col = np.tile(np.arange(64), 128).astype(np.float32)
    inv = (1.0 / (10000.0 ** (np.arange(0, 32, 2, dtype=np.float32) / 32.0))).astype(np.float32)
    ar = row[:, None] * inv[None, :]
    ac = col[:, None] * inv[None, :]
    cos_full = np.concatenate([np.cos(ar), np.cos(ar), np.cos(ac), np.cos(ac)], -1).astype(np.float32)
    sin_sgn = np.concatenate([-np.sin(ar), np.sin(ar), -np.sin(ac), np.sin(ac)], -1).astype(np.float32)
    rope_k = np.ascontiguousarray(np.stack([cos_full, sin_sgn], 1))
    rope_q = np.ascontiguousarray(np.stack([cos_full.T, sin_sgn.T], 0))
    ident = np.eye(128, dtype=np.float32).astype(bf)
    n128 = np.arange(128, dtype=np.float64)
    a1 = 2 * np.pi * np.outer(n128, n128) / 128.0
    sc = 1.0 / math.sqrt(128.0)
    dft_t1 = np.stack([np.cos(a1) * sc, np.sin(a1) * sc], 1).astype(np.float32).astype(bf)
    atw = 2 * np.pi * np.outer(n128, np.arange(64)) / 8192.0
    dft_tw = np.ascontiguousarray(np.stack([np.cos(atw), np.sin(atw)], 1).astype(np.float32))
    n64 = np.arange(64, dtype=np.float64)
    a2 = 2 * np.pi * np.outer(n64, n64) / 64.0
    C2, S2 = np.cos(a2) / 8.0, np.sin(a2) / 8.0
    dft_r2 = np.block([[C2, S2], [-S2, C2]]).astype(np.float32).astype(bf)
    dft_c1 = np.stack([np.cos(a1) * sc, -np.sin(a1) * sc], 1).astype(np.float32).astype(bf)
    n256 = np.arange(256, dtype=np.float64)
    a3 = 2 * np.pi * np.outer(n256, n256) / 256.0
    CS = np.concatenate([np.cos(a3), np.sin(a3)], 1) / 16.0
    dft_ctx = np.ascontiguousarray(CS.reshape(2, 128, 512).transpose(1, 0, 2)).astype(np.float32).astype(bf)
    return dict(rope_k=rope_k, rope_q=rope_q, ident=ident, dft_t1=dft_t1, dft_tw=dft_tw, dft_r2=dft_r2,
                dft_c1=dft_c1, dft_ctx=dft_ctx)


CORE_BATCH = {0: 0, 1: 1, 4: 2, 5: 3}


def make_in_maps(inputs, cores):
    f = lambda a: np.ascontiguousarray(np.asarray(a, dtype=np.float32))
    tabs = _tables()
    w_qup = f(inputs["mla_w_qup"])
    perm = np.array([(d + 16) if (d % 32) < 16 else (d - 16) for d in range(64)])
    w_qup_sw = np.stack([np.concatenate([w_qup[i][:, h * 192 + 128 + perm] for h in range(8)], 1) for i in range(2)])
    w_kvup = f(inputs["mla_w_kvup"])
    w_ukT = np.stack([np.stack([w_kvup[i][:, h * 256:h * 256 + 128].T for h in range(8)]) for i in range(2)])
    shared = dict(
        norm_g=f(inputs["norm_g"]), w_ada=f(inputs["w_ada"]), b_ada=f(inputs["b_ada"]),
        mla_w_in=f(inputs["mla_w_in"]), mla_g_qa=f(inputs["mla_g_qa"]), mla_w_qup=w_qup,
        mla_w_qup_sw=np.ascontiguousarray(w_qup_sw), mla_g_kva=f(inputs["mla_g_kva"]), mla_w_kvup=w_kvup,
        mla_w_ukT=np.ascontiguousarray(w_ukT), mla_w_out=f(inputs["mla_w_out"]), fno_w_in=f(inputs["fno_w_in"]),
        fno_w_out=f(inputs["fno_w_out"]), final_g=f(inputs["final_g"]).reshape(1, D), **tabs)
    x, c, ctx, c_ctx = f(inputs["x"]), f(inputs["c"]), f(inputs["ctx"]), f(inputs["c_ctx"])
    maps = []
    zero = None
    for core in cores:
        b = CORE_BATCH.get(core)
        if b is None:
            if zero is None:
                zero = {k: (v if k in tabs else np.zeros_like(v)) for k, v in maps[0].items()}
            maps.append(zero)
            continue
        cv = np.stack([c[b], c_ctx], 0)
        cvecT = np.ascontiguousarray(cv.reshape(2, 8, 128).transpose(2, 1, 0))
        m = dict(shared)
        m.update(x=x[b], ctx=ctx[b], cvecT=cvecT)
        maps.append(m)
    return maps


def kernel(**inputs):
    nc = Builder().build()
    cores = list(range(8))
    in_maps = make_in_maps(inputs, cores)
    res = run_bass_kernel_spmd(nc, in_maps, core_ids=cores)
    inv = {b: c for c, b in CORE_BATCH.items()}
    out = np.stack([np.asarray(res.results[inv[b]]["out"]) for b in range(4)], 0)
    return out.astype(np.float32)
```

```python
import contextlib
import math
import numpy as np
import ml_dtypes
import concourse.bass as bass
import concourse.mybir as mybir
from concourse.bass_utils import run_bass_kernel_spmd

F32 = mybir.dt.float32
BF16 = mybir.dt.bfloat16
AF = mybir.ActivationFunctionType
ALU = mybir.AluOpType

T = 8192
D = 1024
NCTX = 256
NK = T + NCTX
NKC = NK // 128
EPS = 1e-6
SCALE = 1.0 / math.sqrt(192.0)


class Op:
    __slots__ = ("eng", "fn", "deps", "dma", "semkey", "signal", "sem", "val", "key", "idx", "impl")

    def __init__(self, eng, fn, dma, semkey):
        self.eng = eng
        self.fn = fn
        self.deps = []
        self.dma = dma
        self.semkey = semkey
        self.signal = False
        self.sem = None
        self.val = 0
        self.key = 0.0
        self.idx = 0
        self.impl = []


class Phase:
    ENGS = ("sp", "pe", "act", "dve", "pool")

    def __init__(self, nc, name):
        self.nc = nc
        self.name = name
        self.ops = []
        self.last_writer = {}
        self.readers = {}
        self.stack = contextlib.ExitStack()
        self.n_alloc = 0
        self.cur_key = None

    def sb(self, shape, dtype, name="t"):
        self.n_alloc += 1
        t = self.stack.enter_context(self.nc.sbuf_tensor(f"{self.name}_{name}{self.n_alloc}", list(shape), dtype))
        return t, f"{name}{self.n_alloc}"

    def ps(self, shape, dtype, name="p"):
        self.n_alloc += 1
        t = self.stack.enter_context(self.nc.psum_tensor(f"{self.name}_{name}{self.n_alloc}", list(shape), dtype))
        return t, f"{name}{self.n_alloc}"

    def op(self, eng, fn, reads=(), writes=(), dma=False, semkey=None):
        o = Op(eng, fn, dma, semkey)
        o.idx = len(self.ops)
        o.key = self.cur_key if self.cur_key is not None else 0.0
        deps = []
        for t in list(reads) + list(writes):
            w = self.last_writer.get(t)
            if w is not None:
                deps.append(w)
        for t in writes:
            deps.extend(self.readers.get(t, ()))
        seen = set()
        for d in deps:
            if id(d) in seen or d is o:
                continue
            seen.add(id(d))
            if d.eng == "pe" and eng == "pe" and not d.dma and not dma:
                o.impl.append(d)
                continue
            o.deps.append(d)
            d.signal = True
        for t in writes:
            self.last_writer[t] = o
            self.readers[t] = []
        for t in reads:
            self.readers.setdefault(t, []).append(o)
        self.ops.append(o)
        return o

    def dma(self, queue, out, in_, reads=(), writes=(), semkey=None, slow=False):
        assert semkey is not None
        if slow:
            fn = lambda e: e.dma_start(out=out, in_=in_, allow_slow_non_contiguous=True)
        else:
            fn = lambda e: e.dma_start(out=out, in_=in_)
        return self.op(queue, fn, reads, writes, dma=True, semkey=semkey)

    def emit(self):
        nc = self.nc
        self.ops.sort(key=lambda o: (o.key, o.idx))
        pos = {id(o): i for i, o in enumerate(self.ops)}
        for o in self.ops:
            for d in o.deps + o.impl:
                assert pos[id(d)] < pos[id(o)], f"{self.name}: pipelining key inverts a dependency ({d.eng}->{o.eng})"
        keys = []
        for o in self.ops:
            if o.dma:
                o.signal = True
                if o.semkey not in keys:
                    keys.append(o.semkey)
        sems = {}
        for e in self.ENGS:
            sems[("eng", e)] = nc.alloc_semaphore(name=f"{self.name}_s_{e}")
        for i, k in enumerate(keys):
            sems[("dma", k)] = nc.alloc_semaphore(name=f"{self.name}_d{i}")
        counts = {k: 0 for k in sems}
        for o in self.ops:
            if not o.signal:
                continue
            k = ("dma", o.semkey) if o.dma else ("eng", o.eng)
            counts[k] += 16 if o.dma else 1
            o.sem = k
            o.val = counts[k]
        per_eng = {e: [] for e in self.ENGS}
        for o in self.ops:
            per_eng[o.eng].append(o)
        final_dma = {k: v for k, v in counts.items() if k[0] == "dma" and v > 0}

        def body(ename):
            def f(eng):
                waited = {}
                for o in per_eng[ename]:
                    need = {}
                    for d in o.deps:
                        if need.get(d.sem, 0) < d.val:
                            need[d.sem] = d.val
                    for k, v in need.items():
                        if waited.get(k, 0) >= v:
                            continue
                        eng.wait_ge(sems[k], v)
                        waited[k] = v
                    ins = o.fn(eng)
                    if o.signal:
                        ins.then_inc(sems[o.sem], 16 if o.dma else 1)
                if ename == "sp":
                    for k, v in final_dma.items():
                        eng.wait_ge(sems[k], v)
                    for e2 in ("pe", "act", "dve", "pool"):
                        v = counts[("eng", e2)]
                        if v > 0:
                            eng.wait_ge(sems[("eng", e2)], v)
            return f

        with nc.Block() as block:
            block.sync(body("sp"))
            block.tensor(body("pe"))
            block.scalar(body("act"))
            block.vector(body("dve"))
            block.gpsimd(body("pool"))
        nc.all_engine_barrier()
        nc.clear_and_free_semaphores(list(sems.values()))
        nc.all_engine_barrier()
        self.stack.close()
        return len(self.ops)


class Rot:
    def __init__(self, items):
        self.items = items
        self.i = 0

    def next(self):
        it = self.items[self.i % len(self.items)]
        self.i += 1
        return it


def rot_sb(ph, n, shape, dtype, name):
    return Rot([ph.sb(shape, dtype, name) for _ in range(n)])


def rot_ps(ph, n, shape, dtype, name):
    return Rot([ph.ps(shape, dtype, name) for _ in range(n)])


def OP_mm(ph, out, lhsT, rhs, start, stop, r, w):
    return ph.op("pe", lambda e: e.matmul(out, lhsT, rhs, start=start, stop=stop), r, w)


def OP_tr(ph, out, in_, ident, r, w):
    return ph.op("pe", lambda e: e.transpose(out, in_, ident), r, w)


def OP_act(ph, out, in_, func, r, w, bias=None, scale=None, accum=None):
    kw = {}
    if bias is not None:
        kw["bias"] = bias
    if scale is not None:
        kw["scale"] = scale
    if accum is not None:
        kw["accum_out"] = accum
    return ph.op("act", lambda e: e.activation(out, in_, func, **kw), r, w)


def OP_ts(ph, eng, out, in0, s1, s2, op0, op1, r, w):
    if op1 is None:
        return ph.op(eng, lambda e: e.tensor_scalar(out, in0, s1, None, op0), r, w)
    return ph.op(eng, lambda e: e.tensor_scalar(out, in0, s1, s2, op0, op1), r, w)


def OP_stt(ph, eng, out, in0, scalar, in1, op0, op1, r, w):
    return ph.op(eng, lambda e: e.scalar_tensor_tensor(out, in0, scalar, in1, op0, op1), r, w)


def OP_tt(ph, eng, out, in0, in1, op, r, w):
    return ph.op(eng, lambda e: e.tensor_tensor(out, in0, in1, op), r, w)


def OP_recip(ph, out, in_, r, w):
    return ph.op("dve", lambda e: e.reciprocal(out, in_), r, w)


def OP_rstd(ph, out, in_, k):
    OP_act(ph, out, in_, AF.Sqrt, [k], [k], bias=EPS)
    return OP_recip(ph, out, out, [k], [k])


def OP_cp(ph, eng, out, in_, r, w):
    if eng == "act":
        return ph.op(eng, lambda e: e.activation(out, in_, AF.Copy), r, w)
    return ph.op(eng, lambda e: e.tensor_copy(out, in_), r, w)


class Builder:
    def __init__(self, debug=False, upto=None, steps=None, inject=()):
        self.debug = debug
        self.upto = upto
        self.steps = steps
        self.inject = inject
        nc = bass.Bass("TRN2", target_bir_lowering=False)
        self.nc = nc
        self.I = {}
        self.S = {}
        din = lambda n, s, dt=F32: self.I.__setitem__(n, nc.dram_tensor(n, list(s), dt, kind="ExternalInput").ap())
        din("x", [T, D]); din("ctx", [NCTX, D]); din("cvecT", [128, 8, 2])
        din("norm_g", [4, D]); din("w_ada", [4, D, 3 * D]); din("b_ada", [4, 3 * D])
        din("mla_w_in", [2, D, 1472]); din("mla_g_qa", [2, 256]); din("mla_w_qup", [2, 256, 1536])
        din("mla_w_qup_sw", [2, 256, 512]); din("mla_g_kva", [2, 128]); din("mla_w_kvup", [2, 128, 2048])
        din("mla_w_ukT", [2, 8, 128, 128]); din("mla_w_out", [2, D, D])
        din("fno_w_in", [2, D, 2 * D]); din("fno_w_out", [2, D, D]); din("final_g", [1, D])
        din("rope_k", [T, 2, 64]); din("rope_q", [2, 64, T])
        din("ident", [128, 128], BF16); din("dft_t1", [128, 2, 128], BF16); din("dft_tw", [128, 2, 64])
        din("dft_r2", [128, 128], BF16); din("dft_c1", [128, 2, 128], BF16); din("dft_ctx", [128, 2, 512], BF16)
        self.out = nc.dram_tensor("out", [T, D], F32, kind="ExternalOutput").ap()
        kind = "ExternalOutput" if debug else "Internal"
        dsc = lambda n, s, dt: self.S.__setitem__(n, nc.dram_tensor("s_" + n, list(s), dt, kind=kind).ap())
        dsc("h", [T, D], F32); dsc("hc", [NCTX, D], F32); dsc("modd", [4, 2, 3, D], F32)
        dsc("QA", [8, 128, NK], BF16); dsc("QR", [8, 64, NK], BF16)
        dsc("KLT", [128, NK], BF16); dsc("KRT", [64, NK], BF16); dsc("V", [128, NKC, 128], BF16)
        dsc("GATE", [D, NK], BF16); dsc("Bd", [2, 64, 128, D], BF16)

        for n in inject:
            a = self.S[n]
            self.I["inj_" + n] = nc.dram_tensor("inj_" + n, list(a.shape), a.dtype, kind="ExternalInput").ap()

    def phase_inject(self):
        ph = Phase(self.nc, "inj")
        for n in self.inject:
            ph.dma("sp", self.S[n], self.I["inj_" + n], semkey="inj_" + n)
        ph.emit()

    def phase_mod(self):
        nc, I, S = self.nc, self.I, self.S
        ph = Phase(nc, "p0")
        actT, actT_k = ph.sb([128, 8, 2], F32, "actT")
        ph.dma("sp", actT[:], I["cvecT"], writes=[actT_k], semkey=actT_k)
        OP_act(ph, actT[:], actT[:], AF.Silu, [actT_k], [actT_k])
        wrot = rot_sb(ph, 3, [128, 3 * D], F32, "wada")
        psb = [ph.ps([128, 512], F32, "pm") for _ in range(6)]
        for i in range(4):
            bt, bt_k = ph.sb([2, 3 * D], F32, "bada")
            gt, gt_k = ph.sb([2, D], F32, "ng")
            for s in range(2):
                ph.dma("sp", bt[s:s + 1, :], I["b_ada"][i:i + 1, :], writes=[bt_k], semkey=bt_k)
                ph.dma("sp", gt[s:s + 1, :], I["norm_g"][i:i + 1, :], writes=[gt_k], semkey=gt_k)
            for c in range(8):
                wt, wt_k = wrot.next()
                ph.dma("sp", wt[:], I["w_ada"][i, c * 128:(c + 1) * 128, :], writes=[wt_k], semkey=wt_k)
                for n in range(6):
                    OP_mm(ph, psb[n][0][0:2, :], actT[:, c, :], wt[:, n * 512:(n + 1) * 512], c == 0, c == 7,
                          [actT_k, wt_k], [psb[n][1]])
            md, md_k = ph.sb([2, 3 * D], F32, "mod")
            for n in range(6):
                OP_tt(ph, "dve", md[:, n * 512:(n + 1) * 512], psb[n][0][0:2, :], bt[:, n * 512:(n + 1) * 512],
                      ALU.add, [psb[n][1], bt_k], [md_k])
            OP_stt(ph, "dve", md[:, D:2 * D], md[:, D:2 * D], 1.0, gt[:], ALU.add, ALU.mult, [md_k, gt_k], [md_k])
            for s in range(2):
                ph.dma("sp", S["modd"][i, s:s + 1, 0, :], md[s:s + 1, D:2 * D], reads=[md_k], semkey=md_k + "o")
                ph.dma("sp", S["modd"][i, s:s + 1, 1, :], md[s:s + 1, 0:D], reads=[md_k], semkey=md_k + "o")
                ph.dma("sp", S["modd"][i, s:s + 1, 2, :], md[s:s + 1, 2 * D:3 * D], reads=[md_k], semkey=md_k + "o")
        ph.emit()

    def load_bcast(self, ph, dram_row_ap, n, name):
        t, k = ph.sb([128, n], F32, name)
        ph.dma("sp", t[:], dram_row_ap.partition_broadcast(128), writes=[k], semkey=k)
        return t, k

    def load_w_bf16(self, ph, dst, dst_k, src, stage):
        st, st_k = stage.next()
        n = src.shape[-1]
        ph.dma("sp", st[:, 0:n], src, writes=[st_k], semkey=st_k)
        OP_cp(ph, "pool", dst, st[:, 0:n], [st_k], [dst_k])

    def u_tile(self, ph, R, src_ap, G1b, SHb, mod_keys, uT, uT_k, slot, ident, ident_k, key2=None):
        xt, xt_k = R["x"].next()
        if isinstance(src_ap, (list, tuple)):
            hp = 128 // len(src_ap)
            for j, a in enumerate(src_ap):
                ph.dma("sp", xt[j * hp:(j + 1) * hp, :], a, writes=[xt_k], semkey=xt_k)
        else:
            ph.dma("sp", xt[:], src_ap, writes=[xt_k], semkey=xt_k)
        junk, junk_k = R["junk"].next()
        ss, ss_k = R["ss"].next()
        OP_act(ph, junk[:], xt[:], AF.Square, [xt_k], [junk_k, ss_k], scale=1.0 / 32.0, accum=ss[:, 0:1])
        OP_rstd(ph, ss[:, 1:2], ss[:, 0:1], ss_k)
        un, un_k = R["un"].next()
        OP_stt(ph, "dve", un[:], xt[:], ss[:, 1:2], G1b[:], ALU.mult, ALU.mult, [xt_k, ss_k, mod_keys[0]], [un_k])
        ub, ub_k = R["ub"].next()
        OP_tt(ph, "pool", ub[:], un[:], SHb[:], ALU.add, [un_k, mod_keys[1]], [ub_k])
        if key2 is not None:
            ph.cur_key = key2
        pT, pT_k = R["pT"].next()
        for c in range(8):
            OP_tr(ph, pT[:, c * 128:(c + 1) * 128], ub[:, c * 128:(c + 1) * 128], ident[:], [ub_k, ident_k], [pT_k])
        OP_cp(ph, "act", uT[:, :, slot * 128:(slot + 1) * 128], pT[:].rearrange("p (c t) -> p c t", c=8),
              [pT_k], [uT_k + f"_{slot}"])
        return xt, xt_k

    def phase_mla_a(self, li, src_lat, src_ctx, ctx_q):
        nc, I, S = self.nc, self.I, self.S
        idx = li // 2
        ph = Phase(nc, f"a{li}")
        ident, ident_k = ph.sb([128, 128], BF16, "ident")
        ph.dma("sp", ident[:], I["ident"], writes=[ident_k], semkey=ident_k)
        mods = {}
        for s in range(2):
            mods[s] = (self.load_bcast(ph, S["modd"][li, s:s + 1, 0, :], D, "g1b"),
                       self.load_bcast(ph, S["modd"][li, s:s + 1, 1, :], D, "shb"))
        gqa, gqa_k = self.load_bcast(ph, I["mla_g_qa"][idx:idx + 1, :], 256, "gqa")
        gkva, gkva_k = self.load_bcast(ph, I["mla_g_kva"][idx:idx + 1, :], 128, "gkva")
        stage = rot_sb(ph, 2, [128, 1536], F32, "wst")
        w_in, w_in_k = ph.sb([128, 8, 1472], BF16, "w_in")
        for c in range(8):
            self.load_w_bf16(ph, w_in[:, c, :], w_in_k, I["mla_w_in"][idx, c * 128:(c + 1) * 128, :], stage)
        w_q, w_q_k = ph.sb([128, 2, 1536], BF16, "w_q")
        w_qs, w_qs_k = ph.sb([128, 2, 512], BF16, "w_qs")
        for c in range(2):
            self.load_w_bf16(ph, w_q[:, c, :], w_q_k, I["mla_w_qup"][idx, c * 128:(c + 1) * 128, :], stage)
            self.load_w_bf16(ph, w_qs[:, c, :], w_qs_k, I["mla_w_qup_sw"][idx, c * 128:(c + 1) * 128, :], stage)
        w_uk, w_uk_k = ph.sb([128, 8, 128], BF16, "w_uk")
        for h in range(8):
            self.load_w_bf16(ph, w_uk[:, h, :], w_uk_k, I["mla_w_ukT"][idx, h], stage)

        R = {"x": rot_sb(ph, 3, [128, D], F32, "x"), "junk": rot_sb(ph, 1, [128, D], BF16, "junk"),
             "ss": rot_sb(ph, 4, [128, 8], F32, "ss"), "un": rot_sb(ph, 3, [128, D], F32, "un"),
             "ub": rot_sb(ph, 3, [128, D], BF16, "ub"), "pT": rot_ps(ph, 2, [128, D], BF16, "pT")}
        ss2_r = rot_sb(ph, 3, [128, 8], F32, "ss2")
        junk2_r = rot_sb(ph, 1, [128, 512], BF16, "junk2")
        uTs = rot_sb(ph, 2, [128, 8, 512], BF16, "uT")
        cqTs = rot_sb(ph, 2, [128, 2, 512], BF16, "cqT")
        psm = rot_ps(ph, 1, [128, 512], F32, "psm")
        pT2 = rot_ps(ph, 1, [128, 512], BF16, "pT2")
        pbig = rot_ps(ph, 4, [128, 512], F32, "pbig")
        rk = rot_sb(ph, 2, [128, 2, 64], F32, "rk")
        sm = rot_sb(ph, 2, [128, 512], BF16, "sm")
        tmpk = rot_sb(ph, 2, [128, 3, 64], F32, "tmpk")
        kst = rot_sb(ph, 2, [128, 2, 128], BF16, "kst")
        gst = rot_sb(ph, 3, [128, 512], BF16, "gst")
        qn_r = rot_sb(ph, 2, [128, 512], BF16, "qn")
        qa_r = rot_sb(ph, 3, [128, 512], BF16, "qa")
        qr_r = rot_sb(ph, 3, [64, 512], BF16, "qr")
        rq_r = rot_sb(ph, 2, [64, 2, 512], F32, "rq")
        t12 = rot_sb(ph, 2, [64, 2, 512], F32, "t12")

        groups = [("lat", g * 512, 512) for g in range(T // 512)] + [("ctx", 0, NCTX)]
        tg = 0
        ph.cur_key = -1.0
        for kind, t0, ntok in groups:
            s = 0 if kind == "lat" else 1
            (G1b, G1b_k), (SHb, SHb_k) = mods[s]
            nt = ntok // 128
            uT, uT_k = uTs.next()
            cqT, cqT_k = cqTs.next()
            gcol0 = t0 if kind == "lat" else T
            kcol0 = (NCTX + t0) if kind == "lat" else 0
            uT_toks = [uT_k + f"_{j}" for j in range(nt)]
            cq_toks = [cqT_k + f"_{j}" for j in range(nt)]
            for j in range(nt):
                r0 = t0 + j * 128
                src = (src_lat if kind == "lat" else src_ctx)[r0:r0 + 128, :]
                ph.cur_key = float(tg)
                self.u_tile(ph, R, src, G1b, SHb, (G1b_k, SHb_k), uT, uT_k, j, ident, ident_k, key2=tg + 1.5)
                ph.cur_key = tg + 1.5
                tg += 1
                pm, pm_k = psm.next()
                for c in range(8):
                    OP_mm(ph, pm[:, 0:448], uT[:, c, j * 128:(j + 1) * 128], w_in[:, c, 0:448], c == 0, c == 7,
                          [uT_toks[j], w_in_k], [pm_k])
                ss, ss_k = ss2_r.next()
                junk, junk_k = junk2_r.next()
                OP_act(ph, junk[:, 0:256], pm[:, 0:256], AF.Square, [pm_k], [junk_k, ss_k], scale=1.0 / 16.0,
                       accum=ss[:, 0:1])
                OP_act(ph, junk[:, 256:384], pm[:, 256:384], AF.Square, [pm_k], [junk_k, ss_k],
                       scale=1.0 / math.sqrt(128.0), accum=ss[:, 1:2])
                OP_rstd(ph, ss[:, 2:4], ss[:, 0:2], ss_k)
                smt, smt_k = sm.next()
                OP_stt(ph, "dve", smt[:, 0:256], pm[:, 0:256], ss[:, 2:3], gqa[:], ALU.mult, ALU.mult,
                       [pm_k, ss_k, gqa_k], [smt_k])
                OP_stt(ph, "dve", smt[:, 256:384], pm[:, 256:384], ss[:, 3:4], gkva[:], ALU.mult, ALU.mult,
                       [pm_k, ss_k, gkva_k], [smt_k])
                if kind == "lat":
                    rkt, rkt_k = rk.next()
                    ph.dma("sp", rkt[:], I["rope_k"][r0:r0 + 128], writes=[rkt_k], semkey=rkt_k)
                    tk, tk_k = tmpk.next()
                    OP_tt(ph, "dve", tk[:, 0, :], pm[:, 384:448], rkt[:, 0, :], ALU.mult, [pm_k, rkt_k], [tk_k])
                    pv = pm[:, 384:448].rearrange("p (b h d) -> p b h d", b=2, h=2)
                    sv = rkt[:, 1, :].rearrange("p (b h d) -> p b h d", b=2, h=2)
                    dv = tk[:, 1, :].rearrange("p (b h d) -> p b h d", b=2, h=2)
                    OP_tt(ph, "dve", dv[:, :, 0, :], pv[:, :, 1, :], sv[:, :, 0, :], ALU.mult, [pm_k, rkt_k], [tk_k])
                    OP_tt(ph, "dve", dv[:, :, 1, :], pv[:, :, 0, :], sv[:, :, 1, :], ALU.mult, [pm_k, rkt_k], [tk_k])
                    OP_tt(ph, "dve", smt[:, 384:448], tk[:, 0, :], tk[:, 1, :], ALU.add, [tk_k], [smt_k])
                else:
                    OP_cp(ph, "dve", smt[:, 384:448], pm[:, 384:448], [pm_k], [smt_k])
                ph.dma("pool", S["V"][:, (kcol0 + j * 128) // 128, :], smt[:, 256:384], reads=[smt_k],
                       semkey=smt_k + "v")
                p2, p2_k = pT2.next()
                OP_tr(ph, p2[:, 0:128], smt[:, 0:128], ident[:], [smt_k, ident_k], [p2_k])
                OP_tr(ph, p2[:, 128:256], smt[:, 128:256], ident[:], [smt_k, ident_k], [p2_k])
                OP_tr(ph, p2[:, 256:384], smt[:, 256:384], ident[:], [smt_k, ident_k], [p2_k])
                OP_tr(ph, p2[0:64, 384:512], smt[:, 384:448], ident[:], [smt_k, ident_k], [p2_k])
                OP_cp(ph, "act", cqT[:, :, j * 128:(j + 1) * 128], p2[:, 0:256].rearrange("p (c t) -> p c t", c=2),
                      [p2_k], [cq_toks[j]])
                ks, ks_k = kst.next()
                OP_cp(ph, "act", ks[:, 0, :], p2[:, 256:384], [p2_k], [ks_k])
                OP_cp(ph, "act", ks[0:64, 1, :], p2[0:64, 384:512], [p2_k], [ks_k])
                ph.dma("pool", S["KLT"][:, kcol0 + j * 128:kcol0 + (j + 1) * 128], ks[:, 0, :], reads=[ks_k],
                       semkey=ks_k + "o")
                ph.dma("pool", S["KRT"][:, kcol0 + j * 128:kcol0 + (j + 1) * 128], ks[0:64, 1, :], reads=[ks_k],
                       semkey=ks_k + "o")
            ph.cur_key = tg - 1 + 3.5
            for n in range(8):
                pb, pb_k = pbig.next()
                for c in range(8):
                    OP_mm(ph, pb[:, 0:ntok], w_in[:, c, 448 + n * 128:448 + (n + 1) * 128], uT[:, c, 0:ntok],
                          c == 0, c == 7, uT_toks + [w_in_k], [pb_k])
                g, g_k = gst.next()
                OP_act(ph, g[:, 0:ntok], pb[:, 0:ntok], AF.Silu, [pb_k], [g_k])
                ph.dma("pool", S["GATE"][n * 128:(n + 1) * 128, gcol0:gcol0 + ntok], g[:, 0:ntok], reads=[g_k],
                       semkey=g_k + "o")
            if kind == "ctx" and not ctx_q:
                continue
            if kind == "lat":
                rq, rq_k = rq_r.next()
                ph.dma("sp", rq[:, :, 0:ntok], I["rope_q"][:, :, t0:t0 + ntok].rearrange("a d t -> d a t"),
                       writes=[rq_k], semkey=rq_k)
            for h in range(8):
                pb, pb_k = pbig.next()
                for c in range(2):
                    OP_mm(ph, pb[:, 0:ntok], w_q[:, c, h * 192:h * 192 + 128], cqT[:, c, 0:ntok], c == 0, c == 1,
                          cq_toks + [w_q_k], [pb_k])
                qn, qn_k = qn_r.next()
                OP_cp(ph, "act", qn[:, 0:ntok], pb[:, 0:ntok], [pb_k], [qn_k])
                pr, pr_k = pbig.next()
                for c in range(2):
                    OP_mm(ph, pr[0:64, 0:ntok], w_q[:, c, h * 192 + 128:h * 192 + 192], cqT[:, c, 0:ntok], c == 0,
                          c == 1, cq_toks + [w_q_k], [pr_k])
                qr, qr_k = qr_r.next()
                if kind == "lat":
                    pss, pss_k = pbig.next()
                    for c in range(2):
                        OP_mm(ph, pss[0:64, 0:ntok], w_qs[:, c, h * 64:(h + 1) * 64], cqT[:, c, 0:ntok], c == 0,
                              c == 1, cq_toks + [w_qs_k], [pss_k])
                    tt, tt_k = t12.next()
                    OP_tt(ph, "dve", tt[:, 0, 0:ntok], pr[0:64, 0:ntok], rq[:, 0, 0:ntok], ALU.mult, [pr_k, rq_k],
                          [tt_k])
                    OP_tt(ph, "dve", tt[:, 1, 0:ntok], pss[0:64, 0:ntok], rq[:, 1, 0:ntok], ALU.mult,
                          [pss_k, rq_k], [tt_k])
                    OP_tt(ph, "pool", qr[:, 0:ntok], tt[:, 0, 0:ntok], tt[:, 1, 0:ntok], ALU.add, [tt_k], [qr_k])
                else:
                    OP_cp(ph, "dve", qr[:, 0:ntok], pr[0:64, 0:ntok], [pr_k], [qr_k])
                ph.dma("pool", S["QR"][h, :, gcol0:gcol0 + ntok], qr[:, 0:ntok], reads=[qr_k], semkey=qr_k + "o")
                pb2, pb2_k = pbig.next()
                OP_mm(ph, pb2[:, 0:ntok], w_uk[:, h, :], qn[:, 0:ntok], True, True, [qn_k, w_uk_k], [pb2_k])
                qa, qa_k = qa_r.next()
                OP_cp(ph, "dve", qa[:, 0:ntok], pb2[:, 0:ntok], [pb2_k], [qa_k])
                ph.dma("pool", S["QA"][h, :, gcol0:gcol0 + ntok], qa[:, 0:ntok], reads=[qa_k], semkey=qa_k + "o")
        ph.emit()

    def phase_mla_b(self, li, src_lat, src_ctx, ctx_q):
        nc, I, S = self.nc, self.I, self.S
        idx = li // 2
        ph = Phase(nc, f"b{li}")
        ph.cur_key = -2.0
        KLT, KLT_k = ph.sb([128, NK], BF16, "KLT")
        KRT, KRT_k = ph.sb([128, NK], BF16, "KRT")
        V, V_k = ph.sb([128, NKC, 128], BF16, "V")
        ph.op("pool", lambda e: e.memset(KRT[64:128, :], 0.0), [], [KRT_k])
        for q4 in range(4):
            c0, c1 = q4 * (NK // 4), (q4 + 1) * (NK // 4)
            ph.dma("sp", KLT[:, c0:c1], S["KLT"][:, c0:c1], writes=[KLT_k], semkey=KLT_k)
        ph.dma("sp", KRT[0:64, :], S["KRT"], writes=[KRT_k], semkey=KRT_k)
        for q4 in range(3):
            ph.dma("sp", V[:, q4 * 22:(q4 + 1) * 22, :], S["V"][:, q4 * 22:(q4 + 1) * 22, :], writes=[V_k], semkey=V_k)
        ones, ones_k = ph.sb([128, 128], BF16, "ones")
        ph.op("pool", lambda e: e.memset(ones[:], 1.0), [], [ones_k])
        ph.cur_key = 1.5
        stage = rot_sb(ph, 2, [128, 1024], F32, "wst")
        w_uv, w_uv_k = ph.sb([128, 8, 128], BF16, "w_uv")
        w_o, w_o_k = ph.sb([128, 8, D], BF16, "w_o")
        for h in range(8):
            self.load_w_bf16(ph, w_uv[:, h, :], w_uv_k, I["mla_w_kvup"][idx, :, h * 256 + 128:h * 256 + 256], stage)
            self.load_w_bf16(ph, w_o[:, h, :], w_o_k, I["mla_w_out"][idx, h * 128:(h + 1) * 128, :], stage)
        GT = {}
        for s in range(2):
            GT[s] = self.load_bcast(ph, S["modd"][li, s:s + 1, 2, :], D, "gtb")

        qa_r = rot_sb(ph, 2, [128, 512], BF16, "qa")
        qr_r = rot_sb(ph, 2, [128, 512], BF16, "qr")
        ph.cur_key = -2.0
        for qr_t, qr_tk in qr_r.items:
            ph.op("pool", (lambda t: (lambda e: e.memset(t[64:128, :], 0.0)))(qr_t), [], [qr_tk])
        gt_r = rot_sb(ph, 4, [128, 512], BF16, "gate")
        ps_s = rot_ps(ph, 3, [128, 1024], F32, "pss")
        ps_o = rot_ps(ph, 1, [128, 512], F32, "pso")
        ps_u = rot_ps(ph, 1, [128, 512], F32, "psu")
        ps_l = ps_u
        posb_r = rot_sb(ph, 2, [128, 512], F32, "posb")
        pt_r = rot_sb(ph, 6, [128, 1024], BF16, "pt")
        tmp_r = rot_sb(ph, 2, [128, 1024], BF16, "ptsum")
        accA_r = rot_sb(ph, 4, [128, 1024], F32, "accA")
        accB_r = rot_sb(ph, 1, [128, 8], F32, "accB")
        t2_r = rot_sb(ph, 2, [128, 512], BF16, "tot2")
        rs_r = rot_sb(ph, 2, [128, 512], F32, "rs")
        op_r = rot_sb(ph, 2, [128, 512], BF16, "op")
        go_r = rot_sb(ph, 2, [128, 8, 512], BF16, "go")
        h_r = rot_sb(ph, 2, [128, D], F32, "h")
        t_r = rot_sb(ph, 2, [128, D], F32, "t")
        LAG = 2
        pending = []

        def run_pending():
            while pending:
                f = pending.pop(0)
                if f is not None:
                    f()

        tiles = [("lat", g * 512, 512) for g in range(T // 512)]
        if ctx_q:
            tiles.append(("ctx", 0, NCTX))

        class HC:
            pass

        heads = []
        for kind, t0, ntok in tiles:
            kcs = list(range(NKC)) if kind == "lat" else [0, 1]
            tile_ctx = {}
            for h in range(8):
                hc = HC()
                hc.kind, hc.t0, hc.ntok, hc.h = kind, t0, ntok, h
                hc.s = 0 if kind == "lat" else 1
                hc.gcol0 = t0 if kind == "lat" else T
                hc.pairs = [(kcs[2 * j], kcs[2 * j + 1]) for j in range(len(kcs) // 2)]
                hc.npair = len(hc.pairs)
                hc.tile_ctx = tile_ctx
                heads.append(hc)
        jobs = [(hc, j) for hc in heads for j in range(hc.npair)]

        def head_begin(hc):
            if hc.npair < 12:
                run_pending()
            if hc.h == 0:
                hc.tile_ctx["go"] = go_r.next()
            hc.go, hc.go_k = hc.tile_ctx["go"]
            ntok = hc.ntok
            hc.qa, hc.qa_k = qa_r.next()
            hc.qr, hc.qr_k = qr_r.next()
            hc.gt, hc.gt_k = gt_r.next()
            ph.dma("sp", hc.qa[:, 0:ntok], S["QA"][hc.h, :, hc.gcol0:hc.gcol0 + ntok], writes=[hc.qa_k],
                   semkey=hc.qa_k)
            ph.dma("sp", hc.qr[0:64, 0:ntok], S["QR"][hc.h, :, hc.gcol0:hc.gcol0 + ntok], writes=[hc.qr_k],
                   semkey=hc.qr_k)
            ph.dma("sp", hc.gt[:, 0:ntok], S["GATE"][hc.h * 128:(hc.h + 1) * 128, hc.gcol0:hc.gcol0 + ntok],
                   writes=[hc.gt_k], semkey=hc.gt_k)
            hc.po, hc.po_k = ps_o.next()
            hc.accA, hc.accA_k = accA_r.next()
            hc.usedA = False
            hc.pts = []

        def emit_qk(hc, j):
            ntok = hc.ntok
            pS, pS_k = ps_s.next()
            for half, kc in enumerate(hc.pairs[j]):
                o_ap = pS[:, half * 512:half * 512 + ntok]
                OP_mm(ph, o_ap, KLT[:, kc * 128:(kc + 1) * 128], hc.qa[:, 0:ntok], True, False, [KLT_k, hc.qa_k],
                      [pS_k])
                OP_mm(ph, o_ap, KRT[:, kc * 128:(kc + 1) * 128], hc.qr[:, 0:ntok], False, True, [KRT_k, hc.qr_k],
                      [pS_k])
            pt, pt_k = pt_r.next()
            if ntok == 512:
                sv, pv = pS[:], pt[:]
            else:
                sv = pS[:].rearrange("p (a t) -> p a t", a=2)[:, :, 0:ntok]
                pv = pt[:].rearrange("p (a t) -> p a t", a=2)[:, :, 0:ntok]
            OP_act(ph, pv, sv, AF.Exp, [pS_k], [pt_k], scale=SCALE)
            hc.pts.append((pt, pt_k))

            def accum(src_ap, src_k):
                accA, accA_k = hc.accA, hc.accA_k
                av = accA[:] if ntok == 512 else accA[:].rearrange("p (a t) -> p a t", a=2)[:, :, 0:ntok]
                if not hc.usedA:
                    OP_cp(ph, "dve", av, src_ap, [src_k], [accA_k])
                else:
                    OP_tt(ph, "dve", av, av, src_ap, ALU.add, [src_k, accA_k], [accA_k])
                hc.usedA = True

            if ntok == 512 and j % 2 == 1:
                tmp, tmp_k = tmp_r.next()
                pprev, pprev_k = hc.pts[j - 1]
                OP_tt(ph, "dve", tmp[:], pprev[:], pt[:], ALU.add, [pprev_k, pt_k], [tmp_k])
                accum(tmp[:], tmp_k)
            elif j == hc.npair - 1:
                accum(pv, pt_k)

        def emit_pv(hc, jj):
            ntok = hc.ntok
            ptj, ptj_k = hc.pts[jj]
            for half, kc in enumerate(hc.pairs[jj]):
                OP_mm(ph, hc.po[:, 0:ntok], V[:, kc, :], ptj[:, half * 512:half * 512 + ntok],
                      jj == 0 and half == 0, jj == hc.npair - 1 and half == 1, [V_k, ptj_k], [hc.po_k])

        def ep_stages(hc):
            h, po, po_k, accA, accA_k = hc.h, hc.posb, hc.posb_k, hc.accA, hc.accA_k
            gt, gt_k, go, go_k, ntok = hc.gt, hc.gt_k, hc.go, hc.go_k, hc.ntok
            st = {}

            def s0():
                st["t2"] = t2_r.next()
                t2, t2_k = st["t2"]
                OP_tt(ph, "dve", t2[:, 0:ntok], accA[:, 0:ntok], accA[:, 512:512 + ntok], ALU.add, [accA_k], [t2_k])

            def s1():
                t2, t2_k = st["t2"]
                st["pl"] = ps_l.next()
                pl, pl_k = st["pl"]
                OP_mm(ph, pl[:, 0:ntok], ones[:], t2[:, 0:ntok], True, True, [ones_k, t2_k], [pl_k])

            def s2():
                pl, pl_k = st["pl"]
                st["rs"] = rs_r.next()
                rs, rs_k = st["rs"]
                OP_act(ph, rs[:, 0:ntok], pl[:, 0:ntok], AF.Ln, [pl_k], [rs_k])
                OP_act(ph, rs[:, 0:ntok], rs[:, 0:ntok], AF.Exp, [rs_k], [rs_k], scale=-1.0)

            def s2b():
                rs, rs_k = st["rs"]
                st["opn"] = op_r.next()
                opn, opn_k = st["opn"]
                OP_tt(ph, "dve", opn[:, 0:ntok], po[:, 0:ntok], rs[:, 0:ntok], ALU.mult, [po_k, rs_k], [opn_k])

            def s3():
                opn, opn_k = st["opn"]
                st["pu"] = ps_u.next()
                pu, pu_k = st["pu"]
                OP_mm(ph, pu[:, 0:ntok], w_uv[:, h, :], opn[:, 0:ntok], True, True, [w_uv_k, opn_k], [pu_k])

            def s4():
                pu, pu_k = st["pu"]
                OP_tt(ph, "dve", go[:, h, 0:ntok], pu[:, 0:ntok], gt[:, 0:ntok], ALU.mult, [pu_k, gt_k],
                      [go_k + f"_{h}"])

            return [s0, None, s1, None, s2, None, None, s2b, None, None, s3, None, None, s4, None]

        def tile_out_stage(hc, ts):
            kind, t0, go, go_k, s = hc.kind, hc.t0, hc.go, hc.go_k, hc.s
            go_toks = [go_k + f"_{h}" for h in range(8)]

            def f():
                (GTb, GTb_k) = GT[s]
                r0 = t0 + ts * 128
                ht, ht_k = h_r.next()
                src = (src_lat if kind == "lat" else src_ctx)[r0:r0 + 128, :]
                dst = (S["h"] if kind == "lat" else S["hc"])[r0:r0 + 128, :]
                ph.dma("sp", ht[:], src, writes=[ht_k], semkey=ht_k)
                tt, tt_k = t_r.next()
                for half in range(2):
                    py, py_k = ps_u.next()
                    for h in range(8):
                        OP_mm(ph, py[:], go[:, h, ts * 128:(ts + 1) * 128], w_o[:, h, half * 512:(half + 1) * 512],
                              h == 0, h == 7, go_toks + [w_o_k], [py_k])
                    OP_tt(ph, "dve", tt[:, half * 512:(half + 1) * 512], py[:], GTb[:, half * 512:(half + 1) * 512],
                          ALU.mult, [py_k, GTb_k], [tt_k])
                OP_tt(ph, "pool", tt[:], tt[:], ht[:], ALU.add, [tt_k, ht_k], [tt_k])
                ph.dma("pool", dst, tt[:], reads=[tt_k], semkey=tt_k + "o")
            return f

        for g in range(len(jobs) + LAG):
            ph.cur_key = float(g)
            if g < len(jobs):
                hc, j = jobs[g]
                if j == 0:
                    head_begin(hc)
                if hc.npair >= 12 and j >= 2 and pending:
                    f = pending.pop(0)
                    if f is not None:
                        f()
                emit_qk(hc, j)
            if g >= LAG:
                hc2, j2 = jobs[g - LAG]
                emit_pv(hc2, j2)
                if j2 == hc2.npair - 1:
                    hc2.posb, hc2.posb_k = posb_r.next()
                    OP_cp(ph, "dve", hc2.posb[:, 0:hc2.ntok], hc2.po[:, 0:hc2.ntok], [hc2.po_k], [hc2.posb_k])
                    pending.extend(ep_stages(hc2))
                    if hc2.h == 7:
                        for ts in range(hc2.ntok // 128):
                            pending.append(None)
                            pending.append(tile_out_stage(hc2, ts))
                    if hc2.npair < 12:
                        run_pending()
        run_pending()
        ph.emit()

    def phase_f1(self, li, do_ctx):
        nc, I, S = self.nc, self.I, self.S
        idx = li // 2
        ph = Phase(nc, f"f{li}")
        ident, ident_k = ph.sb([128, 128], BF16, "ident")
        ph.dma("sp", ident[:], I["ident"], writes=[ident_k], semkey=ident_k)
        mods = {}
        for s in range(2):
            mods[s] = (self.load_bcast(ph, S["modd"][li, s:s + 1, 0, :], D, "g1b"),
                       self.load_bcast(ph, S["modd"][li, s:s + 1, 1, :], D, "shb"))
        stage = rot_sb(ph, 2, [128, 1024], F32, "wst")
        self.f_stage = stage
        w_in, w_in_k = ph.sb([128, 8, 2048], BF16, "w_in")
        for c in range(8):
            for hf in range(2):
                self.load_w_bf16(ph, w_in[:, c, hf * 1024:(hf + 1) * 1024], w_in_k,
                                 I["fno_w_in"][idx, c * 128:(c + 1) * 128, hf * 1024:(hf + 1) * 1024], stage)
        t1m, t1m_k = ph.sb([128, 2, 128], BF16, "t1m")
        ph.dma("sp", t1m[:], I["dft_t1"], writes=[t1m_k], semkey=t1m_k)
        tw, tw_k = ph.sb([128, 2, 64], F32, "tw")
        ph.dma("sp", tw[:], I["dft_tw"], writes=[tw_k], semkey=tw_k)
        R = {"x": rot_sb(ph, 3, [128, D], F32, "x"), "junk": rot_sb(ph, 1, [128, D], BF16, "junk"),
             "ss": rot_sb(ph, 4, [128, 8], F32, "ss"), "un": rot_sb(ph, 3, [128, D], F32, "un"),
             "ub": rot_sb(ph, 3, [128, D], BF16, "ub"), "pT": rot_ps(ph, 1, [128, D], BF16, "pT")}
        uTs = rot_sb(ph, 2, [128, 8, 512], BF16, "uT")
        pz = rot_ps(ph, 1, [128, 1024], F32, "pz")
        pa = rot_ps(ph, 1, [128, 2, 1024], F32, "pa")
        pg = rot_ps(ph, 1, [128, 512], F32, "pg")
        zt_r = rot_sb(ph, 2, [128, D], BF16, "zt")
        tm_r = rot_sb(ph, 1, [128, 2, D], F32, "tm")
        b_r = rot_sb(ph, 2, [128, 2, D], BF16, "bt")
        gst = rot_sb(ph, 3, [128, 512], BF16, "gst")
        hv = S["h"].rearrange("(t1 t2) d -> t2 t1 d", t2=64)
        if do_ctx:
            zc, zc_k = ph.sb([128, 2, D], BF16, "zc")
            gc, gc_k = ph.sb([128, 8, NCTX], BF16, "gc")
        groups = [("lat", g * 4, 4) for g in range(16)] + ([("ctx", 0, 2)] if do_ctx else [])
        tg = 0
        ph.cur_key = -1.0
        for kind, j0, nt in groups:
            s = 0 if kind == "lat" else 1
            (G1b, G1b_k), (SHb, SHb_k) = mods[s]
            uT, uT_k = uTs.next()
            ntok = nt * 128
            uT_toks = [uT_k + f"_{j}" for j in range(nt)]
            for j in range(nt):
                if kind == "lat":
                    t2 = j0 + j
                    src = hv[t2]
                else:
                    src = S["hc"][j * 128:(j + 1) * 128, :]
                ph.cur_key = float(tg)
                self.u_tile(ph, R, src, G1b, SHb, (G1b_k, SHb_k), uT, uT_k, j, ident, ident_k, key2=tg + 1.5)
                ph.cur_key = tg + 1.5
                tg += 1
                z, z_k = pz.next()
                for half in range(2):
                    for c in range(8):
                        OP_mm(ph, z[:, half * 512:(half + 1) * 512], uT[:, c, j * 128:(j + 1) * 128],
                              w_in[:, c, half * 512:(half + 1) * 512], c == 0, c == 7, [uT_toks[j], w_in_k], [z_k])
                if kind == "ctx":
                    OP_cp(ph, "dve", zc[:, j, :], z[:], [z_k], [zc_k + f"_{j}"])
                    continue
                zt, zt_k = zt_r.next()
                OP_cp(ph, "dve", zt[:], z[:], [z_k], [zt_k])
                ph.cur_key = (tg - 1) + 2.6
                a, a_k = pa.next()
                for comp in range(2):
                    for half in range(2):
                        OP_mm(ph, a[:, comp, half * 512:(half + 1) * 512], t1m[:, comp, :],
                              zt[:, half * 512:(half + 1) * 512], True, True, [t1m_k, zt_k], [a_k])
                tm, tm_k = tm_r.next()
                OP_act(ph, tm[:, 0, :], a[:, 1, :], AF.Copy, [a_k, tw_k], [tm_k], scale=tw[:, 1, t2:t2 + 1])
                OP_act(ph, tm[:, 1, :], a[:, 1, :], AF.Copy, [a_k, tw_k], [tm_k], scale=tw[:, 0, t2:t2 + 1])
                bt, bt_k = b_r.next()
                OP_stt(ph, "dve", bt[:, 0, :], a[:, 0, :], tw[:, 0, t2:t2 + 1], tm[:, 0, :], ALU.mult, ALU.subtract,
                       [a_k, tw_k, tm_k], [bt_k])
                OP_stt(ph, "dve", bt[:, 1, :], a[:, 0, :], tw[:, 1, t2:t2 + 1], tm[:, 1, :], ALU.mult, ALU.add,
                       [a_k, tw_k, tm_k], [bt_k])
                for comp in range(2):
                    ph.dma("pool", S["Bd"][comp, t2], bt[:, comp, :], reads=[bt_k], semkey=bt_k + "o")
            ph.cur_key = tg - 1 + 3.5
            for n in range(8):
                g_ps, g_ps_k = pg.next()
                for c in range(8):
                    OP_mm(ph, g_ps[:, 0:ntok], w_in[:, c, D + n * 128:D + (n + 1) * 128], uT[:, c, 0:ntok], c == 0,
                          c == 7, uT_toks + [w_in_k], [g_ps_k])
                if kind == "lat":
                    g, g_k = gst.next()
                    OP_act(ph, g[:, 0:ntok], g_ps[:, 0:ntok], AF.Silu, [g_ps_k], [g_k])
                    ph.dma("pool", S["GATE"][n * 128:(n + 1) * 128, j0 * 128:j0 * 128 + ntok], g[:, 0:ntok],
                           reads=[g_k], semkey=g_k + "o")
                else:
                    OP_act(ph, gc[:, n, :], g_ps[:, 0:ntok], AF.Silu, [g_ps_k], [gc_k + f"_{n}"])
        ph.cur_key = 1e6
        if do_ctx:
            self.f_ctx_tail(ph, li, zc, [zc_k + "_0", zc_k + "_1"], gc, [gc_k + f"_{n}" for n in range(8)], pz, pa, pg)
        ph.emit()

    def f_ctx_tail(self, ph, li, zc, zc_toks, gc, gc_toks, pz, pa, pg):
        nc, I, S = self.nc, self.I, self.S
        idx = li // 2
        dc, dc_k = ph.sb([128, 2, 512], BF16, "dctx")
        ph.dma("sp", dc[:], I["dft_ctx"], writes=[dc_k], semkey=dc_k)
        c1, c1_k = ph.sb([128, 2, 128], BF16, "c1")
        ph.dma("sp", c1[:], I["dft_c1"], writes=[c1_k], semkey=c1_k)
        stage = self.f_stage
        w_o, w_o_k = ph.sb([128, 8, D], BF16, "w_o")
        for c in range(8):
            self.load_w_bf16(ph, w_o[:, c, :], w_o_k, I["fno_w_out"][idx, c * 128:(c + 1) * 128, :], stage)
        GTb, GTb_k = self.load_bcast(ph, S["modd"][li, 1:2, 2, :], D, "gtc")
        xt, xt_k = ph.sb([128, 8, 512], BF16, "xT")
        gy, gy_k = ph.sb([128, 8, NCTX], BF16, "gy")
        for g in range(8):
            p, p_k = pg.next()
            for tc in range(2):
                OP_mm(ph, p[:], zc[:, tc, g * 128:(g + 1) * 128], dc[:, tc, :], tc == 0, tc == 1,
                      zc_toks + [dc_k], [p_k])
            OP_cp(ph, "dve", xt[:, g, :], p[:], [p_k], [xt_k + f"_{g}"])
            p2, p2_k = pg.next()
            OP_mm(ph, p2[:, 0:NCTX], c1[:, 0, :], xt[:, g, 0:NCTX], True, False, [c1_k, xt_k + f"_{g}"], [p2_k])
            OP_mm(ph, p2[:, 0:NCTX], c1[:, 1, :], xt[:, g, NCTX:2 * NCTX], False, True, [c1_k, xt_k + f"_{g}"], [p2_k])
            OP_tt(ph, "dve", gy[:, g, :], p2[:, 0:NCTX], gc[:, g, :], ALU.mult, [p2_k, gc_toks[g]], [gy_k + f"_{g}"])
        gy_toks = [gy_k + f"_{g}" for g in range(8)]
        h_r = rot_sb(ph, 2, [128, D], F32, "hcx")
        t_r = rot_sb(ph, 2, [128, D], F32, "tcx")
        for ts in range(2):
            ht, ht_k = h_r.next()
            ph.dma("sp", ht[:], S["hc"][ts * 128:(ts + 1) * 128, :], writes=[ht_k], semkey=ht_k)
            tt, tt_k = t_r.next()
            for half in range(2):
                py, py_k = pg.next()
                for g in range(8):
                    OP_mm(ph, py[:], gy[:, g, ts * 128:(ts + 1) * 128], w_o[:, g, half * 512:(half + 1) * 512],
                          g == 0, g == 7, gy_toks + [w_o_k], [py_k])
                OP_tt(ph, "dve", tt[:, half * 512:(half + 1) * 512], py[:], GTb[:, half * 512:(half + 1) * 512],
                      ALU.mult, [py_k, GTb_k], [tt_k])
            OP_tt(ph, "pool", tt[:], tt[:], ht[:], ALU.add, [tt_k, ht_k], [tt_k])
            ph.dma("pool", S["hc"][ts * 128:(ts + 1) * 128, :], tt[:], reads=[tt_k], semkey=tt_k + "o")

    def phase_f2(self, li, final):
        nc, I, S = self.nc, self.I, self.S
        idx = li // 2
        ph = Phase(nc, f"g{li}")
        r2, r2_k = ph.sb([128, 128], BF16, "r2")
        ph.dma("sp", r2[:], I["dft_r2"], writes=[r2_k], semkey=r2_k)
        c1, c1_k = ph.sb([128, 2, 128], BF16, "c1")
        ph.dma("sp", c1[:], I["dft_c1"], writes=[c1_k], semkey=c1_k)
        stage = rot_sb(ph, 2, [128, 1024], F32, "wst")
        w_o, w_o_k = ph.sb([128, 8, D], BF16, "w_o")
        for c in range(8):
            self.load_w_bf16(ph, w_o[:, c, :], w_o_k, I["fno_w_out"][idx, c * 128:(c + 1) * 128, :], stage)
        GTb, GTb_k = self.load_bcast(ph, S["modd"][li, 0:1, 2, :], D, "gtb")
        if final:
            FGb, FGb_k = self.load_bcast(ph, I["final_g"], D, "fg")
        KB = 8
        b_r = rot_sb(ph, 2, [128, KB, D], BF16, "bblk")
        g_r = rot_sb(ph, 2, [128, 8, KB * 128], BF16, "gblk")
        px = rot_ps(ph, 2, [128, 512], F32, "px")
        pyy = rot_ps(ph, 2, [128, 512], F32, "pyy")
        po = rot_ps(ph, 2, [128, 512], F32, "po")
        xt_r = rot_sb(ph, 2, [128, 8, KB * 128], BF16, "xT")
        gy_r = rot_sb(ph, 2, [128, 8, KB * 64], BF16, "gy")
        h_r = rot_sb(ph, 2, [128, D], F32, "h")
        t_r = rot_sb(ph, 2, [128, D], F32, "t")
        ss_r = rot_sb(ph, 4, [128, 8], F32, "ss")
        junk, junk_k = ph.sb([128, D], BF16, "junk")
        Bv = S["Bd"].rearrange("c t k d -> (c t) k d")
        hv = S["h"].rearrange("(k2 k1) d -> k1 k2 d", k1=128)
        ov = self.out.rearrange("(k2 k1) d -> k1 k2 d", k1=128)
        ph.cur_key = -1.0
        for blk in range(128 // KB):
            k10 = blk * KB
            ph.cur_key = float(blk)
            bb, bb_k = b_r.next()
            for q in range(2):
                ph.dma("sp", bb[:, q * 4:(q + 1) * 4, :], Bv[:, k10 + q * 4:k10 + (q + 1) * 4, :], writes=[bb_k],
                       semkey=bb_k)
            gb, gb_k = g_r.next()
            t20 = k10 % 64
            hi = k10 // 64
            for cc in range(8):
                ph.dma("sp", gb[:, cc, :], S["GATE"][cc * 128:(cc + 1) * 128, t20 * 128:(t20 + KB) * 128],
                       writes=[gb_k], semkey=gb_k)
            xt, xt_k = xt_r.next()
            gy, gy_k = gy_r.next()
            for cc in range(8):
                for q in range(KB // 4):
                    p, p_k = px.next()
                    for kk in range(4):
                        k1 = q * 4 + kk
                        OP_mm(ph, p[:, kk * 128:(kk + 1) * 128], bb[:, k1, cc * 128:(cc + 1) * 128], r2[:], True, True,
                              [bb_k, r2_k], [p_k])
                    xw = xt[:, cc, :].rearrange("p (c k j) -> p k c j", c=2, k=KB)[:, q * 4:(q + 1) * 4, :, :]
                    OP_cp(ph, "act", xw, p[:].rearrange("p (k c j) -> p k c j", k=4, c=2), [p_k], [xt_k + f"_{cc}"])
                ph.cur_key = blk + 1.5
                p2, p2_k = pyy.next()
                p2v = p2[:].rearrange("p (k j) -> p k j", k=KB)
                OP_mm(ph, p2[:], c1[:, 0, :], xt[:, cc, 0:KB * 64], True, False, [c1_k, xt_k + f"_{cc}"], [p2_k])
                OP_mm(ph, p2[:], c1[:, 1, :], xt[:, cc, KB * 64:KB * 128], False, True, [c1_k, xt_k + f"_{cc}"], [p2_k])
                gv = gb[:, cc, :].rearrange("p (t a two) -> p t a two", t=KB, two=2)[:, :, :, hi]
                OP_tt(ph, "dve", gy[:, cc, :].rearrange("p (k j) -> p k j", k=KB), p2v, gv, ALU.mult,
                      [p2_k, gb_k], [gy_k + f"_{cc}"])
                ph.cur_key = float(blk)
            gy_toks = [gy_k + f"_{cc}" for cc in range(8)]
            ph.cur_key = blk + 2.5
            for ts in range(KB // 2):
                ht, ht_k = h_r.next()
                k1a = k10 + 2 * ts
                ph.dma("sp", ht[0:64, :], hv[k1a], writes=[ht_k], semkey=ht_k)
                ph.dma("sp", ht[64:128, :], hv[k1a + 1], writes=[ht_k], semkey=ht_k)
                tt, tt_k = t_r.next()
                for half in range(2):
                    py, py_k = po.next()
                    for cc in range(8):
                        OP_mm(ph, py[:], gy[:, cc, ts * 128:(ts + 1) * 128], w_o[:, cc, half * 512:(half + 1) * 512],
                              cc == 0, cc == 7, gy_toks + [w_o_k], [py_k])
                    OP_tt(ph, "dve", tt[:, half * 512:(half + 1) * 512], py[:], GTb[:, half * 512:(half + 1) * 512],
                          ALU.mult, [py_k, GTb_k], [tt_k])
                OP_tt(ph, "pool", tt[:], tt[:], ht[:], ALU.add, [tt_k, ht_k], [tt_k])
                if final:
                    ss, ss_k = ss_r.next()
                    OP_act(ph, junk[:], tt[:], AF.Square, [tt_k], [junk_k, ss_k], scale=1.0 / 32.0, accum=ss[:, 0:1])
                    OP_rstd(ph, ss[:, 1:2], ss[:, 0:1], ss_k)
                    OP_stt(ph, "dve", tt[:], tt[:], ss[:, 1:2], FGb[:], ALU.mult, ALU.mult, [tt_k, ss_k, FGb_k], [tt_k])
                    dsts = (ov[k1a], ov[k1a + 1])
                else:
                    dsts = (hv[k1a], hv[k1a + 1])
                ph.dma("pool", dsts[0], tt[0:64, :], reads=[tt_k], semkey=tt_k + "o")
                ph.dma("pool", dsts[1], tt[64:128, :], reads=[tt_k], semkey=tt_k + "o")
        ph.emit()

    def build(self):
        S, I = self.S, self.I
        steps = [
            lambda: self.phase_mod(),
            lambda: self.phase_mla_a(0, I["x"], I["ctx"], True),
            lambda: self.phase_mla_b(0, I["x"], I["ctx"], True),
            lambda: self.phase_f1(1, True),
            lambda: self.phase_f2(1, False),
            lambda: self.phase_mla_a(2, S["h"], S["hc"], False),
            lambda: self.phase_mla_b(2, S["h"], S["hc"], False),
            lambda: self.phase_f1(3, False),
            lambda: self.phase_f2(3, True),
        ]
        if self.inject:
            self.phase_inject()
        if self.steps is not None:
            for i in self.steps:
                steps[i]()
            return self.nc
        n = len(steps) if self.upto is None else self.upto
        for i, st in enumerate(steps[:n]):
            with self.nc.named_scope(f"step{i}"):
                st()
        return self.nc


def _tables():
    bf = ml_dtypes.bfloat16
    row = np.repeat(np.arange(128), 64).astype(np.float32)
    col = np.tile(np.arange(64), 128).astype(np.float32)
    inv = (1.0 / (10000.0 ** (np.arange(0, 32, 2, dtype=np.float32) / 32.0))).astype(np.float32)
    ar = row[:, None] * inv[None, :]
    ac = col[:, None] * inv[None, :]
    cos_full = np.concatenate([np.cos(ar), np.cos(ar), np.cos(ac), np.cos(ac)], -1).astype(np.float32)
    sin_sgn = np.concatenate([-np.sin(ar), np.sin(ar), -np.sin(ac), np.sin(ac)], -1).astype(np.float32)
    rope_k = np.ascontiguousarray(np.stack([cos_full, sin_sgn], 1))
    rope_q = np.ascontiguousarray(np.stack([cos_full.T, sin_sgn.T], 0))
    ident = np.eye(128, dtype=np.float32).astype(bf)
    n128 = np.arange(128, dtype=np.float64)
    a1 = 2 * np.pi * np.outer(n128, n128) / 128.0
    sc = 1.0 / math.sqrt(128.0)
    dft_t1 = np.stack([np.cos(a1) * sc, np.sin(a1) * sc], 1).astype(np.float32).astype(bf)
    atw = 2 * np.pi * np.outer(n128, np.arange(64)) / 8192.0
    dft_tw = np.ascontiguousarray(np.stack([np.cos(atw), np.sin(atw)], 1).astype(np.float32))
    n64 = np.arange(64, dtype=np.float64)
    a2 = 2 * np.pi * np.outer(n64, n64) / 64.0
    C2, S2 = np.cos(a2) / 8.0, np.sin(a2) / 8.0
    dft_r2 = np.block([[C2, S2], [-S2, C2]]).astype(np.float32).astype(bf)
    dft_c1 = np.stack([np.cos(a1) * sc, -np.sin(a1) * sc], 1).astype(np.float32).astype(bf)
    n256 = np.arange(256, dtype=np.float64)
    a3 = 2 * np.pi * np.outer(n256, n256) / 256.0
    CS = np.concatenate([np.cos(a3), np.sin(a3)], 1) / 16.0
    dft_ctx = np.ascontiguousarray(CS.reshape(2, 128, 512).transpose(1, 0, 2)).astype(np.float32).astype(bf)
    return dict(rope_k=rope_k, rope_q=rope_q, ident=ident, dft_t1=dft_t1, dft_tw=dft_tw, dft_r2=dft_r2,
                dft_c1=dft_c1, dft_ctx=dft_ctx)


CORE_BATCH = {0: 0, 1: 1, 4: 2, 5: 3}


def make_in_maps(inputs, cores):
    f = lambda a: np.ascontiguousarray(np.asarray(a, dtype=np.float32))
    tabs = _tables()
    w_qup = f(inputs["mla_w_qup"])
    perm = np.array([(d + 16) if (d % 32) < 16 else (d - 16) for d in range(64)])
    w_qup_sw = np.stack([np.concatenate([w_qup[i][:, h * 192 + 128 + perm] for h in range(8)], 1) for i in range(2)])
    w_kvup = f(inputs["mla_w_kvup"])
    w_ukT = np.stack([np.stack([w_kvup[i][:, h * 256:h * 256 + 128].T for h in range(8)]) for i in range(2)])
    shared = dict(
        norm_g=f(inputs["norm_g"]), w_ada=f(inputs["w_ada"]), b_ada=f(inputs["b_ada"]),
        mla_w_in=f(inputs["mla_w_in"]), mla_g_qa=f(inputs["mla_g_qa"]), mla_w_qup=w_qup,
        mla_w_qup_sw=np.ascontiguousarray(w_qup_sw), mla_g_kva=f(inputs["mla_g_kva"]), mla_w_kvup=w_kvup,
        mla_w_ukT=np.ascontiguousarray(w_ukT), mla_w_out=f(inputs["mla_w_out"]), fno_w_in=f(inputs["fno_w_in"]),
        fno_w_out=f(inputs["fno_w_out"]), final_g=f(inputs["final_g"]).reshape(1, D), **tabs)
    x, c, ctx, c_ctx = f(inputs["x"]), f(inputs["c"]), f(inputs["ctx"]), f(inputs["c_ctx"])
    maps = []
    zero = None
    for core in cores:
        b = CORE_BATCH.get(core)
        if b is None:
            if zero is None:
                zero = {k: (v if k in tabs else np.zeros_like(v)) for k, v in maps[0].items()}
            maps.append(zero)
            continue
        cv = np.stack([c[b], c_ctx], 0)
        cvecT = np.ascontiguousarray(cv.reshape(2, 8, 128).transpose(2, 1, 0))
        m = dict(shared)
        m.update(x=x[b], ctx=ctx[b], cvecT=cvecT)
        maps.append(m)
    return maps


def kernel(**inputs):
    nc = Builder().build()
    cores = list(range(8))
    in_maps = make_in_maps(inputs, cores)
    res = run_bass_kernel_spmd(nc, in_maps, core_ids=cores)
    inv = {b: c for c, b in CORE_BATCH.items()}
    out = np.stack([np.asarray(res.results[inv[b]]["out"]) for b in range(4)], 0)
    return out.astype(np.float32)
```
